# Optimizing a Trainium2 kernel written in Bass

```python
import jax, jax.numpy as jnp
from jax import lax
import numpy as np

D_MODEL = 1024
BATCH = 2
SEQ = 8192
DEPTH = 1
DEC_BATCH = 32
DEC_SEQ = 16
PAST_LEN = 1024

CHUNK = 64
LEFT_CHUNKS = 8
HEAD_DIM = 64
H_A = 8
W_A = H_A * HEAD_DIM
REL_CLIP = 128
H_B = 4
DK_B = 64
DV_B = 128
W_BK = H_B * DK_B
W_BV = H_B * DV_B
GATE_RANK = 16
GATE_TAU = 16.0
GLA_BLOCK = 16
N_MEM = 256
H_M = 4
W_M = H_M * HEAD_DIM
EPS = 1e-6
IN_SIZES = (W_A, W_A, W_A, W_A, W_BK, W_BK, W_BV, GATE_RANK, W_BV, W_M, W_M, D_MODEL, D_MODEL, D_MODEL)
D_IN = sum(IN_SIZES)

kernel_name = "hybrid_streaming_encoder_step"


def rms_norm(x, g):
    xf = x.astype(jnp.float32)
    y = xf * lax.rsqrt(jnp.mean(xf * xf, axis=-1, keepdims=True) + EPS)
    return (y * g.astype(jnp.float32)).astype(x.dtype)


def split_heads(z, n):
    return z.reshape(*z.shape[:-1], n, z.shape[-1] // n)


def rel_bias_lookup(table, dist):
    return table[:, jnp.clip(dist, -REL_CLIP, REL_CLIP) + REL_CLIP].astype(jnp.float32)


def band_attention_prompt(q, k, v, table):
    b, t, h, hd = q.shape
    nc = t // CHUNK
    band = (LEFT_CHUNKS + 1) * CHUNK
    qc = q.reshape(b, nc, CHUNK, h, hd)
    pad = ((0, 0), (LEFT_CHUNKS * CHUNK, 0), (0, 0), (0, 0))
    idx = jnp.arange(nc)[:, None] + jnp.arange(LEFT_CHUNKS + 1)[None, :]
    kc = jnp.pad(k, pad).reshape(b, nc + LEFT_CHUNKS, CHUNK, h, hd)[:, idx].reshape(b, nc, band, h, hd)
    vc = jnp.pad(v, pad).reshape(b, nc + LEFT_CHUNKS, CHUNK, h, hd)[:, idx].reshape(b, nc, band, h, hd)
    dist = (LEFT_CHUNKS * CHUNK + jnp.arange(CHUNK))[:, None] - jnp.arange(band)[None, :]
    bias = rel_bias_lookup(table, dist)
    valid = jnp.repeat(idx >= LEFT_CHUNKS, CHUNK, axis=1)
    s = jnp.einsum('bnqhd,bnkhd->bnhqk', qc, kc).astype(jnp.float32) * (hd ** -0.5) + bias[None, None]
    s = jnp.where(valid[None, :, None, None, :], s, -jnp.inf)
    p = jax.nn.softmax(s, axis=-1).astype(v.dtype)
    o = jnp.einsum('bnhqk,bnkhd->bnqhd', p, vc)
    return o.reshape(b, t, h * hd)


def band_attention_step(q, k_new, v_new, k_past, v_past, table):
    b, s_len, h, hd = q.shape
    L = k_past.shape[1]
    kk = jnp.concatenate([k_past, k_new], axis=1)
    vv = jnp.concatenate([v_past, v_new], axis=1)
    dist = (L + jnp.arange(s_len))[:, None] - jnp.arange(L + s_len)[None, :]
    bias = rel_bias_lookup(table, dist)
    s = jnp.einsum('bqhd,bkhd->bhqk', q, kk).astype(jnp.float32) * (hd ** -0.5) + bias[None]
    p = jax.nn.softmax(s, axis=-1).astype(vv.dtype)
    o = jnp.einsum('bhqk,bkhd->bqhd', p, vv)
    return o.reshape(b, s_len, h * hd)


def memory_attention(q, mk, mv):
    b, t, h, hd = q.shape
    s = jnp.einsum('bqhd,bmhd->bhqm', q, mk).astype(jnp.float32) * (hd ** -0.5)
    p = jax.nn.softmax(s, axis=-1).astype(mv.dtype)
    return jnp.einsum('bhqm,bmhd->bqhd', p, mv).reshape(b, t, h * hd)


def gla_scan(q, k, v, log_a, s0):
    b, t, h, _ = q.shape
    pad = (-t) % GLA_BLOCK

    def blocks(z):
        z = jnp.pad(z.astype(jnp.float32), ((0, 0), (0, pad), (0, 0), (0, 0)))
        return z.reshape(b, -1, GLA_BLOCK, h, z.shape[-1]).transpose(1, 0, 3, 2, 4)

    causal = jnp.tril(jnp.ones((GLA_BLOCK, GLA_BLOCK), dtype=bool))

    def step(S, blk):
        qi, ki, vi, ai = blk
        cum = jnp.cumsum(ai, axis=-2)
        last = cum[..., -1:, :]
        diff = jnp.where(causal[:, :, None], cum[..., :, None, :] - cum[..., None, :, :], -jnp.inf)
        att = jnp.einsum('bhtd,bhsd,bhtsd->bhts', qi, ki, jnp.exp(diff))
        o = jnp.einsum('bhts,bhsv->bhtv', att, vi) + jnp.einsum('bhtd,bhdv->bhtv', qi * jnp.exp(cum), S)
        S = S * jnp.exp(last)[..., 0, :, None] + jnp.einsum('bhsd,bhsv->bhdv', ki * jnp.exp(last - cum), vi)
        return S, o

    S, o = lax.scan(step, s0.astype(jnp.float32), (blocks(q), blocks(k), blocks(v), blocks(log_a)))
    o = o.transpose(1, 0, 3, 2, 4).reshape(b, -1, h, v.shape[-1])[:, :t]
    return o.astype(v.dtype), S.astype(s0.dtype)


def memory_kv(mem, g_mem, w_mem_kv, g_km):
    mh = rms_norm(mem, g_mem)
    mk, mv = jnp.split(mh @ w_mem_kv, 2, axis=-1)
    return rms_norm(split_heads(mk, H_M), g_km), split_heads(mv, H_M)


def hybrid_layer(x, attend_a, mem_k, mem_v, s0, norm_in, w_in, g_qa, g_ka, rel_bias,
                 w_gate2, b_gate, g_gla_out, g_qm, w_up_a, w_up_b, w_up_m, w_out):
    b, t, _ = x.shape
    h = rms_norm(x, norm_in)
    proj = h @ w_in
    (qa, ka, va, za, qb, kb, vb, glr, zb, qm, zm, gate_a, gate_b, gate_m) = jnp.split(
        proj, np.cumsum(IN_SIZES)[:-1].tolist(), axis=-1)
    qa = rms_norm(split_heads(qa, H_A), g_qa)
    ka = rms_norm(split_heads(ka, H_A), g_ka)
    va = split_heads(va, H_A)
    out_a = attend_a(qa, ka, va, rel_bias)
    log_a = jax.nn.log_sigmoid((glr @ w_gate2 + b_gate).astype(jnp.float32)) / GATE_TAU
    o_b, s_new = gla_scan(split_heads(qb, H_B) * (DK_B ** -0.5), split_heads(kb, H_B),
                          split_heads(vb, H_B), split_heads(log_a, H_B), s0)
    out_b = rms_norm(o_b, g_gla_out).reshape(b, t, W_BV)
    out_m = memory_attention(rms_norm(split_heads(qm, H_M), g_qm), mem_k, mem_v)
    u = (jax.nn.sigmoid(gate_a) * ((out_a * jax.nn.silu(za)) @ w_up_a)
         + jax.nn.sigmoid(gate_b) * ((out_b * jax.nn.silu(zb)) @ w_up_b)
         + jax.nn.sigmoid(gate_m) * ((out_m * jax.nn.silu(zm)) @ w_up_m))
    return x + u @ w_out, ka, va, s_new


def setup_inputs(seed: int = 0) -> dict:
    key = jax.random.key(seed)
    ks = iter(jax.random.split(key, 32))

    def nrm(shape, scale):
        return jax.random.normal(next(ks), shape, jnp.float32) * scale

    a_keep = min(LEFT_CHUNKS * CHUNK, PAST_LEN)
    return {
        "x_prompt": nrm((BATCH, SEQ, D_MODEL), 1.0),
        "x_sample": nrm((DEC_BATCH, DEC_SEQ, D_MODEL), 1.0),
        "mem_prompt": nrm((BATCH, N_MEM, D_MODEL), 1.0),
        "cache_a_k": nrm((DEPTH, DEC_BATCH, a_keep, H_A, HEAD_DIM), 1.0),
        "cache_a_v": nrm((DEPTH, DEC_BATCH, a_keep, H_A, HEAD_DIM), 1.0),
        "state_gla": nrm((DEPTH, DEC_BATCH, H_B, DK_B, DV_B), 1.0),
        "cache_mem_k": nrm((DEPTH, DEC_BATCH, N_MEM, H_M, HEAD_DIM), 1.0),
        "cache_mem_v": nrm((DEPTH, DEC_BATCH, N_MEM, H_M, HEAD_DIM), 1.0),
        "norm_in": 1.0 + nrm((DEPTH, D_MODEL), 0.02),
        "w_in": nrm((DEPTH, D_MODEL, D_IN), D_MODEL ** -0.5),
        "g_qa": 1.0 + nrm((DEPTH, HEAD_DIM), 0.02),
        "g_ka": 1.0 + nrm((DEPTH, HEAD_DIM), 0.02),
        "rel_bias": nrm((DEPTH, H_A, 2 * REL_CLIP + 1), 0.5),
        "w_gate2": nrm((DEPTH, GATE_RANK, W_BK), GATE_RANK ** -0.5),
        "b_gate": nrm((DEPTH, W_BK), 0.1),
        "g_gla_out": 1.0 + nrm((DEPTH, DV_B), 0.02),
        "g_mem": 1.0 + nrm((DEPTH, D_MODEL), 0.02),
        "w_mem_kv": nrm((DEPTH, D_MODEL, 2 * W_M), D_MODEL ** -0.5),
        "g_qm": 1.0 + nrm((DEPTH, HEAD_DIM), 0.02),
        "g_km": 1.0 + nrm((DEPTH, HEAD_DIM), 0.02),
        "w_up_a": nrm((DEPTH, W_A, D_MODEL), W_A ** -0.5),
        "w_up_b": nrm((DEPTH, W_BV, D_MODEL), W_BV ** -0.5),
        "w_up_m": nrm((DEPTH, W_M, D_MODEL), W_M ** -0.5),
        "w_out": nrm((DEPTH, D_MODEL, D_MODEL), 0.5 * D_MODEL ** -0.5),
    }


def reference(x_prompt, x_sample, mem_prompt, cache_a_k, cache_a_v, state_gla, cache_mem_k,
              cache_mem_v, norm_in, w_in, g_qa, g_ka, rel_bias, w_gate2, b_gate, g_gla_out,
              g_mem, w_mem_kv, g_qm, g_km, w_up_a, w_up_b, w_up_m, w_out):
    xp, xs = x_prompt, x_sample
    keep_p = min(LEFT_CHUNKS * CHUNK, x_prompt.shape[1])
    akp, avp, sgp, mkp, mvp, aks, avs, sgs = [], [], [], [], [], [], [], []
    for l in range(DEPTH):
        weights = (norm_in[l], w_in[l], g_qa[l], g_ka[l], rel_bias[l], w_gate2[l], b_gate[l],
                   g_gla_out[l], g_qm[l], w_up_a[l], w_up_b[l], w_up_m[l], w_out[l])
        mk, mv = memory_kv(mem_prompt, g_mem[l], w_mem_kv[l], g_km[l])
        s0 = jnp.zeros((xp.shape[0], H_B, DK_B, DV_B), state_gla.dtype)
        xp, ka, va, sp = hybrid_layer(xp, band_attention_prompt, mk, mv, s0, *weights)
        akp.append(ka[:, -keep_p:])
        avp.append(va[:, -keep_p:])
        sgp.append(sp)
        mkp.append(mk)
        mvp.append(mv)
        past_k, past_v = cache_a_k[l], cache_a_v[l]
        attend_step = lambda q, k, v, tab: band_attention_step(q, k, v, past_k, past_v, tab)
        xs, ka_s, va_s, ss = hybrid_layer(xs, attend_step, cache_mem_k[l], cache_mem_v[l],
                                          state_gla[l], *weights)
        aks.append(ka_s)
        avs.append(va_s)
        sgs.append(ss)
    return (xp, xs, jnp.stack(akp), jnp.stack(avp), jnp.stack(sgp), jnp.stack(mkp), jnp.stack(mvp),
            jnp.stack(aks), jnp.stack(avs), jnp.stack(sgs))
```

```python
import numpy as np
from contextlib import ExitStack
import concourse.bass as bass
import concourse.mybir as mybir
from concourse.bass_utils import run_bass_kernel_spmd

F32 = mybir.dt.float32
BF16 = mybir.dt.bfloat16
AF = mybir.ActivationFunctionType
ALU = mybir.AluOpType

EPS = 1e-6
NCORES = 8
NT_MAIN = 16
NT_HALO = 4
C_QA, C_KA, C_VA, C_ZA, C_QB, C_KB, C_VB, C_GLR, C_ZB, C_QM, C_ZM, C_GA, C_GB, C_GM = (
    0, 512, 1024, 1536, 2048, 2304, 2560, 3072, 3088, 3600, 3856, 4112, 5136, 6160)
D_IN = 7184
NX = 4112
NEG = -30000.0


class Tr:
    def __init__(self, nc, es):
        self.nc = nc
        self.es = es
        self.engs = {"pe": nc.tensor, "act": nc.scalar, "dve": nc.vector,
                     "pool": nc.gpsimd, "sp": nc.sync}
        self.sems = {}
        self.cnt = {}
        for e in self.engs:
            self.sems[e] = es.enter_context(nc.semaphore("s_" + e))
            self.cnt[e] = 0
        self.waited = {e: {} for e in self.engs}
        self.lastw = {}
        self.readers = {}

    def _deps(self, r, w):
        deps = {}

        def add(tag):
            if tag is not None and deps.get(tag[0], 0) < tag[1]:
                deps[tag[0]] = tag[1]
        for b in r:
            add(self.lastw.get(b))
        for b in w:
            add(self.lastw.get(b))
            for t in self.readers.get(b, ()):
                add(t)
        return deps

    def _wait(self, eng, deps):
        E = self.engs[eng]
        for s, v in deps.items():
            if eng == "pe" and s == "pe":
                continue
            if self.waited[eng].get(s, 0) < v:
                E.wait_ge(self.sems[s], v)
                self.waited[eng][s] = v

    def _commit(self, tag, r, w):
        for b in w:
            self.lastw[b] = tag
            self.readers[b] = []
        for b in r:
            self.readers.setdefault(b, []).append(tag)

    def op(self, eng, fn, r=(), w=()):
        ps = [b for b in r if b[:2] in ("pf", "pb")]
        if ps:
            w = list(w) + ps
        self._wait(eng, self._deps(r, w))
        ins = fn(self.engs[eng])
        self.cnt[eng] += 1
        ins.then_inc(self.sems[eng], 1)
        self._commit((eng, self.cnt[eng]), r, w)
        return ins

    def dma(self, q, out, in_, r=(), w=(), sem=None, **kw):
        if sem not in self.sems:
            self.sems[sem] = self.es.enter_context(self.nc.semaphore("d_" + sem))
            self.cnt[sem] = 0
        deps = self._deps(r, w)
        if self.cnt[sem] > 0:
            deps[sem] = max(deps.get(sem, 0), self.cnt[sem])
        self._wait(q, deps)
        ins = self.engs[q].dma_start(out=out, in_=in_, **kw)
        self.cnt[sem] += 16
        ins.then_inc(self.sems[sem], 16)
        self._commit((sem, self.cnt[sem]), r, w)
        return ins

    def barrier(self, engs=("pe", "act", "dve", "pool", "sp")):
        for e in engs:
            for s, v in self.cnt.items():
                if v > 0 and self.waited[e].get(s, 0) < v and not (e == "pe" and s == "pe"):
                    self.engs[e].wait_ge(self.sems[s], v)
                    self.waited[e][s] = v


class _Stop(Exception):
    pass


def build_nc(kstop=None):
    nc = bass.Bass("TRN2", target_bir_lowering=False)

    def din(name, shape):
        return nc.dram_tensor(name, list(shape), F32, kind="ExternalInput").ap()

    def dout(name, shape):
        return nc.dram_tensor(name, list(shape), F32, kind="ExternalOutput").ap()

    xh = din("xh", [512, 1024]); xm = din("xm", [2048, 1024]); xsd = din("xs", [128, 1024])
    memd = din("mem", [256, 1024])
    cak = din("cak", [4, 512, 512]); cav = din("cav", [4, 512, 512])
    sgd = din("sg", [4, 4, 64, 128])
    cmk = din("cmk", [4, 256, 256]); cmv = din("cmv", [4, 256, 256])
    w_in = din("w_in", [1024, D_IN]); w_mem = din("w_mem", [1024, 512])
    w_ua = din("w_ua", [512, 1024]); w_ub = din("w_ub", [512, 1024]); w_um = din("w_um", [256, 1024])
    w_o = din("w_o", [1024, 1024])
    norm_in = din("norm_in", [1, 1024]); g_mem = din("g_mem", [1, 1024])
    g_qa = din("g_qa", [64, 1]); g_ka = din("g_ka", [64, 1]); g_qm = din("g_qm", [64, 1]); g_km = din("g_km", [64, 1])
    w_g2 = din("w_g2", [16, 256]); b_g = din("b_g", [1, 256]); g_go = din("g_go", [128, 1])
    epd = din("ep", [128, 8, 640])
    identd = din("ident", [128, 128]); trip = din("trip", [128, 128]); onep = din("onep", [128, 128])
    tris = din("tris", [128, 128]); ones_s = din("ones_s", [128, 128]); blk64d = din("blk64", [128, 128])
    colsd = din("cols", [128, 4])

    y_p = dout("y_p", [2048, 1024]); y_s = dout("y_s", [4, 16, 1024])
    ko_p = dout("ko_p", [512, 512]); vo_p = dout("vo_p", [512, 512])
    go_p = dout("go_p", [4, 64, 128])
    mko = dout("mko", [256, 256]); mvo = dout("mvo", [256, 256])
    ko_s = dout("ko_s", [4, 16, 512]); vo_s = dout("vo_s", [4, 16, 512])
    go_s = dout("go_s", [4, 4, 64, 128])

    try:
      with ExitStack() as es:
        t = Tr(nc, es)

        def stage(k):
            if kstop is not None and k == kstop:
                t.barrier(("sp",))
                raise _Stop()

        def sb(name, shape, dt, ctx=es):
            return ctx.enter_context(nc.sbuf_tensor("sb_" + name, list(shape), dt))

        pf = [es.enter_context(nc.psum_tensor("pf%d" % i, [128, 512], F32)) for i in range(7)]
        pb = [es.enter_context(nc.psum_tensor("pb%d" % i, [128, 1024], BF16)) for i in range(1)]
        rr = {"f": 0, "g": 0, "s": 0, "p": 0}

        def nextf():
            i = rr["f"]; rr["f"] = (i + 1) % 4
            return pf[i], "pf%d" % i

        def nextp():
            i = (0, 1, 2, 3, 6)[rr["p"]]; rr["p"] = (rr["p"] + 1) % 5
            return pf[i], "pf%d" % i

        def nexts():
            return nextp()

        rr["a"] = 0

        def nexta():
            i = (4, 5)[rr["a"]]; rr["a"] = (rr["a"] + 1) % 2
            return pf[i], "pf%d" % i

        gpool = {"banks": (6,)}

        def nextg():
            return nextp()

        def nextb():
            return pb[0], "pb0"

        id16 = sb("id16", [128, 128], BF16)
        blk64 = sb("blk64", [128, 128], BF16)
        tri_p = sb("tri_p", [128, 128], F32); one_p = sb("one_p", [128, 128], F32)
        tri_s = sb("tri_s", [128, 128], F32); one_s = sb("one_s", [128, 128], F32)
        cols = sb("cols", [128, 4], F32)
        ones8 = sb("ones8", [128, 8], F32)
        gq8 = sb("gq8", [128, 1], F32); gk = sb("gk", [128, 1], F32)
        gqm8 = sb("gqm8", [128, 1], F32); gkm = sb("gkm", [128, 1], F32)
        ggo = sb("ggo", [128, 1], F32)
        wg2 = sb("wg2", [32, 256], F32)
        glrT = sb("glrT", [32, 128], F32)
        GT = sb("GT", [128, 17, 10, 128], BF16)
        hwstg_all = GT[:, 10:17, :, :].rearrange("p t c k -> p (t c k)")[:, 0:8192].bitcast(F32)
        hwstg = [hwstg_all[:, i * 1024:(i + 1) * 1024] for i in range(4)]
        hwn = ["hwstg%d" % i for i in range(4)]
        hwc = {"n": 0}

        def gtw(tile_):
            return ["GT"] + (hwn if tile_ >= 10 else [])

        def hw_wload(dst3, src, ks, c0, c1, bname, d0=None):
            d0 = c0 if d0 is None else d0
            for k in ks:
                for a in range(c0, c1, 1024):
                    b_ = min(a + 1024, c1)
                    i_ = hwc["n"] % 4
                    hwc["n"] += 1
                    t.dma("sp", hwstg[i_][:, 0:b_ - a], src[k * 128:(k + 1) * 128, a:b_], w=[hwn[i_]], sem="hw%d" % i_)
                    dst = dst3[:, k, d0 + a - c0:d0 + b_ - c0]
                    if i_ % 2 == 0:
                        t.op("act", lambda e: e.copy(out=dst, in_=hwstg[i_][:, 0:b_ - a]), r=[hwn[i_]], w=[bname])
                    else:
                        t.op("dve", lambda e: e.tensor_copy(out=dst, in_=hwstg[i_][:, 0:b_ - a]), r=[hwn[i_]], w=[bname])
        S32 = sb("S32", [128, 2, 128], F32); S16 = sb("S16", [128, 2, 128], BF16)
        S32s = sb("S32s", [128, 8, 128], F32); S16s = sb("S16s", [128, 8, 128], BF16)
        xt = [sb("xt%d" % i, [128, 1024], F32) for i in range(2)]
        xs16 = sb("xs16", [128, 1024], BF16)
        hTs = [sb("hT%d" % i, [128, 1024], BF16) for i in range(2)]
        st4 = [sb("st4_%d" % i, [128, 4], F32) for i in range(3)]
        WXG = sb("WXG", [128, 8 * NX], BF16)
        WX = WXG[:, :].rearrange("p (k n) -> p k n", k=8)
        WG = WXG[:, 0:8 * 3072].rearrange("p (k n) -> p k n", k=8)
        WO = WXG[:, 8 * 3072:8 * 4096].rearrange("p (k n) -> p k n", k=8)

        t.dma("pool", id16[:], identd, w=["id16"], sem="c0")
        t.dma("pool", blk64[:], blk64d, w=["blk64"], sem="c1")
        t.dma("sp", tri_p[:], trip, w=["tri_p"], sem="c2")
        t.dma("sp", one_p[:], onep, w=["one_p"], sem="c3")
        t.dma("sp", tri_s[:], tris, w=["tri_s"], sem="c4")
        t.dma("sp", one_s[:], ones_s, w=["one_s"], sem="c5")
        t.dma("sp", cols[:], colsd, w=["cols"], sem="c6")
        for half in range(2):
            sl = slice(half * 64, half * 64 + 64)
            t.dma("sp", gq8[sl, :], g_qa, w=["gq8"], sem="c9")
            t.dma("sp", gk[sl, :], g_ka, w=["gk"], sem="c10")
            t.dma("sp", gqm8[sl, :], g_qm, w=["gqm8"], sem="c11")
            t.dma("sp", gkm[sl, :], g_km, w=["gkm"], sem="c12")
        t.dma("sp", ggo[:], g_go, w=["ggo"], sem="c13")
        t.dma("sp", wg2[0:16, :], w_g2, w=["wg2"], sem="c14")
        t.dma("sp", wg2[16:17, :], b_g, w=["wg2"], sem="c14")
        t.op("dve", lambda e: e.tensor_scalar_mul(out=gq8[:], in0=gq8[:], scalar1=0.125), r=["gq8"], w=["gq8"])
        t.op("dve", lambda e: e.tensor_scalar_mul(out=gqm8[:], in0=gqm8[:], scalar1=0.125), r=["gqm8"], w=["gqm8"])
        t.op("dve", lambda e: e.memset(ones8[:], 1.0), w=["ones8"])
        t.op("dve", lambda e: e.memset(glrT[:], 1.0), w=["glrT"])
        t.op("dve", lambda e: e.memset(S32[:], 0.0), w=["S32"])
        t.op("dve", lambda e: e.memset(S16[:], 0.0), w=["S16"])

        wsem = {"n": 0}

        def wload(dst3, src, nk, c0, c1, bname, d0=None, extra_w=(), ks=None):
            d0 = c0 if d0 is None else d0
            for k in (range(nk) if ks is None else ks):
                for a in range(c0, c1, 2048):
                    b_ = min(a + 2048, c1)
                    t.dma("pool", dst3[:, k, d0 + a - c0:d0 + b_ - c0], src[k * 128:(k + 1) * 128, a:b_],
                          w=[bname] + list(extra_w), sem="w%d" % (wsem["n"] % 16))
                    wsem["n"] += 1

        mhalf = sb("mhalf", [128, 1], F32)
        t.op("dve", lambda e: e.memset(mhalf[:], -0.5), w=["mhalf"])

        def prep(src_rows, nrow, nrow_name, xs_, hs_, ss_, xpool=None, use_pool=False, defer=False, q="sp"):
            xpool = xt if xpool is None else xpool
            x_, xn = xpool[xs_], "xt%d" % xs_
            hT, hn = hTs[hs_], "hT%d" % hs_
            sv, svn = st4[ss_], "st4_%d" % ss_
            t.dma(q, x_[:], src_rows, w=[xn], sem="x%d" % xs_)
            t.op("act", lambda e: e.activation(out=hT[:], in_=x_[:], func=AF.Square, accum_out=sv[:, 0:1]),
                 r=[xn], w=[svn, hn])
            if use_pool:
                t.op("pool", lambda e: e.tensor_scalar(out=sv[:, 1:2], in0=sv[:, 0:1], scalar1=1.0 / 1024, scalar2=EPS,
                                                       op0=ALU.mult, op1=ALU.add), r=[svn], w=[svn])
                t.op("pool", lambda e: e.tensor_tensor(out=sv[:, 2:3], in0=sv[:, 1:2], in1=mhalf[:], op=ALU.pow),
                     r=[svn, "mhalf"], w=[svn])
            else:
                t.op("act", lambda e: e.activation(out=sv[:, 1:2], in_=sv[:, 0:1], func=AF.Ln, scale=1.0 / 1024, bias=EPS),
                     r=[svn], w=[svn])
                t.op("act", lambda e: e.activation(out=sv[:, 2:3], in_=sv[:, 1:2], func=AF.Exp, scale=-0.5), r=[svn], w=[svn])
            t.op("dve", lambda e: e.scalar_tensor_tensor(out=xs16[:], in0=x_[:], scalar=sv[:, 2:3], in1=nrow[:],
                                                         op0=ALU.mult, op1=ALU.mult),
                 r=[xn, svn, nrow_name], w=["xs16"])
            def fin():
                p, pn = nextb()
                for k in range(8):
                    t.op("pe", lambda e: e.transpose(out=p[:, k * 128:(k + 1) * 128], in_=xs16[:, k * 128:(k + 1) * 128],
                                                     identity=id16[:]), r=["xs16", "id16"], w=[pn])
                t.op("dve", lambda e: e.tensor_copy(out=hT[:], in_=p[:]), r=[pn], w=[hn])
            if defer:
                return hT, hn, x_, xn, fin
            fin()
            return hT, hn, x_, xn

        def proj_tok(hT, hn, W, wn, c0, n, evac, pool=None):
            p, pn = (pool or nextf)()
            for k in range(8):
                t.op("pe", lambda e: e.matmul(p[:, 0:n], lhsT=hT[:, k * 128:(k + 1) * 128], rhs=W[:, k, c0:c0 + n],
                                              start=(k == 0), stop=(k == 7)), r=[hn, wn], w=[pn])
            evac(p, pn)

        def proj_feat(hT, hn, W, wn, c0, m, pout, pn):
            for k in range(8):
                t.op("pe", lambda e: e.matmul(pout, lhsT=W[:, k, c0:c0 + m], rhs=hT[:, k * 128:(k + 1) * 128],
                                              start=(k == 0), stop=(k == 7)), r=[hn, wn], w=[pn])

        def qknorm(p, pn, n, gcol, gname, out16, outname, wk, outB=None):
            sq, r1 = wk
            t.op("act", lambda e: e.activation(out=sq[:, 0:n], in_=p[:, 0:n], func=AF.Square), r=[pn], w=["qn16k"])
            p2, p2n = nextp()
            t.op("pe", lambda e: e.matmul(p2[:, 0:n], lhsT=blk64[:], rhs=sq[:, 0:n], start=True, stop=True),
                 r=["qn16k", "blk64"], w=[p2n])
            t.op("act", lambda e: e.activation(out=r1[:, 0:n], in_=p2[:, 0:n], func=AF.Ln, scale=1.0 / 64, bias=EPS),
                 r=[p2n], w=["nr1"])
            t.op("act", lambda e: e.activation(out=r1[:, 0:n], in_=r1[:, 0:n], func=AF.Exp, scale=-0.5), r=["nr1"], w=["nr1"])
            if outB is None:
                t.op("dve", lambda e: e.scalar_tensor_tensor(out=out16, in0=p[:, 0:n], scalar=gcol[:], in1=r1[:, 0:n],
                                                             op0=ALU.mult, op1=ALU.mult),
                     r=[pn, gname, "nr1"], w=[outname])
            else:
                t.op("dve", lambda e: e.scalar_tensor_tensor(out=out16[0:64, 0:n], in0=p[0:64, 0:n], scalar=gcol[0:64, :],
                                                             in1=r1[0:64, 0:n], op0=ALU.mult, op1=ALU.mult),
                     r=[pn, gname, "nr1"], w=[outname])
                t.op("dve", lambda e: e.scalar_tensor_tensor(out=outB[64:128, 0:n], in0=p[64:128, 0:n], scalar=gcol[64:128, :],
                                                             in1=r1[64:128, 0:n], op0=ALU.mult, op1=ALU.mult),
                     r=[pn, gname, "nr1"], w=[outname])

        def transpose_out(src16, sname, nchunk, dst32, dname):
            p, pn = nextb()
            for c in range(nchunk):
                t.op("pe", lambda e: e.transpose(out=p[:, c * 128:(c + 1) * 128], in_=src16[:, c * 128:(c + 1) * 128],
                                                 identity=id16[:]), r=[sname, "id16"], w=[pn])
            t.op("act", lambda e: e.copy(out=dst32[:, 0:nchunk * 128], in_=p[:, 0:nchunk * 128]), r=[pn], w=[dname])

        def attend(nh, qT, qname, qcols, blocks, orow, PTs, ptn, otile, oname, zs16, zname, gout, goname):
            nq = qcols.stop - qcols.start
            po = pon = pov = None
            for h in range(nh):
                hh = h % 4
                g0 = h - hh
                pr = h // 2
                if hh == 0:
                    po, pon = nexta()
                    pov = po[:, 0:260].rearrange("p (h d) -> p h d", h=4)
                bl = blocks(h)
                banks = []
                col = 0
                cur = None
                for bi, (kT, kname, nk, E, ename, I, V, vname, kbase) in enumerate(bl):
                    if cur is None or col + nq > 512:
                        cur = nexts(); col = 0
                        banks.append([cur, []])
                    p, pn = cur
                    o_ap = p[kbase:kbase + nk, col:col + nq]
                    t.op("pe", lambda e: e.matmul(o_ap, lhsT=kT, rhs=qT[h % 2][:, pr * 128 + qcols.start:pr * 128 + qcols.stop],
                                                  start=True, stop=(E is None), tile_position=(0, kbase)),
                         r=[kname, qname], w=[pn])
                    if E is not None:
                        t.op("pe", lambda e: e.matmul(o_ap, lhsT=E, rhs=I, start=False, stop=True,
                                                      tile_position=(0, kbase)),
                             r=[ename, "id16"], w=[pn])
                    banks[-1][1].append((bi, col, nk, kbase))
                    col += nq
                PT, pt_name = PTs[h % 2], "%s%d" % (ptn, h % 2)
                for (p, pn), lst in banks:
                    i = 0
                    while i < len(lst):
                        j = i
                        while j + 1 < len(lst) and lst[j + 1][2] == lst[i][2] and lst[j + 1][3] == lst[i][3]:
                            j += 1
                        bi0, c0, nk, kb_ = lst[i]
                        c1 = lst[j][1] + nq
                        t.op("act", lambda e: e.activation(out=PT[kb_:kb_ + nk, bi0 * nq:bi0 * nq + (c1 - c0)],
                                                           in_=p[kb_:kb_ + nk, c0:c1], func=AF.Exp),
                             r=[pn], w=[pt_name])
                        i = j + 1
                yield
                for bi, (kT, kname, nk, E, ename, I, V, vname, kbase) in enumerate(bl):
                    t.op("pe", lambda e: e.matmul(pov[orow, hh, :], lhsT=PT[kbase:kbase + nk, bi * nq:(bi + 1) * nq], rhs=V,
                                                  start=(bi == 0), stop=(bi == len(bl) - 1),
                                                  tile_position=(kbase, orow.start)),
                         r=[pt_name, vname], w=[pon])
                if hh == 3:
                    rd = otile
                    t.op("dve", lambda e: e.reciprocal(out=rd[orow, g0:g0 + 4], in_=pov[orow, :, 64]), r=[pon], w=[oname])
                    for h2 in range(4):
                        hx = g0 + h2
                        t.op("dve", lambda e: e.scalar_tensor_tensor(out=gout[orow, hx * 64:(hx + 1) * 64], in0=pov[orow, h2, 0:64],
                                                                     scalar=rd[orow, hx:hx + 1], in1=zs16[orow, hx * 64:(hx + 1) * 64],
                                                                     op0=ALU.mult, op1=ALU.mult),
                             r=[pon, oname, zname], w=[goname])

        def run_all(*gens):
            gens = [g for g in gens if g is not None]
            while gens:
                for g in list(gens):
                    try:
                        next(g)
                    except StopIteration:
                        gens.remove(g)

        def chain(*gens):
            for g in gens:
                for _ in g:
                    yield

        Ep = sb("Ep", [128, 8, 640], BF16)
        KT = sb("KT", [128, 4, 6 * 128], BF16)
        V = sb("V", [128, 6, 8, 65], BF16)
        Epf_ = Ep[:, :, :].rearrange("p h n -> p (h n)")
        KTf_ = KT[:, :, :].rearrange("p a b -> p (a b)")
        Vf_ = V[:, :, :, :].rearrange("p a h d -> p (a h d)")
        WUc = [Epf_[:, c_ * 1024:(c_ + 1) * 1024] for c_ in range(5)] + \
              [KTf_[:, c_ * 1024:(c_ + 1) * 1024] for c_ in range(3)] + [Vf_[:, c_ * 1024:(c_ + 1) * 1024] for c_ in range(2)]
        with ExitStack() as ex:
            WM = sb("WM", [128, 8, 512], BF16, ex)
            nrow = sb("nrow", [128, 1024], F32, ex)
            MKT = sb("MKT", [128, 2, 256], BF16, ex); MV = sb("MV", [128, 2, 4, 65], BF16, ex)
            nr1 = sb("nr1", [128, 512], F32, ex)
            qm16 = sb("qm16", [128, 256], BF16, ex)
            RB = sb("RB", [128, 4864], BF16, ex)
            BS = []
            for si in range(2):
                d_ = {}
                off_ = 0
                for nm_, w_ in (("qnA", 512), ("qnB", 512), ("qmA", 256), ("qmB", 256), ("qpA", 256), ("qpB", 256),
                                ("zas", 512), ("zbs", 512), ("zms", 256), ("vb16", 512), ("kpT", 256), ("kpk", 256)):
                    if si == 0:
                        d_[nm_] = sb("%s_%d" % (nm_, si), [128, w_], BF16, ex)
                    else:
                        d_[nm_] = RB[:, off_:off_ + w_]
                        off_ += w_
                if si == 0:
                    d_["etot"] = sb("etot_%d" % si, [128, 256], F32, ex)
                else:
                    d_["etot"] = RB[:, off_:off_ + 512].bitcast(F32)
                d_["i"] = si
                for nm_ in ("qnA", "qnB", "qmA", "qmB", "qpA", "qpB"):
                    z_ = d_[nm_]
                    t.op("dve", lambda e: e.memset(z_[:], 0.0), w=[])
                BS.append(d_)

            def bn(bs, nm_):
                return "%s_%d" % (nm_, bs["i"])
            qn16k = sb("qn16k", [128, 512], BF16, ex)
            nsq = qn16k
            wk = (nsq, nr1)
            o32 = sb("o32", [128, 512], F32, ex); k32 = nr1
            sp32 = sb("sp32", [128, 256], F32, ex)
            E1T = sb("E1T", [128, 256], F32, ex); E2T = sb("E2T", [128, 256], F32, ex)
            E2k = sb("E2k", [128, 256], F32, ex)
            attT = sb("attT", [128, 512], BF16, ex)
            Tst = sb("Tst", [128, 2, 128], F32, ex)
            ssb = sb("ssb", [128, 12], F32, ex)
            gb16 = sb("gb16", [128, 512], BF16, ex); ga16 = sb("ga16", [128, 512], BF16, ex); gm16 = sb("gm16", [128, 256], BF16, ex)
            PTa = [sb("PTa%d" % i, [128, 640], BF16, ex) for i in range(2)]
            PTm = [sb("PTm%d" % i, [128, 256], BF16, ex) for i in range(2)]
            rda = sb("rda", [128, 8], F32, ex); rdm = sb("rdm", [128, 4], F32, ex)
            KS = sb("KSb", [128, 4, 128], BF16, ex); VS = sb("VSb", [128, 8, 65], BF16, ex)
            ck16 = WM[:, 0:4, :]; CKT = WM[:, 4:8, :]
            CV = RB[:, 0:2080].rearrange("p (a h d) -> p a h d", a=4, h=8)
            cm16 = RB[:, 2080:2592].rearrange("p (a f) -> p a f", a=2)
            CMKT = RB[:, 2592:3104].rearrange("p (a f) -> p a f", a=2)
            CMV = RB[:, 3104:3624].rearrange("p (a h d) -> p a h d", a=2, h=4)
            rbn = ["%s_1" % n_ for n_ in ("qn16", "qmAB", "qpT", "zas", "zbs", "zms", "vb16", "kpT", "kpk", "etot")]

            segs = {"qa": (0, 512), "kava": (512, 1536), "zaqb": (1536, 2304), "kbvbgl": (2304, 3600), "qmzm": (3600, 4112)}

            def wn(c0):
                for k_, (a, b_) in segs.items():
                    if a <= c0 < b_:
                        return "WX_" + k_
                raise KeyError(c0)
            t.dma("sp", nrow[:], bass.AP(g_mem.tensor, 0, [[0, 128], [1, 1024]]), w=["nrow"], sem="c15")
            wload(WM, w_mem, 8, 0, 512, "WM")
            wload(WX, w_in, 8, segs["kava"][0], segs["kava"][1], "WX_kava")
            hw_wload(WX, w_in, range(8), segs["kbvbgl"][0], segs["kbvbgl"][1], "WX_kbvbgl")
            for h in range(8):
                t.dma("pool", Ep[:, h, :], epd[:, h, :], w=["Ep"], sem="ce%d" % (h % 2))
            hw_wload(WX, w_in, range(8), segs["qa"][0], segs["qa"][1], "WX_qa")
            wload(WX, w_in, 8, segs["zaqb"][0], segs["zaqb"][1], "WX_zaqb")
            hw_wload(WX, w_in, range(8), segs["qmzm"][0], segs["qmzm"][1], "WX_qmzm")
            t.dma("sp", S32s[:], sgd.rearrange("j (p two) d v -> (two d) (j p) v", two=2), w=["S32s"], sem="c17")
            t.op("act", lambda e: e.copy(out=S16s[:], in_=S32s[:]), r=["S32s"], w=["S16s"])

            for mt in range(2):
                hT, hn, _, _ = prep(memd[mt * 128:(mt + 1) * 128, :], nrow, "nrow", mt % 2, mt % 2, mt % 2, q="act")
                p, pn = nextp()
                for c in range(2):
                    proj_feat(hT, hn, WM, "WM", c * 128, 128, p[:, c * 128:(c + 1) * 128], pn)
                qknorm(p, pn, 256, gkm, "gkm", qm16[:, 0:256], "qm16", wk)
                for c in range(2):
                    t.op("act", lambda e: e.copy(out=MKT[:, c, mt * 128:(mt + 1) * 128], in_=qm16[:, c * 128:(c + 1) * 128]),
                         r=["qm16"], w=["MKT"])
                transpose_out(qm16, "qm16", 2, k32, "nr1")
                t.dma("sp", mko[mt * 128:(mt + 1) * 128, :], k32[:, 0:256], r=["nr1"], sem="o_mk")

                def ev_mv(p, pn, mt=mt):
                    t.op("act", lambda e: e.copy(out=MV[:, mt, :, 0:64], in_=p[:, 0:256].rearrange("p (h d) -> p h d", h=4)),
                         r=[pn], w=["MV"])
                    t.op("dve", lambda e: e.tensor_copy(out=o32[:, 0:256], in_=p[:, 0:256]), r=[pn], w=["o32"])
                    t.dma("sp", mvo[mt * 128:(mt + 1) * 128, :], o32[:, 0:256], r=["o32"], sem="o_mv")
                proj_tok(hT, hn, WM, "WM", 256, 256, ev_mv, nextp)
                t.op("dve", lambda e: e.tensor_copy(out=MV[:, mt, :, 64:65], in_=ones8[:, 0:4].rearrange("p (h o) -> p h o", o=1)),
                     r=["ones8"], w=["MV"])
            t.dma("sp", nrow[:], bass.AP(norm_in.tensor, 0, [[0, 128], [1, 1024]]), r=[], w=["nrow"], sem="c15")

            def gla_gate(hT, hn):
                pg, pgn = nextp()
                proj_feat(hT, hn, WX, wn(C_GLR), C_GLR, 16, pg[0:16, 0:128], pgn)
                t.op("act", lambda e: e.copy(out=glrT[0:16, :], in_=pg[0:16, 0:128]), r=[pgn], w=["glrT"])
                pg2, pg2n = nextp()
                t.op("pe", lambda e: e.matmul(pg2[:, 0:256], lhsT=glrT[0:17, :], rhs=wg2[0:17, :], start=True, stop=True),
                     r=["glrT", "wg2"], w=[pg2n])
                t.op("act", lambda e: e.activation(out=sp32[:], in_=pg2[:, 0:256], func=AF.Exp, scale=-1.0), r=[pg2n], w=["sp32"])
                t.op("act", lambda e: e.activation(out=sp32[:], in_=sp32[:], func=AF.Ln, bias=1.0), r=["sp32"], w=["sp32"])

            def gla_cum(tri, trin, onem, onen, full, bs):
                etot = bs["etot"]
                pc, pcn = nextp()
                for pr in range(2):
                    if full:
                        t.op("pe", lambda e: e.matmul(pc[:, pr * 128:(pr + 1) * 128], lhsT=sp32[:, pr * 128:(pr + 1) * 128], rhs=tri[:],
                                                      start=True, stop=True), r=["sp32", trin], w=[pcn])
                    t.op("pe", lambda e: e.matmul(pc[:, 256 + pr * 128:256 + (pr + 1) * 128], lhsT=sp32[:, pr * 128:(pr + 1) * 128],
                                                  rhs=onem[:], start=True, stop=True), r=["sp32", onen], w=[pcn])
                pct, pctn = nextp()
                t.op("pe", lambda e: e.matmul(pct[:, 0:256], lhsT=tri[:], rhs=sp32[:], start=True, stop=True),
                     r=["sp32", trin], w=[pctn])
                if full:
                    t.op("act", lambda e: e.activation(out=E1T[:], in_=pc[:, 0:256], func=AF.Exp, scale=-1.0 / 16), r=[pcn], w=["E1T"])
                    t.op("act", lambda e: e.activation(out=E2T[:], in_=pc[:, 0:256], func=AF.Exp, scale=1.0 / 16), r=[pcn], w=["E2T"])
                t.op("act", lambda e: e.activation(out=etot[:], in_=pc[:, 256:512], func=AF.Exp, scale=-1.0 / 16), r=[pcn], w=[bn(bs, "etot")])
                t.op("act", lambda e: e.activation(out=E2k[:], in_=pct[:, 0:256], func=AF.Exp, scale=1.0 / 16), r=[pctn], w=["E2k"])

            def gla_proj(hT, hn, full, bs):
                kpk, vb16, kpT, qpA, qpB = bs["kpk"], bs["vb16"], bs["kpT"], bs["qpA"], bs["qpB"]

                def ev_kb(p, pn):
                    t.op("dve", lambda e: e.tensor_tensor(out=kpk[:], in0=p[:, 0:256], in1=E2k[:], op=ALU.mult),
                         r=[pn, "E2k"], w=[bn(bs, "kpk")])
                proj_tok(hT, hn, WX, wn(C_KB), C_KB, 256, ev_kb, nextp)
                yield

                def ev_vb(p, pn):
                    t.op("dve", lambda e: e.tensor_copy(out=vb16[:], in_=p[:, 0:512]), r=[pn], w=[bn(bs, "vb16")])
                proj_tok(hT, hn, WX, wn(C_VB), C_VB, 512, ev_vb, nextp)
                yield
                if full:
                    pqk, pqkn = nextp()
                    for c in range(2):
                        proj_feat(hT, hn, WX, wn(C_QB), C_QB + c * 128, 128, pqk[:, c * 128:(c + 1) * 128], pqkn)
                        proj_feat(hT, hn, WX, wn(C_KB), C_KB + c * 128, 128, pqk[:, 256 + c * 128:256 + (c + 1) * 128], pqkn)
                    t.op("dve", lambda e: e.scalar_tensor_tensor(out=qpA[0:64, :], in0=pqk[0:64, 0:256], scalar=0.125, in1=E1T[0:64, :],
                                                                 op0=ALU.mult, op1=ALU.mult), r=[pqkn, "E1T"], w=[bn(bs, "qpT")])
                    t.op("dve", lambda e: e.scalar_tensor_tensor(out=qpB[64:128, :], in0=pqk[64:128, 0:256], scalar=0.125, in1=E1T[64:128, :],
                                                                 op0=ALU.mult, op1=ALU.mult), r=[pqkn, "E1T"], w=[bn(bs, "qpT")])
                    t.op("dve", lambda e: e.tensor_tensor(out=kpT[:], in0=pqk[:, 256:512], in1=E2T[:], op=ALU.mult),
                         r=[pqkn, "E2T"], w=[bn(bs, "kpT")])
                    yield

            def gla_core(nseq, tri, trin, Sf, Sfn, Sb, Sbn, full, gt_tile, bs):
                kpk, vb16, kpT, qpA, qpB, zbs, etot = (bs["kpk"], bs["vb16"], bs["kpT"], bs["qpA"], bs["qpB"], bs["zbs"], bs["etot"])
                L = 128 // nseq
                if full:
                    pa, pan = nextg()
                    for h in range(4):
                        pr = h // 2
                        qp = (qpA, qpB)[h % 2]
                        t.op("pe", lambda e: e.matmul(pa[:, h * 128:(h + 1) * 128], lhsT=kpT[:, pr * 128:(pr + 1) * 128],
                                                      rhs=qp[:, pr * 128:(pr + 1) * 128], start=True, stop=True),
                             r=[bn(bs, "kpT"), bn(bs, "qpT")], w=[pan])
                    for h in range(4):
                        t.op("dve", lambda e: e.tensor_tensor(out=attT[:, h * 128:(h + 1) * 128], in0=pa[:, h * 128:(h + 1) * 128],
                                                              in1=tri[:], op=ALU.mult), r=[pan, trin], w=["attT"])
                    yield
                    po, pon = nextg()
                    for h in range(4):
                        pr, base = h // 2, (h % 2) * 64
                        t.op("pe", lambda e: e.matmul(po[:, h * 128:(h + 1) * 128], lhsT=attT[:, h * 128:(h + 1) * 128],
                                                      rhs=vb16[:, h * 128:(h + 1) * 128], start=True, stop=False),
                             r=["attT", bn(bs, "vb16")], w=[pon])
                        qp = (qpA, qpB)[h % 2]
                        for j in range(nseq):
                            t.op("pe", lambda e: e.matmul(po[j * L:(j + 1) * L, h * 128:(h + 1) * 128],
                                                          lhsT=qp[:, pr * 128 + j * L:pr * 128 + (j + 1) * L],
                                                          rhs=Sb[:, j * 2 + pr, :], start=False, stop=True,
                                                          tile_position=(0, j * L)),
                                 r=[bn(bs, "qpT"), Sbn], w=[pon])
                    for h in range(4):
                        t.op("act", lambda e: e.activation(out=gb16[:, h * 128:(h + 1) * 128], in_=po[:, h * 128:(h + 1) * 128], func=AF.Square,
                                                           accum_out=ssb[:, h:h + 1]), r=[pon], w=["ssb", "gb16"])
                    t.op("act", lambda e: e.activation(out=ssb[:, 4:8], in_=ssb[:, 0:4], func=AF.Ln, scale=1.0 / 128, bias=EPS),
                         r=["ssb"], w=["ssb"])
                    t.op("act", lambda e: e.activation(out=ssb[:, 8:12], in_=ssb[:, 4:8], func=AF.Exp, scale=-0.5), r=["ssb"], w=["ssb"])
                    for h in range(4):
                        t.op("dve", lambda e: e.scalar_tensor_tensor(out=gb16[:, h * 128:(h + 1) * 128], in0=po[:, h * 128:(h + 1) * 128],
                                                                     scalar=ssb[:, 8 + h:9 + h], in1=zbs[:, h * 128:(h + 1) * 128],
                                                                     op0=ALU.mult, op1=ALU.mult),
                             r=[pon, "ssb", bn(bs, "zbs")], w=["gb16"])
                    yield
                    p, pn = nextb()
                    for c in range(4):
                        t.op("pe", lambda e: e.transpose(out=p[:, c * 128:(c + 1) * 128], in_=gb16[:, c * 128:(c + 1) * 128],
                                                         identity=id16[:]), r=["gb16", "id16"], w=[pn])
                    t.op("act", lambda e: e.copy(out=GT[:, gt_tile, 4:8, :], in_=p[:, 0:512].rearrange("p (c t) -> p c t", c=4)),
                         r=[pn], w=gtw(gt_tile))
                    yield
                for j in range(nseq):
                    pu, pun = nextg()
                    if nseq == 1:
                        for pr in range(2):
                            t.op("pe", lambda e: e.matmul(pu[:, pr * 256:(pr + 1) * 256], lhsT=kpk[:, pr * 128:(pr + 1) * 128],
                                                          rhs=vb16[:, pr * 256:(pr + 1) * 256], start=True, stop=True),
                                 r=[bn(bs, "kpk"), bn(bs, "vb16")], w=[pun])
                    else:
                        for h in range(4):
                            pr, base = h // 2, (h % 2) * 64
                            sl = pr * 256 + (h % 2) * 128
                            t.op("pe", lambda e: e.matmul(pu[base:base + 64, sl:sl + 128], lhsT=kpk[j * L:(j + 1) * L, h * 64:(h + 1) * 64],
                                                          rhs=vb16[j * L:(j + 1) * L, h * 128:(h + 1) * 128], start=True, stop=True,
                                                          tile_position=(j * L, base)), r=[bn(bs, "kpk"), bn(bs, "vb16")], w=[pun])
                    for pr in range(2):
                        slot = j * 2 + pr
                        ec = etot[:, pr * 128 + j * L:pr * 128 + j * L + 1]
                        t.op("act", lambda e: e.activation(out=Tst[:, pr, :], in_=Sf[:, slot, :], func=AF.Copy, scale=ec),
                             r=[Sfn, bn(bs, "etot")], w=["Tst%d" % pr])
                        for hb in range(2):
                            rows = slice(hb * 64, hb * 64 + 64)
                            sl = pr * 256 + hb * 128
                            t.op("dve", lambda e: e.scalar_tensor_tensor(out=Sf[rows, slot, :], in0=pu[rows, sl:sl + 128], scalar=ec[rows, :],
                                                                         in1=Tst[rows, pr, :], op0=ALU.mult, op1=ALU.add),
                                 r=[pun, bn(bs, "etot"), "Tst%d" % pr], w=[Sfn])
                        t.op("act", lambda e: e.copy(out=Sb[:, slot, :], in_=Sf[:, slot, :]), r=[Sfn], w=[Sbn])
                    yield

            def kv_tile(hT, hn, bt, vcol, want_out, rows_list=None, dst=None):
                slot = bt % 6
                if dst is None:
                    Kd, kn, Vd, vn = KT[:, :, slot * 128:(slot + 1) * 128], "KT%d" % slot, V[:, slot, :, :], "V%d" % slot
                else:
                    Kd, kn, Vd, vn = dst
                p, pn = nextp()
                for c in range(4):
                    proj_feat(hT, hn, WX, wn(C_KA), C_KA + c * 128, 128, p[:, c * 128:(c + 1) * 128], pn)
                qknorm(p, pn, 512, gk, "gk", qn16k[:, :], "qn16k", wk)
                t.op("dve", lambda e: e.tensor_copy(out=Kd, in_=qn16k[:, :].rearrange("p (c t) -> p c t", c=4)), r=["qn16k"], w=[kn])
                yield

                def ev_va(p, pn):
                    t.op("dve", lambda e: e.tensor_copy(out=Vd[:, :, 0:64], in_=p[:, 0:512].rearrange("p (h d) -> p h d", h=8)),
                         r=[pn], w=[vn])
                    if want_out is not None:
                        t.op("dve", lambda e: e.tensor_copy(out=o32[:], in_=p[:, 0:512]), r=[pn], w=["o32"])
                proj_tok(hT, hn, WX, wn(C_VA), C_VA, 512, ev_va, nextp)
                t.op("dve", lambda e: e.tensor_scalar(out=Vd[:, :, 64:65], in0=ones8[:, :].rearrange("p (h o) -> p h o", o=1),
                                                      scalar1=cols[:, vcol:vcol + 1], scalar2=None, op0=ALU.mult),
                     r=["ones8", "cols"], w=[vn])
                if want_out is not None:
                    kdst, vdst = want_out
                    transpose_out(qn16k, "qn16k", 4, k32, "nr1")
                    for (dk, dv, rows) in zip(kdst, vdst, rows_list):
                        t.dma("sp", dk, k32[rows, :], r=["nr1"], sem="o_k")
                        t.dma("sp", dv, o32[rows, :], r=["o32"], sem="o_v")
                yield

            def q_and_gates(hT, hn, bs):
                p, pn = nextp()
                for c in range(4):
                    proj_feat(hT, hn, WX, wn(C_QA), C_QA + c * 128, 128, p[:, c * 128:(c + 1) * 128], pn)
                qknorm(p, pn, 512, gq8, "gq8", bs["qnA"], bn(bs, "qn16"), wk, outB=bs["qnB"])
                yield
                p, pn = nextp()
                for c in range(2):
                    proj_feat(hT, hn, WX, wn(C_QM), C_QM + c * 128, 128, p[:, c * 128:(c + 1) * 128], pn)
                qknorm(p, pn, 256, gqm8, "gqm8", bs["qmA"], bn(bs, "qmAB"), wk, outB=bs["qmB"])
                yield

            def silu_gates(hT, hn, bs):
                zas, zms, zbs = bs["zas"], bs["zms"], bs["zbs"]

                def silu_ev(dst, dname, n):
                    def ev(p, pn):
                        t.op("act", lambda e: e.activation(out=nr1[:, 0:n], in_=p[:, 0:n], func=AF.Exp, scale=-1.0), r=[pn], w=["nr1"])
                        t.op("act", lambda e: e.activation(out=nr1[:, 0:n], in_=nr1[:, 0:n], func=AF.Ln, bias=1.0), r=["nr1"], w=["nr1"])
                        t.op("act", lambda e: e.activation(out=nr1[:, 0:n], in_=nr1[:, 0:n], func=AF.Exp, scale=-1.0), r=["nr1"], w=["nr1"])
                        t.op("dve", lambda e: e.tensor_tensor(out=dst[:, 0:n], in0=p[:, 0:n], in1=nr1[:, 0:n], op=ALU.mult),
                             r=[pn, "nr1"], w=[dname])
                    return ev
                proj_tok(hT, hn, WX, wn(C_ZA), C_ZA, 512, silu_ev(zas, bn(bs, "zas"), 512), nextp)
                yield
                proj_tok(hT, hn, WX, wn(C_ZM), C_ZM, 256, silu_ev(zms, bn(bs, "zms"), 256), nextp)
                yield
                proj_tok(hT, hn, WX, wn(C_ZB), C_ZB, 512, silu_ev(zbs, bn(bs, "zbs"), 512), nextp)
                yield

            def P_stage(hT, hn, bs, bt, vcol, want_out, rows_list, tri, trin, onem, onen, full, kvdst=None):
                gla_gate(hT, hn)
                yield
                for _ in kv_tile(hT, hn, bt, vcol, want_out, rows_list, kvdst):
                    yield
                gla_cum(tri, trin, onem, onen, full, bs)
                yield
                if full:
                    for _ in q_and_gates(hT, hn, bs):
                        yield
                for _ in gla_proj(hT, hn, full, bs):
                    yield
                if full:
                    for _ in silu_gates(hT, hn, bs):
                        yield

            def finish_am(gt_tile):
                p, pn = nextb()
                for c in range(4):
                    t.op("pe", lambda e: e.transpose(out=p[:, c * 128:(c + 1) * 128], in_=ga16[:, c * 128:(c + 1) * 128],
                                                     identity=id16[:]), r=["ga16", "id16"], w=[pn])
                for c in range(2):
                    t.op("pe", lambda e: e.transpose(out=p[:, (4 + c) * 128:(5 + c) * 128], in_=gm16[:, c * 128:(c + 1) * 128],
                                                     identity=id16[:]), r=["gm16", "id16"], w=[pn])
                t.op("act", lambda e: e.copy(out=GT[:, gt_tile, 0:4, :], in_=p[:, 0:512].rearrange("p (c t) -> p c t", c=4)),
                     r=[pn], w=gtw(gt_tile))
                t.op("dve", lambda e: e.tensor_copy(out=GT[:, gt_tile, 8:10, :], in_=p[:, 512:768].rearrange("p (c t) -> p c t", c=2)),
                     r=[pn], w=gtw(gt_tile))
                yield

            def C_stage(i, bs):
                bt = i + NT_HALO

                def blocks_a(h, bt=bt):
                    pr = h // 2
                    out = []
                    for b in range(5):
                        s_ = (bt - 4 + b) % 6
                        out.append((KT[:, pr, s_ * 128:(s_ + 1) * 128], "KT%d" % s_, 128,
                                    Ep[:, h, b * 128:(b + 1) * 128], "Ep", id16[:], V[:, s_, h, :], "V%d" % s_, 0))
                    return out

                def blocks_m(h):
                    pr = h // 2
                    return [(MKT[:, pr, b * 128:(b + 1) * 128], "MKT", 128, None, None, None,
                             MV[:, b, h, :], "MV", 0) for b in range(2)]
                def par(*gs):
                    gs = list(gs)
                    while gs:
                        for g in list(gs):
                            try:
                                next(g)
                            except StopIteration:
                                gs.remove(g)
                        yield
                att = chain(par(
                    attend(8, (bs["qnA"], bs["qnB"]), bn(bs, "qn16"), slice(0, 128), blocks_a, slice(0, 128), PTa, "PTa", rda, "rda",
                           bs["zas"], bn(bs, "zas"), ga16, "ga16"),
                    attend(4, (bs["qmA"], bs["qmB"]), bn(bs, "qmAB"), slice(0, 128), blocks_m, slice(0, 128), PTm, "PTm", rdm, "rdm",
                           bs["zms"], bn(bs, "zms"), gm16, "gm16")),
                    finish_am(i))
                def delayed(g, n):
                    for _ in range(n):
                        yield
                    for _ in g:
                        yield
                return att, delayed(gla_core(1, tri_p, "tri_p", S32, "S32", S16, "S16", True, i, bs), 3)

            def fin_gen(fin):
                for _ in range(5):
                    yield
                fin()
                yield

            def hsrc(b):
                return xh[b * 128:(b + 1) * 128, :] if b < NT_HALO else xm[(b - NT_HALO) * 128:(b - NT_HALO + 1) * 128, :]
            hp = {0: prep(hsrc(0), nrow, "nrow", 0, 0, 0, q="act"), 1: prep(hsrc(1), nrow, "nrow", 1, 1, 1, q="act")}
            for b in (0, 2):
                run_all(P_stage(hp[b][0], hp[b][1], BS[0], b, 0, None, None, tri_p, "tri_p", one_p, "one_p", False),
                        P_stage(hp[b + 1][0], hp[b + 1][1], BS[1], b + 1, 0, None, None, tri_p, "tri_p", one_p, "one_p", False))
                hp[b + 2] = prep(hsrc(b + 2), nrow, "nrow", 0, 0, 0, q="act")
                run_all(gla_core(1, tri_p, "tri_p", S32, "S32", S16, "S16", False, None, BS[0]))
                hp[b + 3] = prep(hsrc(b + 3), nrow, "nrow", 1, 1, 1, q="act")
                run_all(gla_core(1, tri_p, "tri_p", S32, "S32", S16, "S16", False, None, BS[1]))

            def tsrc(k):
                return xm[k * 128:(k + 1) * 128, :] if k < NT_MAIN else xsd[:, :]

            def P_main(k, hT, hn):
                bt = k + NT_HALO
                if k >= NT_MAIN - 4:
                    r0 = (k - (NT_MAIN - 4)) * 128
                    return P_stage(hT, hn, BS[k % 2], bt, 1, ([ko_p[r0:r0 + 128, :]], [vo_p[r0:r0 + 128, :]]), [slice(0, 128)],
                                   tri_p, "tri_p", one_p, "one_p", True)
                return P_stage(hT, hn, BS[k % 2], bt, 1, None, None, tri_p, "tri_p", one_p, "one_p", True)

            hts = {0: hp[NT_HALO], 1: hp[NT_HALO + 1]}
            run_all(P_main(0, hts[0][0], hts[0][1]))
            for i in range(NT_MAIN):
                gens = list(C_stage(i, BS[i % 2]))
                if i + 2 <= NT_MAIN:
                    k = i + 2
                    hts[k] = prep(tsrc(k), nrow, "nrow", k % 2, k % 2, k % 2, defer=True)
                    gens.append(fin_gen(hts[k][4]))
                if i + 1 < NT_MAIN:
                    gens.append(P_main(i + 1, hts[i + 1][0], hts[i + 1][1]))
                else:
                    gens.append(P_stage(hts[NT_MAIN][0], hts[NT_MAIN][1], BS[0], NT_MAIN + NT_HALO, 2,
                                        ([ko_s[j] for j in range(4)], [vo_s[j] for j in range(4)]),
                                        [slice(j * 32, j * 32 + 16) for j in range(4)], tri_s, "tri_s", one_s, "one_s", True,
                                        kvdst=(KS[:, :, :], "KSb", VS[:, :, :], "VSb")))
                run_all(*gens)
            t.dma("sp", go_p.rearrange("(p two) d v -> (two d) p v", two=2), S32[:], r=["S32"], sem="o_g")

            gpool["banks"] = (6, 5)
            bs = BS[0]
            hT, hn = hts[NT_MAIN][0], hts[NT_MAIN][1]
            bt = NT_MAIN + NT_HALO
            allwx = ["WX_" + k_ for k_ in segs]
            allkv = ["KT%d" % i_ for i_ in range(6)] + ["V%d" % i_ for i_ in range(6)]
            KTf = KT[:, :, :].rearrange("p a b -> p (a b)")
            Vf = V[:, :, :, :].rearrange("p a h d -> p (a h d)")
            CS = [dict(CKT=CKT, CKTn=["CKT", "WM"], CV=CV, CVn=["CV"] + rbn, CMKT=CMKT, CMKTn=["CMKT"] + rbn, CMV=CMV, CMVn=["CMV"] + rbn),
                  dict(CKT=KTf[:, 0:2048].rearrange("p (a f) -> p a f", a=4), CKTn=["CKT1"] + allkv,
                       CV=Vf[:, 0:2080].rearrange("p (a h d) -> p a h d", a=4, h=8), CVn=["CV1"] + allkv,
                       CMKT=KTf[:, 2048:2560].rearrange("p (a f) -> p a f", a=2), CMKTn=["CMKT1"] + allkv,
                       CMV=Vf[:, 2080:2600].rearrange("p (a h d) -> p a h d", a=2, h=4), CMVn=["CMV1"] + allkv)]
            stgc = {"n": 0}
            stg = [(o32, "o32"), (nr1, "nr1"), (xs16[:, :].bitcast(F32), "xs16"), (hTs[1][:, :].bitcast(F32), "hT1"), (xt[1][:, 0:512], "xt1")]

            def ld_cast(dst, dnames, src, ncol):
                sg_, sgn = stg[stgc["n"] % len(stg)]
                eng = ("act", "dve")[stgc["n"] % 2]
                stgc["n"] += 1
                t.dma("sp", sg_[:, 0:ncol], src, w=[sgn], sem="stg" + sgn)
                src_v = sg_[:, 0:ncol] if len(dst.shape) == 2 else sg_[:, 0:ncol].rearrange("p (h d) -> p h d", d=64)
                if eng == "act":
                    t.op("act", lambda e: e.copy(out=dst, in_=src_v), r=[sgn], w=dnames)
                else:
                    t.op("dve", lambda e: e.tensor_copy(out=dst, in_=src_v), r=[sgn], w=dnames)

            def loads(j):
                c_ = CS[j % 2]
                for kb in range(4):
                    ld_cast(ck16[:, kb, :], ["ck16", "WM"], cak[j, kb * 128:(kb + 1) * 128, :], 512)
                for kb in range(2):
                    ld_cast(cm16[:, kb, :], ["cm16"] + rbn, cmk[j, kb * 128:(kb + 1) * 128, :], 256)
                yield
                for kb in range(4):
                    ld_cast(c_["CV"][:, kb, :, 0:64], c_["CVn"], cav[j, kb * 128:(kb + 1) * 128, :], 512)
                    t.op("dve", lambda e: e.tensor_copy(out=c_["CV"][:, kb, :, 64:65], in_=ones8[:, :].rearrange("p (h o) -> p h o", o=1)),
                         r=["ones8"], w=c_["CVn"])
                yield
                for kb in range(2):
                    ld_cast(c_["CMV"][:, kb, :, 0:64], c_["CMVn"], cmv[j, kb * 128:(kb + 1) * 128, :], 256)
                    t.op("dve", lambda e: e.tensor_copy(out=c_["CMV"][:, kb, :, 64:65], in_=ones8[:, 0:4].rearrange("p (h o) -> p h o", o=1)),
                         r=["ones8"], w=c_["CMVn"])
                yield
                for pr in range(4):
                    p, pn = nextb()
                    for kb in range(4):
                        t.op("pe", lambda e: e.transpose(out=p[:, kb * 128:(kb + 1) * 128], in_=ck16[:, kb, pr * 128:(pr + 1) * 128],
                                                         identity=id16[:]), r=["ck16", "id16"], w=[pn])
                    t.op("act", lambda e: e.copy(out=c_["CKT"][:, pr, :], in_=p[:, 0:512]), r=[pn], w=c_["CKTn"])
                    yield
                p, pn = nextb()
                for pr in range(2):
                    for kb in range(2):
                        t.op("pe", lambda e: e.transpose(out=p[:, (pr * 2 + kb) * 128:(pr * 2 + kb + 1) * 128],
                                                         in_=cm16[:, kb, pr * 128:(pr + 1) * 128], identity=id16[:]),
                             r=["cm16", "id16"], w=[pn])
                t.op("act", lambda e: e.copy(out=c_["CMKT"][:, :, :], in_=p[:, 0:512].rearrange("p (c t) -> p c t", c=2)),
                     r=[pn], w=c_["CMKTn"])
                yield

            def att_seq(j):
                c_ = CS[j % 2]
                qc = slice(j * 32, j * 32 + 32)

                def blocks_sa(h):
                    pr = h // 2
                    out = []
                    for b in range(4):
                        out.append((c_["CKT"][:, pr, b * 128:(b + 1) * 128], c_["CKTn"][0], 128,
                                    Ep[:, h, b * 128:(b + 1) * 128], "Ep", id16[:, 0:32], c_["CV"][:, b, h, :], c_["CVn"][0], 0))
                    out.append((KS[:, pr, j * 32:j * 32 + 32], "KSb", 32,
                                Ep[:, h, 512:544], "Ep", id16[:, 0:32], VS[j * 32:j * 32 + 32, h, :], "VSb", j * 32))
                    return out

                def blocks_sm(h):
                    pr = h // 2
                    return [(c_["CMKT"][:, pr, b * 128:(b + 1) * 128], c_["CMKTn"][0], 128, None, None, None,
                             c_["CMV"][:, b, h, :], c_["CMVn"][0], 0) for b in range(2)]
                def par2(*gs):
                    gs = list(gs)
                    while gs:
                        for g in list(gs):
                            try:
                                next(g)
                            except StopIteration:
                                gs.remove(g)
                        yield
                return par2(
                    attend(8, (bs["qnA"], bs["qnB"]), bn(bs, "qn16"), qc, blocks_sa, qc, PTa, "PTa", rda, "rda",
                           bs["zas"], bn(bs, "zas"), ga16, "ga16"),
                    attend(4, (bs["qmA"], bs["qmB"]), bn(bs, "qmAB"), qc, blocks_sm, qc, PTm, "PTm", rdm, "rdm",
                           bs["zms"], bn(bs, "zms"), gm16, "gm16"))
            run_all(loads(0))
            gsamp = gla_core(4, tri_s, "tri_s", S32s, "S32s", S16s, "S16s", True, 16, bs)
            for j in range(4):
                if j > 0:
                    g = 2 * (j - 1)
                    wload(WG, w_in, 8, NX + g * 512, NX + (g + 2) * 512, "WG%d" % g, d0=g * 512,
                          extra_w=allwx + ["WG%d" % (g + 1)])
                run_all(att_seq(j), loads(j + 1) if j + 1 < 4 else None, gsamp if j == 0 else None)
            wload(WO, w_o, 8, 0, 1024, "WO", extra_w=allwx)
            usrc = [(w_ua, k, "WUa") for k in range(4)] + [(w_ub, k, "WUb") for k in range(4)] + [(w_um, k, "WUm") for k in range(2)]
            wu_w = ["Ep"] + allkv + ["CKT1", "CV1", "CMKT1", "CMV1"]
            for c, (src_, k, nm) in enumerate(usrc):
                for hf in range(2):
                    sg_, sgn = stg[stgc["n"] % len(stg)]
                    eng = ("act", "dve")[stgc["n"] % 2]
                    stgc["n"] += 1
                    t.dma("sp", sg_[:, 0:512], src_[k * 128:(k + 1) * 128, hf * 512:(hf + 1) * 512], w=[sgn], sem="stg" + sgn)
                    dst = WUc[c][:, hf * 512:(hf + 1) * 512]
                    if nm == "WUb":
                        t.op("act", lambda e: e.activation(out=dst, in_=sg_[:, 0:512], func=AF.Copy, scale=ggo[:]),
                             r=[sgn, "ggo"], w=[nm] + wu_w)
                    elif eng == "act":
                        t.op("act", lambda e: e.copy(out=dst, in_=sg_[:, 0:512]), r=[sgn], w=[nm] + wu_w)
                    else:
                        t.op("dve", lambda e: e.tensor_copy(out=dst, in_=sg_[:, 0:512]), r=[sgn], w=[nm] + wu_w)
            run_all(finish_am(16))
            t.dma("sp", go_s.rearrange("j (p two) d v -> (two d) (j p) v", two=2), S32s[:], r=["S32s"], sem="o_gs")
            t.barrier()

        with ExitStack() as ey:
            wub32 = [sb("wub32_%d" % i, [128, 1024], F32, ey) for i in range(2)]
            nrow = sb("nrowY", [128, 1024], F32, ey)
            xtY = xt + [sb("xt2", [128, 1024], F32, ey)]
            sg16 = [sb("sg16_%d" % i, [128, 3072], BF16, ey) for i in range(2)]
            acc = [sb("acc%d" % i, [128, 512], F32, ey) for i in range(2)]
            tmp = [sb("tmp%d" % i, [128, 512], F32, ey) for i in range(2)]
            u16 = sb("u16", [128, 1024], BF16, ey)
            uT = [sb("uT%d" % i, [128, 1024], BF16, ey) for i in range(2)]
            y32 = [sb("y32_%d" % i, [128, 1024], F32, ey) for i in range(2)]
            t.dma("sp", nrow[:], bass.AP(norm_in.tensor, 0, [[0, 128], [1, 1024]]), w=["nrowY"], sem="c15")
            wun = ["WUa", "WUb", "WUm"]

            NTY = NT_MAIN + 1
            kch = [(0, 4), (4, 8), (8, 10)]

            def ysrc(i):
                return xm[i * 128:(i + 1) * 128, :] if i < NT_MAIN else xsd[:, :]

            def gates(i, hT, hn):
                sg = sg16[i % 2]
                for g in range(6):
                    def ev_g(p, pn, g=g):
                        t.op("act", lambda e: e.activation(out=sg[:, g * 512:(g + 1) * 512], in_=p[:, 0:512], func=AF.Sigmoid),
                             r=[pn], w=["sg%d_%d" % (i % 2, g)])
                    proj_tok(hT, hn, WG, "WG%d" % g, g * 512, 512, ev_g)

            def up_merge(i):
                sg = sg16[i % 2]
                for half in range(2):
                    f0 = half * 512
                    a_, an = acc[half], "acc%d" % half
                    m_, mn = tmp[half], "tmp%d" % half
                    pus = []
                    for b in range(3):
                        p, pn = nextf()
                        k0, k1 = kch[b]
                        for c in range(k0, k1):
                            t.op("pe", lambda e: e.matmul(p[:, 0:512], lhsT=GT[:, i, c, :], rhs=WUc[c][:, f0:f0 + 512],
                                                          start=(c == k0), stop=(c == k1 - 1)), r=["GT", wun[b]], w=[pn])
                        pus.append((p, pn))
                    t.op("dve", lambda e: e.tensor_tensor(out=a_[:], in0=pus[0][0][:, 0:512], in1=sg[:, f0:f0 + 512], op=ALU.mult),
                         r=[pus[0][1], "sg%d_%d" % (i % 2, half)], w=[an])
                    t.op("dve", lambda e: e.tensor_tensor(out=m_[:], in0=pus[1][0][:, 0:512], in1=sg[:, 1024 + f0:1024 + f0 + 512], op=ALU.mult),
                         r=[pus[1][1], "sg%d_%d" % (i % 2, 2 + half)], w=[mn])
                    t.op("pool", lambda e: e.tensor_tensor(out=a_[:], in0=a_[:], in1=m_[:], op=ALU.add), r=[an, mn], w=[an])
                    t.op("dve", lambda e: e.tensor_tensor(out=m_[:], in0=pus[2][0][:, 0:512], in1=sg[:, 2048 + f0:2048 + f0 + 512], op=ALU.mult),
                         r=[pus[2][1], "sg%d_%d" % (i % 2, 4 + half)], w=[mn])
                    t.op("pool", lambda e: e.tensor_tensor(out=u16[:, f0:f0 + 512], in0=a_[:], in1=m_[:], op=ALU.add),
                         r=[an, mn], w=["u16"])

            def u_transpose(i):
                p, pn = nextb()
                for k in range(8):
                    t.op("pe", lambda e: e.transpose(out=p[:, k * 128:(k + 1) * 128], in_=u16[:, k * 128:(k + 1) * 128], identity=id16[:]),
                         r=["u16", "id16"], w=[pn])
                t.op("act", lambda e: e.copy(out=uT[i % 2][:], in_=p[:]), r=[pn], w=["uT%d" % (i % 2)])

            def out_proj(i, x_, xn):
                y_, yn = y32[i % 2], "y32_%d" % (i % 2)
                u_, un = uT[i % 2], "uT%d" % (i % 2)
                for half in range(2):
                    f0 = half * 512
                    p, pn = nextf()
                    for k in range(8):
                        t.op("pe", lambda e: e.matmul(p[:, 0:512], lhsT=u_[:, k * 128:(k + 1) * 128], rhs=WO[:, k, f0:f0 + 512],
                                                      start=(k == 0), stop=(k == 7)), r=[un, "WO"], w=[pn])
                    t.op("dve", lambda e: e.tensor_tensor(out=y_[:, f0:f0 + 512], in0=p[:, 0:512], in1=x_[:, f0:f0 + 512], op=ALU.add),
                         r=[pn, xn], w=[yn])
                if i < NT_MAIN:
                    t.dma("sp", y_p[i * 128:(i + 1) * 128, :], y_[:], r=[yn], sem="o_y%d" % (i % 2))
                else:
                    for j in range(4):
                        t.dma("sp", y_s[j], y_[j * 32:j * 32 + 16, :], r=[yn], sem="o_y%d" % (i % 2))

            preps = {0: prep(ysrc(0), nrow, "nrowY", 0, 0, 0, xtY, use_pool=True)}
            for i in range(NTY):
                hT, hn, x_, xn = preps[i][0:4]
                fin = None
                if i + 1 < NTY:
                    preps[i + 1] = prep(ysrc(i + 1), nrow, "nrowY", (i + 1) % 3, (i + 1) % 2, (i + 1) % 3, xtY,
                                        use_pool=True, defer=True)
                    fin = preps[i + 1][4]
                gates(i, hT, hn)
                if fin is not None:
                    fin()
                up_merge(i)
                if i >= 1:
                    out_proj(i - 1, preps[i - 1][2], preps[i - 1][3])
                u_transpose(i)
            out_proj(NTY - 1, preps[NTY - 1][2], preps[NTY - 1][3])
            t.barrier(("sp",))
    except _Stop:
        pass
    return nc


_NC_CACHE = {}


def _host_consts():
    ident = np.eye(128, dtype=np.float32)
    s = np.arange(128)
    trip = (s[:, None] <= s[None, :]).astype(np.float32)
    onep = np.ones((128, 128), np.float32)
    same = (s[:, None] // 32) == (s[None, :] // 32)
    tris = (same & (s[:, None] <= s[None, :])).astype(np.float32)
    ones_s = (same & ((s[:, None] % 32) < 16)).astype(np.float32)
    blk64 = ((s[:, None] // 64) == (s[None, :] // 64)).astype(np.float32)
    return ident, trip, onep, tris, ones_s, blk64


def _bias_tables(rel_bias):
    tab = np.asarray(rel_bias)[0]
    j = np.arange(128)[:, None]
    i = np.arange(640)[None, :]
    idx = np.clip(512 + j - i, -128, 128) + 128
    ep = tab[:, idx]
    invalid = ((i < 64) & (j >= 64)) | ((i >= 576) & (j < 64))
    ep = np.where(invalid[None], np.float32(NEG), ep)
    ep = np.ascontiguousarray(ep.transpose(1, 0, 2)).astype(np.float32)
    j2 = np.arange(32)[:, None]
    i2 = np.arange(544)[None, :]
    idx2 = np.clip(512 + j2 - i2, -128, 128) + 128
    es_ = np.ascontiguousarray(tab[:, idx2].transpose(1, 0, 2)).astype(np.float32)
    return ep, es_


def _make_in_maps(x_prompt, x_sample, mem_prompt, cache_a_k, cache_a_v, state_gla, cache_mem_k, cache_mem_v,
                  norm_in, w_in, g_qa, g_ka, rel_bias, w_gate2, b_gate, g_gla_out, g_mem, w_mem_kv, g_qm, g_km,
                  w_up_a, w_up_b, w_up_m, w_out):
    f = lambda a: np.ascontiguousarray(np.asarray(a, dtype=np.float32))
    x_prompt, x_sample, mem_prompt = f(x_prompt), f(x_sample), f(mem_prompt)
    ident, trip, onep, tris, ones_s, blk64 = _host_consts()
    ep, es_ = _bias_tables(f(rel_bias))
    shared = {
        "w_in": f(w_in)[0], "w_mem": f(w_mem_kv)[0], "w_ua": f(w_up_a)[0], "w_ub": f(w_up_b)[0],
        "w_um": f(w_up_m)[0], "w_o": f(w_out)[0], "norm_in": f(norm_in), "g_mem": f(g_mem),
        "g_qa": f(g_qa).reshape(64, 1), "g_ka": f(g_ka).reshape(64, 1), "g_qm": f(g_qm).reshape(64, 1),
        "g_km": f(g_km).reshape(64, 1), "w_g2": f(w_gate2)[0], "b_g": f(b_gate), "g_go": f(g_gla_out).reshape(128, 1),
        "ep": ep, "ident": ident, "trip": trip, "onep": onep, "tris": tris, "ones_s": ones_s, "blk64": blk64,
    }
    in_maps = []
    for c in range(NCORES):
        sq, seg = c // 4, c % 4
        t0 = seg * 2048
        xh = x_prompt[sq, t0 - 512:t0] if seg > 0 else np.zeros((512, 1024), np.float32)
        xs = np.zeros((4, 32, 1024), np.float32)
        xs[:, :16] = x_sample[c * 4:(c + 1) * 4]
        cols = np.zeros((128, 4), np.float32)
        cols[:, 0] = 1.0 if seg > 0 else 0.0
        cols[:, 1] = 1.0
        cols[:, 2] = (np.arange(128) % 32 < 16).astype(np.float32)
        m = dict(shared)
        m.update({
            "xh": f(xh), "xm": f(x_prompt[sq, t0:t0 + 2048]), "xs": xs.reshape(128, 1024),
            "mem": f(mem_prompt[sq]),
            "cak": f(cache_a_k[0, c * 4:(c + 1) * 4]).reshape(4, 512, 512),
            "cav": f(cache_a_v[0, c * 4:(c + 1) * 4]).reshape(4, 512, 512),
            "sg": f(state_gla[0, c * 4:(c + 1) * 4]),
            "cmk": f(cache_mem_k[0, c * 4:(c + 1) * 4]).reshape(4, 256, 256),
            "cmv": f(cache_mem_v[0, c * 4:(c + 1) * 4]).reshape(4, 256, 256),
            "cols": cols,
        })
        in_maps.append(m)
    return in_maps


def kernel(**inputs):
    in_maps = _make_in_maps(**inputs)
    if "nc" not in _NC_CACHE:
        _NC_CACHE["nc"] = build_nc()
    nc = _NC_CACHE["nc"]
    res = run_bass_kernel_spmd(nc, in_maps, core_ids=list(range(NCORES)))
    return _assemble(res.results)


def _assemble(R):
    y_prompt = np.stack([np.concatenate([R[s * 4 + g]["y_p"] for g in range(4)], 0) for s in range(2)])
    y_sample = np.concatenate([R[c]["y_s"] for c in range(NCORES)], 0)
    akp = np.stack([R[s * 4 + 3]["ko_p"].reshape(512, 8, 64) for s in range(2)])[None]
    avp = np.stack([R[s * 4 + 3]["vo_p"].reshape(512, 8, 64) for s in range(2)])[None]
    sgp = np.stack([R[s * 4 + 3]["go_p"] for s in range(2)])[None]
    mkp = np.stack([R[s * 4]["mko"].reshape(256, 4, 64) for s in range(2)])[None]
    mvp = np.stack([R[s * 4]["mvo"].reshape(256, 4, 64) for s in range(2)])[None]
    aks = np.concatenate([R[c]["ko_s"].reshape(4, 16, 8, 64) for c in range(NCORES)], 0)[None]
    avs = np.concatenate([R[c]["vo_s"].reshape(4, 16, 8, 64) for c in range(NCORES)], 0)[None]
    sgs = np.concatenate([R[c]["go_s"] for c in range(NCORES)], 0)[None]
    out = (y_prompt, y_sample, akp, avp, sgp, mkp, mvp, aks, avs, sgs)
    return tuple(np.ascontiguousarray(o, dtype=np.float32) for o in out)
```

```python
import numpy as np
from contextlib import ExitStack
import concourse.bass as bass
import concourse.mybir as mybir
from concourse.bass_utils import run_bass_kernel_spmd

F32 = mybir.dt.float32
BF16 = mybir.dt.bfloat16
AF = mybir.ActivationFunctionType
ALU = mybir.AluOpType

EPS = 1e-6
NCORES = 8
NT_MAIN = 16
NT_HALO = 4
C_QA, C_KA, C_VA, C_ZA, C_QB, C_KB, C_VB, C_GLR, C_ZB, C_QM, C_ZM, C_GA, C_GB, C_GM = (
    0, 512, 1024, 1536, 2048, 2304, 2560, 3072, 3088, 3600, 3856, 4112, 5136, 6160)
D_IN = 7184
NX = 4112
NEG = -30000.0


class Tr:
    def __init__(self, nc, es):
        self.nc = nc
        self.es = es
        self.engs = {"pe": nc.tensor, "act": nc.scalar, "dve": nc.vector,
                     "pool": nc.gpsimd, "sp": nc.sync}
        self.sems = {}
        self.cnt = {}
        for e in self.engs:
            self.sems[e] = es.enter_context(nc.semaphore("s_" + e))
            self.cnt[e] = 0
        self.waited = {e: {} for e in self.engs}
        self.lastw = {}
        self.readers = {}

    def _deps(self, r, w):
        deps = {}

        def add(tag):
            if tag is not None and deps.get(tag[0], 0) < tag[1]:
                deps[tag[0]] = tag[1]
        for b in r:
            add(self.lastw.get(b))
        for b in w:
            add(self.lastw.get(b))
            for t in self.readers.get(b, ()):
                add(t)
        return deps

    def _wait(self, eng, deps):
        E = self.engs[eng]
        for s, v in deps.items():
            if eng == "pe" and s == "pe":
                continue
            if self.waited[eng].get(s, 0) < v:
                E.wait_ge(self.sems[s], v)
                self.waited[eng][s] = v

    def _commit(self, tag, r, w):
        for b in w:
            self.lastw[b] = tag
            self.readers[b] = []
        for b in r:
            self.readers.setdefault(b, []).append(tag)

    def op(self, eng, fn, r=(), w=()):
        ps = [b for b in r if b[:2] in ("pf", "pb")]
        if ps:
            w = list(w) + ps
        self._wait(eng, self._deps(r, w))
        ins = fn(self.engs[eng])
        self.cnt[eng] += 1
        ins.then_inc(self.sems[eng], 1)
        self._commit((eng, self.cnt[eng]), r, w)
        return ins

    def dma(self, q, out, in_, r=(), w=(), sem=None, **kw):
        if sem not in self.sems:
            self.sems[sem] = self.es.enter_context(self.nc.semaphore("d_" + sem))
            self.cnt[sem] = 0
        deps = self._deps(r, w)
        if self.cnt[sem] > 0:
            deps[sem] = max(deps.get(sem, 0), self.cnt[sem])
        self._wait(q, deps)
        ins = self.engs[q].dma_start(out=out, in_=in_, **kw)
        self.cnt[sem] += 16
        ins.then_inc(self.sems[sem], 16)
        self._commit((sem, self.cnt[sem]), r, w)
        return ins

    def barrier(self, engs=("pe", "act", "dve", "pool", "sp")):
        for e in engs:
            for s, v in self.cnt.items():
                if v > 0 and self.waited[e].get(s, 0) < v and not (e == "pe" and s == "pe"):
                    self.engs[e].wait_ge(self.sems[s], v)
                    self.waited[e][s] = v


class _Stop(Exception):
    pass


def build_nc(kstop=None):
    nc = bass.Bass("TRN2", target_bir_lowering=False)

    def din(name, shape):
        return nc.dram_tensor(name, list(shape), F32, kind="ExternalInput").ap()

    def dout(name, shape):
        return nc.dram_tensor(name, list(shape), F32, kind="ExternalOutput").ap()

    xh = din("xh", [512, 1024]); xm = din("xm", [2048, 1024]); xsd = din("xs", [128, 1024])
    memd = din("mem", [256, 1024])
    cak = din("cak", [4, 512, 512]); cav = din("cav", [4, 512, 512])
    sgd = din("sg", [4, 4, 64, 128])
    cmk = din("cmk", [4, 256, 256]); cmv = din("cmv", [4, 256, 256])
    w_in = din("w_in", [1024, D_IN]); w_mem = din("w_mem", [1024, 512])
    w_ua = din("w_ua", [512, 1024]); w_ub = din("w_ub", [512, 1024]); w_um = din("w_um", [256, 1024])
    w_o = din("w_o", [1024, 1024])
    norm_in = din("norm_in", [1, 1024]); g_mem = din("g_mem", [1, 1024])
    g_qa = din("g_qa", [64, 1]); g_ka = din("g_ka", [64, 1]); g_qm = din("g_qm", [64, 1]); g_km = din("g_km", [64, 1])
    w_g2 = din("w_g2", [16, 256]); b_g = din("b_g", [1, 256]); g_go = din("g_go", [128, 1])
    epd = din("ep", [128, 8, 640])
    identd = din("ident", [128, 128]); trip = din("trip", [128, 128]); onep = din("onep", [128, 128])
    tris = din("tris", [128, 128]); ones_s = din("ones_s", [128, 128]); blk64d = din("blk64", [128, 128])
    colsd = din("cols", [128, 4])

    y_p = dout("y_p", [2048, 1024]); y_s = dout("y_s", [4, 16, 1024])
    ko_p = dout("ko_p", [512, 512]); vo_p = dout("vo_p", [512, 512])
    go_p = dout("go_p", [4, 64, 128])
    mko = dout("mko", [256, 256]); mvo = dout("mvo", [256, 256])
    ko_s = dout("ko_s", [4, 16, 512]); vo_s = dout("vo_s", [4, 16, 512])
    go_s = dout("go_s", [4, 4, 64, 128])

    try:
      with ExitStack() as es:
        t = Tr(nc, es)

        def stage(k):
            if kstop is not None and k == kstop:
                t.barrier(("sp",))
                raise _Stop()

        def sb(name, shape, dt, ctx=es):
            return ctx.enter_context(nc.sbuf_tensor("sb_" + name, list(shape), dt))

        pf = [es.enter_context(nc.psum_tensor("pf%d" % i, [128, 512], F32)) for i in range(7)]
        pb = [es.enter_context(nc.psum_tensor("pb%d" % i, [128, 1024], BF16)) for i in range(1)]
        rr = {"f": 0, "g": 0, "s": 0, "p": 0}

        def nextf():
            i = rr["f"]; rr["f"] = (i + 1) % 4
            return pf[i], "pf%d" % i

        def nextp():
            i = (0, 1, 2, 3, 6)[rr["p"]]; rr["p"] = (rr["p"] + 1) % 5
            return pf[i], "pf%d" % i

        def nexts():
            return nextp()

        rr["a"] = 0

        def nexta():
            i = (4, 5)[rr["a"]]; rr["a"] = (rr["a"] + 1) % 2
            return pf[i], "pf%d" % i

        gpool = {"banks": (6,)}

        def nextg():
            return nextp()

        def nextb():
            return pb[0], "pb0"

        id16 = sb("id16", [128, 128], BF16)
        blk64 = sb("blk64", [128, 128], BF16)
        tri_p = sb("tri_p", [128, 128], F32); one_p = sb("one_p", [128, 128], F32)
        tri_s = sb("tri_s", [128, 128], F32); one_s = sb("one_s", [128, 128], F32)
        cols = sb("cols", [128, 4], F32)
        ones8 = sb("ones8", [128, 8], F32)
        gq8 = sb("gq8", [128, 1], F32); gk = sb("gk", [128, 1], F32)
        gqm8 = sb("gqm8", [128, 1], F32); gkm = sb("gkm", [128, 1], F32)
        ggo = sb("ggo", [128, 1], F32)
        wg2 = sb("wg2", [32, 256], F32)
        glrT = sb("glrT", [32, 128], F32)
        GT = sb("GT", [128, 17, 10, 128], BF16)
        hwstg_all = GT[:, 10:17, :, :].rearrange("p t c k -> p (t c k)")[:, 0:8192].bitcast(F32)
        hwstg = [hwstg_all[:, i * 1024:(i + 1) * 1024] for i in range(4)]
        hwn = ["hwstg%d" % i for i in range(4)]
        hwc = {"n": 0}

        def gtw(tile_):
            return ["GT"] + (hwn if tile_ >= 10 else [])

        def hw_wload(dst3, src, ks, c0, c1, bname, d0=None):
            d0 = c0 if d0 is None else d0
            for k in ks:
                for a in range(c0, c1, 1024):
                    b_ = min(a + 1024, c1)
                    i_ = hwc["n"] % 4
                    hwc["n"] += 1
                    t.dma("sp", hwstg[i_][:, 0:b_ - a], src[k * 128:(k + 1) * 128, a:b_], w=[hwn[i_]], sem="hw%d" % i_)
                    dst = dst3[:, k, d0 + a - c0:d0 + b_ - c0]
                    if i_ % 2 == 0:
                        t.op("act", lambda e: e.copy(out=dst, in_=hwstg[i_][:, 0:b_ - a]), r=[hwn[i_]], w=[bname])
                    else:
                        t.op("dve", lambda e: e.tensor_copy(out=dst, in_=hwstg[i_][:, 0:b_ - a]), r=[hwn[i_]], w=[bname])
        S32 = sb("S32", [128, 2, 128], F32); S16 = sb("S16", [128, 2, 128], BF16)
        S32s = sb("S32s", [128, 8, 128], F32); S16s = sb("S16s", [128, 8, 128], BF16)
        xt = [sb("xt%d" % i, [128, 1024], F32) for i in range(2)]
        xs16 = sb("xs16", [128, 1024], BF16)
        hTs = [sb("hT%d" % i, [128, 1024], BF16) for i in range(2)]
        st4 = [sb("st4_%d" % i, [128, 4], F32) for i in range(3)]
        WXG = sb("WXG", [128, 8 * NX], BF16)
        WX = WXG[:, :].rearrange("p (k n) -> p k n", k=8)
        WG = WXG[:, 0:8 * 3072].rearrange("p (k n) -> p k n", k=8)
        WO = WXG[:, 8 * 3072:8 * 4096].rearrange("p (k n) -> p k n", k=8)

        t.dma("pool", id16[:], identd, w=["id16"], sem="c0")
        t.dma("pool", blk64[:], blk64d, w=["blk64"], sem="c1")
        t.dma("sp", tri_p[:], trip, w=["tri_p"], sem="c2")
        t.dma("sp", one_p[:], onep, w=["one_p"], sem="c3")
        t.dma("sp", tri_s[:], tris, w=["tri_s"], sem="c4")
        t.dma("sp", one_s[:], ones_s, w=["one_s"], sem="c5")
        t.dma("sp", cols[:], colsd, w=["cols"], sem="c6")
        for half in range(2):
            sl = slice(half * 64, half * 64 + 64)
            t.dma("sp", gq8[sl, :], g_qa, w=["gq8"], sem="c9")
            t.dma("sp", gk[sl, :], g_ka, w=["gk"], sem="c10")
            t.dma("sp", gqm8[sl, :], g_qm, w=["gqm8"], sem="c11")
            t.dma("sp", gkm[sl, :], g_km, w=["gkm"], sem="c12")
        t.dma("sp", ggo[:], g_go, w=["ggo"], sem="c13")
        t.dma("sp", wg2[0:16, :], w_g2, w=["wg2"], sem="c14")
        t.dma("sp", wg2[16:17, :], b_g, w=["wg2"], sem="c14")
        t.op("dve", lambda e: e.tensor_scalar_mul(out=gq8[:], in0=gq8[:], scalar1=0.125), r=["gq8"], w=["gq8"])
        t.op("dve", lambda e: e.tensor_scalar_mul(out=gqm8[:], in0=gqm8[:], scalar1=0.125), r=["gqm8"], w=["gqm8"])
        t.op("dve", lambda e: e.memset(ones8[:], 1.0), w=["ones8"])
        t.op("dve", lambda e: e.memset(glrT[:], 1.0), w=["glrT"])
        t.op("dve", lambda e: e.memset(S32[:], 0.0), w=["S32"])
        t.op("dve", lambda e: e.memset(S16[:], 0.0), w=["S16"])

        wsem = {"n": 0}

        def wload(dst3, src, nk, c0, c1, bname, d0=None, extra_w=(), ks=None):
            d0 = c0 if d0 is None else d0
            for k in (range(nk) if ks is None else ks):
                for a in range(c0, c1, 2048):
                    b_ = min(a + 2048, c1)
                    t.dma("pool", dst3[:, k, d0 + a - c0:d0 + b_ - c0], src[k * 128:(k + 1) * 128, a:b_],
                          w=[bname] + list(extra_w), sem="w%d" % (wsem["n"] % 16))
                    wsem["n"] += 1

        mhalf = sb("mhalf", [128, 1], F32)
        t.op("dve", lambda e: e.memset(mhalf[:], -0.5), w=["mhalf"])

        def prep(src_rows, nrow, nrow_name, xs_, hs_, ss_, xpool=None, use_pool=False, defer=False, q="sp"):
            xpool = xt if xpool is None else xpool
            x_, xn = xpool[xs_], "xt%d" % xs_
            hT, hn = hTs[hs_], "hT%d" % hs_
            sv, svn = st4[ss_], "st4_%d" % ss_
            t.dma(q, x_[:], src_rows, w=[xn], sem="x%d" % xs_)
            t.op("act", lambda e: e.activation(out=hT[:], in_=x_[:], func=AF.Square, accum_out=sv[:, 0:1]),
                 r=[xn], w=[svn, hn])
            if use_pool:
                t.op("pool", lambda e: e.tensor_scalar(out=sv[:, 1:2], in0=sv[:, 0:1], scalar1=1.0 / 1024, scalar2=EPS,
                                                       op0=ALU.mult, op1=ALU.add), r=[svn], w=[svn])
                t.op("pool", lambda e: e.tensor_tensor(out=sv[:, 2:3], in0=sv[:, 1:2], in1=mhalf[:], op=ALU.pow),
                     r=[svn, "mhalf"], w=[svn])
            else:
                t.op("act", lambda e: e.activation(out=sv[:, 1:2], in_=sv[:, 0:1], func=AF.Ln, scale=1.0 / 1024, bias=EPS),
                     r=[svn], w=[svn])
                t.op("act", lambda e: e.activation(out=sv[:, 2:3], in_=sv[:, 1:2], func=AF.Exp, scale=-0.5), r=[svn], w=[svn])
            t.op("dve", lambda e: e.scalar_tensor_tensor(out=xs16[:], in0=x_[:], scalar=sv[:, 2:3], in1=nrow[:],
                                                         op0=ALU.mult, op1=ALU.mult),
                 r=[xn, svn, nrow_name], w=["xs16"])
            def fin():
                p, pn = nextb()
                for k in range(8):
                    t.op("pe", lambda e: e.transpose(out=p[:, k * 128:(k + 1) * 128], in_=xs16[:, k * 128:(k + 1) * 128],
                                                     identity=id16[:]), r=["xs16", "id16"], w=[pn])
                t.op("dve", lambda e: e.tensor_copy(out=hT[:], in_=p[:]), r=[pn], w=[hn])
            if defer:
                return hT, hn, x_, xn, fin
            fin()
            return hT, hn, x_, xn

        def proj_tok(hT, hn, W, wn, c0, n, evac, pool=None):
            p, pn = (pool or nextf)()
            for k in range(8):
                t.op("pe", lambda e: e.matmul(p[:, 0:n], lhsT=hT[:, k * 128:(k + 1) * 128], rhs=W[:, k, c0:c0 + n],
                                              start=(k == 0), stop=(k == 7)), r=[hn, wn], w=[pn])
            evac(p, pn)

        def proj_feat(hT, hn, W, wn, c0, m, pout, pn):
            for k in range(8):
                t.op("pe", lambda e: e.matmul(pout, lhsT=W[:, k, c0:c0 + m], rhs=hT[:, k * 128:(k + 1) * 128],
                                              start=(k == 0), stop=(k == 7)), r=[hn, wn], w=[pn])

        def qknorm(p, pn, n, gcol, gname, out16, outname, wk, outB=None):
            sq, r1 = wk
            t.op("act", lambda e: e.activation(out=sq[:, 0:n], in_=p[:, 0:n], func=AF.Square), r=[pn], w=["qn16k"])
            p2, p2n = nextp()
            t.op("pe", lambda e: e.matmul(p2[:, 0:n], lhsT=blk64[:], rhs=sq[:, 0:n], start=True, stop=True),
                 r=["qn16k", "blk64"], w=[p2n])
            t.op("act", lambda e: e.activation(out=r1[:, 0:n], in_=p2[:, 0:n], func=AF.Ln, scale=1.0 / 64, bias=EPS),
                 r=[p2n], w=["nr1"])
            t.op("act", lambda e: e.activation(out=r1[:, 0:n], in_=r1[:, 0:n], func=AF.Exp, scale=-0.5), r=["nr1"], w=["nr1"])
            if outB is None:
                t.op("dve", lambda e: e.scalar_tensor_tensor(out=out16, in0=p[:, 0:n], scalar=gcol[:], in1=r1[:, 0:n],
                                                             op0=ALU.mult, op1=ALU.mult),
                     r=[pn, gname, "nr1"], w=[outname])
            else:
                t.op("dve", lambda e: e.scalar_tensor_tensor(out=out16[0:64, 0:n], in0=p[0:64, 0:n], scalar=gcol[0:64, :],
                                                             in1=r1[0:64, 0:n], op0=ALU.mult, op1=ALU.mult),
                     r=[pn, gname, "nr1"], w=[outname])
                t.op("dve", lambda e: e.scalar_tensor_tensor(out=outB[64:128, 0:n], in0=p[64:128, 0:n], scalar=gcol[64:128, :],
                                                             in1=r1[64:128, 0:n], op0=ALU.mult, op1=ALU.mult),
                     r=[pn, gname, "nr1"], w=[outname])

        def transpose_out(src16, sname, nchunk, dst32, dname):
            p, pn = nextb()
            for c in range(nchunk):
                t.op("pe", lambda e: e.transpose(out=p[:, c * 128:(c + 1) * 128], in_=src16[:, c * 128:(c + 1) * 128],
                                                 identity=id16[:]), r=[sname, "id16"], w=[pn])
            t.op("act", lambda e: e.copy(out=dst32[:, 0:nchunk * 128], in_=p[:, 0:nchunk * 128]), r=[pn], w=[dname])

        def attend(nh, qT, qname, qcols, blocks, orow, PTs, ptn, otile, oname, zs16, zname, gout, goname):
            nq = qcols.stop - qcols.start
            po = pon = pov = None
            for h in range(nh):
                hh = h % 4
                g0 = h - hh
                pr = h // 2
                if hh == 0:
                    po, pon = nexta()
                    pov = po[:, 0:260].rearrange("p (h d) -> p h d", h=4)
                bl = blocks(h)
                banks = []
                col = 0
                cur = None
                for bi, (kT, kname, nk, E, ename, I, V, vname, kbase) in enumerate(bl):
                    if cur is None or col + nq > 512:
                        cur = nexts(); col = 0
                        banks.append([cur, []])
                    p, pn = cur
                    o_ap = p[kbase:kbase + nk, col:col + nq]
                    t.op("pe", lambda e: e.matmul(o_ap, lhsT=kT, rhs=qT[h % 2][:, pr * 128 + qcols.start:pr * 128 + qcols.stop],
                                                  start=True, stop=(E is None), tile_position=(0, kbase)),
                         r=[kname, qname], w=[pn])
                    if E is not None:
                        t.op("pe", lambda e: e.matmul(o_ap, lhsT=E, rhs=I, start=False, stop=True,
                                                      tile_position=(0, kbase)),
                             r=[ename, "id16"], w=[pn])
                    banks[-1][1].append((bi, col, nk, kbase))
                    col += nq
                PT, pt_name = PTs[h % 2], "%s%d" % (ptn, h % 2)
                for (p, pn), lst in banks:
                    i = 0
                    while i < len(lst):
                        j = i
                        while j + 1 < len(lst) and lst[j + 1][2] == lst[i][2] and lst[j + 1][3] == lst[i][3]:
                            j += 1
                        bi0, c0, nk, kb_ = lst[i]
                        c1 = lst[j][1] + nq
                        t.op("act", lambda e: e.activation(out=PT[kb_:kb_ + nk, bi0 * nq:bi0 * nq + (c1 - c0)],
                                                           in_=p[kb_:kb_ + nk, c0:c1], func=AF.Exp),
                             r=[pn], w=[pt_name])
                        i = j + 1
                yield
                for bi, (kT, kname, nk, E, ename, I, V, vname, kbase) in enumerate(bl):
                    t.op("pe", lambda e: e.matmul(pov[orow, hh, :], lhsT=PT[kbase:kbase + nk, bi * nq:(bi + 1) * nq], rhs=V,
                                                  start=(bi == 0), stop=(bi == len(bl) - 1),
                                                  tile_position=(kbase, orow.start)),
                         r=[pt_name, vname], w=[pon])
                if hh == 3:
                    rd = otile
                    t.op("dve", lambda e: e.reciprocal(out=rd[orow, g0:g0 + 4], in_=pov[orow, :, 64]), r=[pon], w=[oname])
                    for h2 in range(4):
                        hx = g0 + h2
                        t.op("dve", lambda e: e.scalar_tensor_tensor(out=gout[orow, hx * 64:(hx + 1) * 64], in0=pov[orow, h2, 0:64],
                                                                     scalar=rd[orow, hx:hx + 1], in1=zs16[orow, hx * 64:(hx + 1) * 64],
                                                                     op0=ALU.mult, op1=ALU.mult),
                             r=[pon, oname, zname], w=[goname])

        def run_all(*gens):
            gens = [g for g in gens if g is not None]
            while gens:
                for g in list(gens):
                    try:
                        next(g)
                    except StopIteration:
                        gens.remove(g)

        def chain(*gens):
            for g in gens:
                for _ in g:
                    yield

        Ep = sb("Ep", [128, 8, 640], BF16)
        KT = sb("KT", [128, 4, 6 * 128], BF16)
        V = sb("V", [128, 6, 8, 65], BF16)
        Epf_ = Ep[:, :, :].rearrange("p h n -> p (h n)")
        KTf_ = KT[:, :, :].rearrange("p a b -> p (a b)")
        Vf_ = V[:, :, :, :].rearrange("p a h d -> p (a h d)")
        WUc = [Epf_[:, c_ * 1024:(c_ + 1) * 1024] for c_ in range(5)] + \
              [KTf_[:, c_ * 1024:(c_ + 1) * 1024] for c_ in range(3)] + [Vf_[:, c_ * 1024:(c_ + 1) * 1024] for c_ in range(2)]
        with ExitStack() as ex:
            WM = sb("WM", [128, 8, 512], BF16, ex)
            nrow = sb("nrow", [128, 1024], F32, ex)
            MKT = sb("MKT", [128, 2, 256], BF16, ex); MV = sb("MV", [128, 2, 4, 65], BF16, ex)
            nr1 = sb("nr1", [128, 512], F32, ex)
            qm16 = sb("qm16", [128, 256], BF16, ex)
            RB = sb("RB", [128, 4864], BF16, ex)
            BS = []
            for si in range(2):
                d_ = {}
                off_ = 0
                for nm_, w_ in (("qnA", 512), ("qnB", 512), ("qmA", 256), ("qmB", 256), ("qpA", 256), ("qpB", 256),
                                ("zas", 512), ("zbs", 512), ("zms", 256), ("vb16", 512), ("kpT", 256), ("kpk", 256)):
                    if si == 0:
                        d_[nm_] = sb("%s_%d" % (nm_, si), [128, w_], BF16, ex)
                    else:
                        d_[nm_] = RB[:, off_:off_ + w_]
                        off_ += w_
                if si == 0:
                    d_["etot"] = sb("etot_%d" % si, [128, 256], F32, ex)
                else:
                    d_["etot"] = RB[:, off_:off_ + 512].bitcast(F32)
                d_["i"] = si
                for nm_ in ("qnA", "qnB", "qmA", "qmB", "qpA", "qpB"):
                    z_ = d_[nm_]
                    t.op("dve", lambda e: e.memset(z_[:], 0.0), w=[])
                BS.append(d_)

            def bn(bs, nm_):
                return "%s_%d" % (nm_, bs["i"])
            qn16k = sb("qn16k", [128, 512], BF16, ex)
            nsq = qn16k
            wk = (nsq, nr1)
            o32 = sb("o32", [128, 512], F32, ex); k32 = nr1
            sp32 = sb("sp32", [128, 256], F32, ex)
            E1T = sb("E1T", [128, 256], F32, ex); E2T = sb("E2T", [128, 256], F32, ex)
            E2k = sb("E2k", [128, 256], F32, ex)
            attT = sb("attT", [128, 512], BF16, ex)
            Tst = sb("Tst", [128, 2, 128], F32, ex)
            ssb = sb("ssb", [128, 12], F32, ex)
            gb16 = sb("gb16", [128, 512], BF16, ex); ga16 = sb("ga16", [128, 512], BF16, ex); gm16 = sb("gm16", [128, 256], BF16, ex)
            PTa = [sb("PTa%d" % i, [128, 640], BF16, ex) for i in range(2)]
            PTm = [sb("PTm%d" % i, [128, 256], BF16, ex) for i in range(2)]
            rda = sb("rda", [128, 8], F32, ex); rdm = sb("rdm", [128, 4], F32, ex)
            KS = sb("KSb", [128, 4, 128], BF16, ex); VS = sb("VSb", [128, 8, 65], BF16, ex)
            ck16 = WM[:, 0:4, :]; CKT = WM[:, 4:8, :]
            CV = RB[:, 0:2080].rearrange("p (a h d) -> p a h d", a=4, h=8)
            cm16 = RB[:, 2080:2592].rearrange("p (a f) -> p a f", a=2)
            CMKT = RB[:, 2592:3104].rearrange("p (a f) -> p a f", a=2)
            CMV = RB[:, 3104:3624].rearrange("p (a h d) -> p a h d", a=2, h=4)
            rbn = ["%s_1" % n_ for n_ in ("qn16", "qmAB", "qpT", "zas", "zbs", "zms", "vb16", "kpT", "kpk", "etot")]

            segs = {"qa": (0, 512), "kava": (512, 1536), "zaqb": (1536, 2304), "kbvbgl": (2304, 3600), "qmzm": (3600, 4112)}

            def wn(c0):
                for k_, (a, b_) in segs.items():
                    if a <= c0 < b_:
                        return "WX_" + k_
                raise KeyError(c0)
            t.dma("sp", nrow[:], bass.AP(g_mem.tensor, 0, [[0, 128], [1, 1024]]), w=["nrow"], sem="c15")
            wload(WM, w_mem, 8, 0, 512, "WM")
            wload(WX, w_in, 8, segs["kava"][0], segs["kava"][1], "WX_kava")
            hw_wload(WX, w_in, range(8), segs["kbvbgl"][0], segs["kbvbgl"][1], "WX_kbvbgl")
            for h in range(8):
                t.dma("pool", Ep[:, h, :], epd[:, h, :], w=["Ep"], sem="ce%d" % (h % 2))
            hw_wload(WX, w_in, range(8), segs["qa"][0], segs["qa"][1], "WX_qa")
            wload(WX, w_in, 8, segs["zaqb"][0], segs["zaqb"][1], "WX_zaqb")
            hw_wload(WX, w_in, range(8), segs["qmzm"][0], segs["qmzm"][1], "WX_qmzm")
            t.dma("sp", S32s[:], sgd.rearrange("j (p two) d v -> (two d) (j p) v", two=2), w=["S32s"], sem="c17")
            t.op("act", lambda e: e.copy(out=S16s[:], in_=S32s[:]), r=["S32s"], w=["S16s"])

            for mt in range(2):
                hT, hn, _, _ = prep(memd[mt * 128:(mt + 1) * 128, :], nrow, "nrow", mt % 2, mt % 2, mt % 2, q="act")
                p, pn = nextp()
                for c in range(2):
                    proj_feat(hT, hn, WM, "WM", c * 128, 128, p[:, c * 128:(c + 1) * 128], pn)
                qknorm(p, pn, 256, gkm, "gkm", qm16[:, 0:256], "qm16", wk)
                for c in range(2):
                    t.op("act", lambda e: e.copy(out=MKT[:, c, mt * 128:(mt + 1) * 128], in_=qm16[:, c * 128:(c + 1) * 128]),
                         r=["qm16"], w=["MKT"])
                transpose_out(qm16, "qm16", 2, k32, "nr1")
                t.dma("sp", mko[mt * 128:(mt + 1) * 128, :], k32[:, 0:256], r=["nr1"], sem="o_mk")

                def ev_mv(p, pn, mt=mt):
                    t.op("act", lambda e: e.copy(out=MV[:, mt, :, 0:64], in_=p[:, 0:256].rearrange("p (h d) -> p h d", h=4)),
                         r=[pn], w=["MV"])
                    t.op("dve", lambda e: e.tensor_copy(out=o32[:, 0:256], in_=p[:, 0:256]), r=[pn], w=["o32"])
                    t.dma("sp", mvo[mt * 128:(mt + 1) * 128, :], o32[:, 0:256], r=["o32"], sem="o_mv")
                proj_tok(hT, hn, WM, "WM", 256, 256, ev_mv, nextp)
                t.op("dve", lambda e: e.tensor_copy(out=MV[:, mt, :, 64:65], in_=ones8[:, 0:4].rearrange("p (h o) -> p h o", o=1)),
                     r=["ones8"], w=["MV"])
            t.dma("sp", nrow[:], bass.AP(norm_in.tensor, 0, [[0, 128], [1, 1024]]), r=[], w=["nrow"], sem="c15")

            def gla_gate(hT, hn):
                pg, pgn = nextp()
                proj_feat(hT, hn, WX, wn(C_GLR), C_GLR, 16, pg[0:16, 0:128], pgn)
                t.op("act", lambda e: e.copy(out=glrT[0:16, :], in_=pg[0:16, 0:128]), r=[pgn], w=["glrT"])
                pg2, pg2n = nextp()
                t.op("pe", lambda e: e.matmul(pg2[:, 0:256], lhsT=glrT[0:17, :], rhs=wg2[0:17, :], start=True, stop=True),
                     r=["glrT", "wg2"], w=[pg2n])
                t.op("act", lambda e: e.activation(out=sp32[:], in_=pg2[:, 0:256], func=AF.Exp, scale=-1.0), r=[pg2n], w=["sp32"])
                t.op("act", lambda e: e.activation(out=sp32[:], in_=sp32[:], func=AF.Ln, bias=1.0), r=["sp32"], w=["sp32"])

            def gla_cum(tri, trin, onem, onen, full, bs):
                etot = bs["etot"]
                pc, pcn = nextp()
                for pr in range(2):
                    if full:
                        t.op("pe", lambda e: e.matmul(pc[:, pr * 128:(pr + 1) * 128], lhsT=sp32[:, pr * 128:(pr + 1) * 128], rhs=tri[:],
                                                      start=True, stop=True), r=["sp32", trin], w=[pcn])
                    t.op("pe", lambda e: e.matmul(pc[:, 256 + pr * 128:256 + (pr + 1) * 128], lhsT=sp32[:, pr * 128:(pr + 1) * 128],
                                                  rhs=onem[:], start=True, stop=True), r=["sp32", onen], w=[pcn])
                pct, pctn = nextp()
                t.op("pe", lambda e: e.matmul(pct[:, 0:256], lhsT=tri[:], rhs=sp32[:], start=True, stop=True),
                     r=["sp32", trin], w=[pctn])
                if full:
                    t.op("act", lambda e: e.activation(out=E1T[:], in_=pc[:, 0:256], func=AF.Exp, scale=-1.0 / 16), r=[pcn], w=["E1T"])
                    t.op("act", lambda e: e.activation(out=E2T[:], in_=pc[:, 0:256], func=AF.Exp, scale=1.0 / 16), r=[pcn], w=["E2T"])
                t.op("act", lambda e: e.activation(out=etot[:], in_=pc[:, 256:512], func=AF.Exp, scale=-1.0 / 16), r=[pcn], w=[bn(bs, "etot")])
                t.op("act", lambda e: e.activation(out=E2k[:], in_=pct[:, 0:256], func=AF.Exp, scale=1.0 / 16), r=[pctn], w=["E2k"])

            def gla_proj(hT, hn, full, bs):
                kpk, vb16, kpT, qpA, qpB = bs["kpk"], bs["vb16"], bs["kpT"], bs["qpA"], bs["qpB"]

                def ev_kb(p, pn):
                    t.op("dve", lambda e: e.tensor_tensor(out=kpk[:], in0=p[:, 0:256], in1=E2k[:], op=ALU.mult),
                         r=[pn, "E2k"], w=[bn(bs, "kpk")])
                proj_tok(hT, hn, WX, wn(C_KB), C_KB, 256, ev_kb, nextp)
                yield

                def ev_vb(p, pn):
                    t.op("dve", lambda e: e.tensor_copy(out=vb16[:], in_=p[:, 0:512]), r=[pn], w=[bn(bs, "vb16")])
                proj_tok(hT, hn, WX, wn(C_VB), C_VB, 512, ev_vb, nextp)
                yield
                if full:
                    pqk, pqkn = nextp()
                    for c in range(2):
                        proj_feat(hT, hn, WX, wn(C_QB), C_QB + c * 128, 128, pqk[:, c * 128:(c + 1) * 128], pqkn)
                        proj_feat(hT, hn, WX, wn(C_KB), C_KB + c * 128, 128, pqk[:, 256 + c * 128:256 + (c + 1) * 128], pqkn)
                    t.op("dve", lambda e: e.scalar_tensor_tensor(out=qpA[0:64, :], in0=pqk[0:64, 0:256], scalar=0.125, in1=E1T[0:64, :],
                                                                 op0=ALU.mult, op1=ALU.mult), r=[pqkn, "E1T"], w=[bn(bs, "qpT")])
                    t.op("dve", lambda e: e.scalar_tensor_tensor(out=qpB[64:128, :], in0=pqk[64:128, 0:256], scalar=0.125, in1=E1T[64:128, :],
                                                                 op0=ALU.mult, op1=ALU.mult), r=[pqkn, "E1T"], w=[bn(bs, "qpT")])
                    t.op("dve", lambda e: e.tensor_tensor(out=kpT[:], in0=pqk[:, 256:512], in1=E2T[:], op=ALU.mult),
                         r=[pqkn, "E2T"], w=[bn(bs, "kpT")])
                    yield

            def gla_core(nseq, tri, trin, Sf, Sfn, Sb, Sbn, full, gt_tile, bs):
                kpk, vb16, kpT, qpA, qpB, zbs, etot = (bs["kpk"], bs["vb16"], bs["kpT"], bs["qpA"], bs["qpB"], bs["zbs"], bs["etot"])
                L = 128 // nseq
                if full:
                    pa, pan = nextg()
                    for h in range(4):
                        pr = h // 2
                        qp = (qpA, qpB)[h % 2]
                        t.op("pe", lambda e: e.matmul(pa[:, h * 128:(h + 1) * 128], lhsT=kpT[:, pr * 128:(pr + 1) * 128],
                                                      rhs=qp[:, pr * 128:(pr + 1) * 128], start=True, stop=True),
                             r=[bn(bs, "kpT"), bn(bs, "qpT")], w=[pan])
                    for h in range(4):
                        t.op("dve", lambda e: e.tensor_tensor(out=attT[:, h * 128:(h + 1) * 128], in0=pa[:, h * 128:(h + 1) * 128],
                                                              in1=tri[:], op=ALU.mult), r=[pan, trin], w=["attT"])
                    yield
                    po, pon = nextg()
                    for h in range(4):
                        pr, base = h // 2, (h % 2) * 64
                        t.op("pe", lambda e: e.matmul(po[:, h * 128:(h + 1) * 128], lhsT=attT[:, h * 128:(h + 1) * 128],
                                                      rhs=vb16[:, h * 128:(h + 1) * 128], start=True, stop=False),
                             r=["attT", bn(bs, "vb16")], w=[pon])
                        qp = (qpA, qpB)[h % 2]
                        for j in range(nseq):
                            t.op("pe", lambda e: e.matmul(po[j * L:(j + 1) * L, h * 128:(h + 1) * 128],
                                                          lhsT=qp[:, pr * 128 + j * L:pr * 128 + (j + 1) * L],
                                                          rhs=Sb[:, j * 2 + pr, :], start=False, stop=True,
                                                          tile_position=(0, j * L)),
                                 r=[bn(bs, "qpT"), Sbn], w=[pon])
                    for h in range(4):
                        t.op("act", lambda e: e.activation(out=gb16[:, h * 128:(h + 1) * 128], in_=po[:, h * 128:(h + 1) * 128], func=AF.Square,
                                                           accum_out=ssb[:, h:h + 1]), r=[pon], w=["ssb", "gb16"])
                    t.op("act", lambda e: e.activation(out=ssb[:, 4:8], in_=ssb[:, 0:4], func=AF.Ln, scale=1.0 / 128, bias=EPS),
                         r=["ssb"], w=["ssb"])
                    t.op("act", lambda e: e.activation(out=ssb[:, 8:12], in_=ssb[:, 4:8], func=AF.Exp, scale=-0.5), r=["ssb"], w=["ssb"])
                    for h in range(4):
                        t.op("dve", lambda e: e.scalar_tensor_tensor(out=gb16[:, h * 128:(h + 1) * 128], in0=po[:, h * 128:(h + 1) * 128],
                                                                     scalar=ssb[:, 8 + h:9 + h], in1=zbs[:, h * 128:(h + 1) * 128],
                                                                     op0=ALU.mult, op1=ALU.mult),
                             r=[pon, "ssb", bn(bs, "zbs")], w=["gb16"])
                    yield
                    p, pn = nextb()
                    for c in range(4):
                        t.op("pe", lambda e: e.transpose(out=p[:, c * 128:(c + 1) * 128], in_=gb16[:, c * 128:(c + 1) * 128],
                                                         identity=id16[:]), r=["gb16", "id16"], w=[pn])
                    t.op("act", lambda e: e.copy(out=GT[:, gt_tile, 4:8, :], in_=p[:, 0:512].rearrange("p (c t) -> p c t", c=4)),
                         r=[pn], w=gtw(gt_tile))
                    yield
                for j in range(nseq):
                    pu, pun = nextg()
                    if nseq == 1:
                        for pr in range(2):
                            t.op("pe", lambda e: e.matmul(pu[:, pr * 256:(pr + 1) * 256], lhsT=kpk[:, pr * 128:(pr + 1) * 128],
                                                          rhs=vb16[:, pr * 256:(pr + 1) * 256], start=True, stop=True),
                                 r=[bn(bs, "kpk"), bn(bs, "vb16")], w=[pun])
                    else:
                        for h in range(4):
                            pr, base = h // 2, (h % 2) * 64
                            sl = pr * 256 + (h % 2) * 128
                            t.op("pe", lambda e: e.matmul(pu[base:base + 64, sl:sl + 128], lhsT=kpk[j * L:(j + 1) * L, h * 64:(h + 1) * 64],
                                                          rhs=vb16[j * L:(j + 1) * L, h * 128:(h + 1) * 128], start=True, stop=True,
                                                          tile_position=(j * L, base)), r=[bn(bs, "kpk"), bn(bs, "vb16")], w=[pun])
                    for pr in range(2):
                        slot = j * 2 + pr
                        ec = etot[:, pr * 128 + j * L:pr * 128 + j * L + 1]
                        t.op("act", lambda e: e.activation(out=Tst[:, pr, :], in_=Sf[:, slot, :], func=AF.Copy, scale=ec),
                             r=[Sfn, bn(bs, "etot")], w=["Tst%d" % pr])
                        for hb in range(2):
                            rows = slice(hb * 64, hb * 64 + 64)
                            sl = pr * 256 + hb * 128
                            t.op("dve", lambda e: e.scalar_tensor_tensor(out=Sf[rows, slot, :], in0=pu[rows, sl:sl + 128], scalar=ec[rows, :],
                                                                         in1=Tst[rows, pr, :], op0=ALU.mult, op1=ALU.add),
                                 r=[pun, bn(bs, "etot"), "Tst%d" % pr], w=[Sfn])
                        t.op("act", lambda e: e.copy(out=Sb[:, slot, :], in_=Sf[:, slot, :]), r=[Sfn], w=[Sbn])
                    yield

            def kv_tile(hT, hn, bt, vcol, want_out, rows_list=None, dst=None):
                slot = bt % 6
                if dst is None:
                    Kd, kn, Vd, vn = KT[:, :, slot * 128:(slot + 1) * 128], "KT%d" % slot, V[:, slot, :, :], "V%d" % slot
                else:
                    Kd, kn, Vd, vn = dst
                p, pn = nextp()
                for c in range(4):
                    proj_feat(hT, hn, WX, wn(C_KA), C_KA + c * 128, 128, p[:, c * 128:(c + 1) * 128], pn)
                qknorm(p, pn, 512, gk, "gk", qn16k[:, :], "qn16k", wk)
                t.op("dve", lambda e: e.tensor_copy(out=Kd, in_=qn16k[:, :].rearrange("p (c t) -> p c t", c=4)), r=["qn16k"], w=[kn])
                yield

                def ev_va(p, pn):
                    t.op("dve", lambda e: e.tensor_copy(out=Vd[:, :, 0:64], in_=p[:, 0:512].rearrange("p (h d) -> p h d", h=8)),
                         r=[pn], w=[vn])
                    if want_out is not None:
                        t.op("dve", lambda e: e.tensor_copy(out=o32[:], in_=p[:, 0:512]), r=[pn], w=["o32"])
                proj_tok(hT, hn, WX, wn(C_VA), C_VA, 512, ev_va, nextp)
                t.op("dve", lambda e: e.tensor_scalar(out=Vd[:, :, 64:65], in0=ones8[:, :].rearrange("p (h o) -> p h o", o=1),
                                                      scalar1=cols[:, vcol:vcol + 1], scalar2=None, op0=ALU.mult),
                     r=["ones8", "cols"], w=[vn])
                if want_out is not None:
                    kdst, vdst = want_out
                    transpose_out(qn16k, "qn16k", 4, k32, "nr1")
                    for (dk, dv, rows) in zip(kdst, vdst, rows_list):
                        t.dma("sp", dk, k32[rows, :], r=["nr1"], sem="o_k")
                        t.dma("sp", dv, o32[rows, :], r=["o32"], sem="o_v")
                yield

            def q_and_gates(hT, hn, bs):
                p, pn = nextp()
                for c in range(4):
                    proj_feat(hT, hn, WX, wn(C_QA), C_QA + c * 128, 128, p[:, c * 128:(c + 1) * 128], pn)
                qknorm(p, pn, 512, gq8, "gq8", bs["qnA"], bn(bs, "qn16"), wk, outB=bs["qnB"])
                yield
                p, pn = nextp()
                for c in range(2):
                    proj_feat(hT, hn, WX, wn(C_QM), C_QM + c * 128, 128, p[:, c * 128:(c + 1) * 128], pn)
                qknorm(p, pn, 256, gqm8, "gqm8", bs["qmA"], bn(bs, "qmAB"), wk, outB=bs["qmB"])
                yield

            def silu_gates(hT, hn, bs):
                zas, zms, zbs = bs["zas"], bs["zms"], bs["zbs"]

                def silu_ev(dst, dname, n):
                    def ev(p, pn):
                        t.op("act", lambda e: e.activation(out=nr1[:, 0:n], in_=p[:, 0:n], func=AF.Exp, scale=-1.0), r=[pn], w=["nr1"])
                        t.op("act", lambda e: e.activation(out=nr1[:, 0:n], in_=nr1[:, 0:n], func=AF.Ln, bias=1.0), r=["nr1"], w=["nr1"])
                        t.op("act", lambda e: e.activation(out=nr1[:, 0:n], in_=nr1[:, 0:n], func=AF.Exp, scale=-1.0), r=["nr1"], w=["nr1"])
                        t.op("dve", lambda e: e.tensor_tensor(out=dst[:, 0:n], in0=p[:, 0:n], in1=nr1[:, 0:n], op=ALU.mult),
                             r=[pn, "nr1"], w=[dname])
                    return ev
                proj_tok(hT, hn, WX, wn(C_ZA), C_ZA, 512, silu_ev(zas, bn(bs, "zas"), 512), nextp)
                yield
                proj_tok(hT, hn, WX, wn(C_ZM), C_ZM, 256, silu_ev(zms, bn(bs, "zms"), 256), nextp)
                yield
                proj_tok(hT, hn, WX, wn(C_ZB), C_ZB, 512, silu_ev(zbs, bn(bs, "zbs"), 512), nextp)
                yield

            def P_stage(hT, hn, bs, bt, vcol, want_out, rows_list, tri, trin, onem, onen, full, kvdst=None):
                gla_gate(hT, hn)
                yield
                for _ in kv_tile(hT, hn, bt, vcol, want_out, rows_list, kvdst):
                    yield
                gla_cum(tri, trin, onem, onen, full, bs)
                yield
                if full:
                    for _ in q_and_gates(hT, hn, bs):
                        yield
                for _ in gla_proj(hT, hn, full, bs):
                    yield
                if full:
                    for _ in silu_gates(hT, hn, bs):
                        yield

            def finish_am(gt_tile):
                p, pn = nextb()
                for c in range(4):
                    t.op("pe", lambda e: e.transpose(out=p[:, c * 128:(c + 1) * 128], in_=ga16[:, c * 128:(c + 1) * 128],
                                                     identity=id16[:]), r=["ga16", "id16"], w=[pn])
                for c in range(2):
                    t.op("pe", lambda e: e.transpose(out=p[:, (4 + c) * 128:(5 + c) * 128], in_=gm16[:, c * 128:(c + 1) * 128],
                                                     identity=id16[:]), r=["gm16", "id16"], w=[pn])
                t.op("act", lambda e: e.copy(out=GT[:, gt_tile, 0:4, :], in_=p[:, 0:512].rearrange("p (c t) -> p c t", c=4)),
                     r=[pn], w=gtw(gt_tile))
                t.op("dve", lambda e: e.tensor_copy(out=GT[:, gt_tile, 8:10, :], in_=p[:, 512:768].rearrange("p (c t) -> p c t", c=2)),
                     r=[pn], w=gtw(gt_tile))
                yield

            def C_stage(i, bs):
                bt = i + NT_HALO

                def blocks_a(h, bt=bt):
                    pr = h // 2
                    out = []
                    for b in range(5):
                        s_ = (bt - 4 + b) % 6
                        out.append((KT[:, pr, s_ * 128:(s_ + 1) * 128], "KT%d" % s_, 128,
                                    Ep[:, h, b * 128:(b + 1) * 128], "Ep", id16[:], V[:, s_, h, :], "V%d" % s_, 0))
                    return out

                def blocks_m(h):
                    pr = h // 2
                    return [(MKT[:, pr, b * 128:(b + 1) * 128], "MKT", 128, None, None, None,
                             MV[:, b, h, :], "MV", 0) for b in range(2)]
                def par(*gs):
                    gs = list(gs)
                    while gs:
                        for g in list(gs):
                            try:
                                next(g)
                            except StopIteration:
                                gs.remove(g)
                        yield
                att = chain(par(
                    attend(8, (bs["qnA"], bs["qnB"]), bn(bs, "qn16"), slice(0, 128), blocks_a, slice(0, 128), PTa, "PTa", rda, "rda",
                           bs["zas"], bn(bs, "zas"), ga16, "ga16"),
                    attend(4, (bs["qmA"], bs["qmB"]), bn(bs, "qmAB"), slice(0, 128), blocks_m, slice(0, 128), PTm, "PTm", rdm, "rdm",
                           bs["zms"], bn(bs, "zms"), gm16, "gm16")),
                    finish_am(i))
                def delayed(g, n):
                    for _ in range(n):
                        yield
                    for _ in g:
                        yield
                return att, delayed(gla_core(1, tri_p, "tri_p", S32, "S32", S16, "S16", True, i, bs), 3)

            def fin_gen(fin):
                for _ in range(8):
                    yield
                fin()
                yield

            def hsrc(b):
                return xh[b * 128:(b + 1) * 128, :] if b < NT_HALO else xm[(b - NT_HALO) * 128:(b - NT_HALO + 1) * 128, :]
            hp = {0: prep(hsrc(0), nrow, "nrow", 0, 0, 0, q="act"), 1: prep(hsrc(1), nrow, "nrow", 1, 1, 1, q="act")}
            for b in (0, 2):
                run_all(P_stage(hp[b][0], hp[b][1], BS[0], b, 0, None, None, tri_p, "tri_p", one_p, "one_p", False),
                        P_stage(hp[b + 1][0], hp[b + 1][1], BS[1], b + 1, 0, None, None, tri_p, "tri_p", one_p, "one_p", False))
                hp[b + 2] = prep(hsrc(b + 2), nrow, "nrow", 0, 0, 0, q="act")
                run_all(gla_core(1, tri_p, "tri_p", S32, "S32", S16, "S16", False, None, BS[0]))
                hp[b + 3] = prep(hsrc(b + 3), nrow, "nrow", 1, 1, 1, q="act")
                run_all(gla_core(1, tri_p, "tri_p", S32, "S32", S16, "S16", False, None, BS[1]))

            def tsrc(k):
                return xm[k * 128:(k + 1) * 128, :] if k < NT_MAIN else xsd[:, :]

            def P_main(k, hT, hn):
                bt = k + NT_HALO
                if k >= NT_MAIN - 4:
                    r0 = (k - (NT_MAIN - 4)) * 128
                    return P_stage(hT, hn, BS[k % 2], bt, 1, ([ko_p[r0:r0 + 128, :]], [vo_p[r0:r0 + 128, :]]), [slice(0, 128)],
                                   tri_p, "tri_p", one_p, "one_p", True)
                return P_stage(hT, hn, BS[k % 2], bt, 1, None, None, tri_p, "tri_p", one_p, "one_p", True)

            hts = {0: hp[NT_HALO], 1: hp[NT_HALO + 1]}
            run_all(P_main(0, hts[0][0], hts[0][1]))
            for i in range(NT_MAIN):
                gens = list(C_stage(i, BS[i % 2]))
                if i + 2 <= NT_MAIN:
                    k = i + 2
                    hts[k] = prep(tsrc(k), nrow, "nrow", k % 2, k % 2, k % 2, defer=True)
                    gens.append(fin_gen(hts[k][4]))
                if i + 1 < NT_MAIN:
                    gens.append(P_main(i + 1, hts[i + 1][0], hts[i + 1][1]))
                else:
                    gens.append(P_stage(hts[NT_MAIN][0], hts[NT_MAIN][1], BS[0], NT_MAIN + NT_HALO, 2,
                                        ([ko_s[j] for j in range(4)], [vo_s[j] for j in range(4)]),
                                        [slice(j * 32, j * 32 + 16) for j in range(4)], tri_s, "tri_s", one_s, "one_s", True,
                                        kvdst=(KS[:, :, :], "KSb", VS[:, :, :], "VSb")))
                run_all(*gens)
            t.dma("sp", go_p.rearrange("(p two) d v -> (two d) p v", two=2), S32[:], r=["S32"], sem="o_g")

            gpool["banks"] = (6, 5)
            bs = BS[0]
            hT, hn = hts[NT_MAIN][0], hts[NT_MAIN][1]
            bt = NT_MAIN + NT_HALO
            allwx = ["WX_" + k_ for k_ in segs]
            allkv = ["KT%d" % i_ for i_ in range(6)] + ["V%d" % i_ for i_ in range(6)]
            KTf = KT[:, :, :].rearrange("p a b -> p (a b)")
            Vf = V[:, :, :, :].rearrange("p a h d -> p (a h d)")
            CS = [dict(CKT=CKT, CKTn=["CKT", "WM"], CV=CV, CVn=["CV"] + rbn, CMKT=CMKT, CMKTn=["CMKT"] + rbn, CMV=CMV, CMVn=["CMV"] + rbn),
                  dict(CKT=KTf[:, 0:2048].rearrange("p (a f) -> p a f", a=4), CKTn=["CKT1"] + allkv,
                       CV=Vf[:, 0:2080].rearrange("p (a h d) -> p a h d", a=4, h=8), CVn=["CV1"] + allkv,
                       CMKT=KTf[:, 2048:2560].rearrange("p (a f) -> p a f", a=2), CMKTn=["CMKT1"] + allkv,
                       CMV=Vf[:, 2080:2600].rearrange("p (a h d) -> p a h d", a=2, h=4), CMVn=["CMV1"] + allkv)]
            stgc = {"n": 0}
            stg = [(o32, "o32"), (nr1, "nr1"), (xs16[:, :].bitcast(F32), "xs16"), (hTs[1][:, :].bitcast(F32), "hT1"), (xt[1][:, 0:512], "xt1")]

            def ld_cast(dst, dnames, src, ncol):
                sg_, sgn = stg[stgc["n"] % len(stg)]
                eng = ("act", "dve")[stgc["n"] % 2]
                stgc["n"] += 1
                t.dma("sp", sg_[:, 0:ncol], src, w=[sgn], sem="stg" + sgn)
                src_v = sg_[:, 0:ncol] if len(dst.shape) == 2 else sg_[:, 0:ncol].rearrange("p (h d) -> p h d", d=64)
                if eng == "act":
                    t.op("act", lambda e: e.copy(out=dst, in_=src_v), r=[sgn], w=dnames)
                else:
                    t.op("dve", lambda e: e.tensor_copy(out=dst, in_=src_v), r=[sgn], w=dnames)

            def loads(j):
                c_ = CS[j % 2]
                for kb in range(4):
                    ld_cast(ck16[:, kb, :], ["ck16", "WM"], cak[j, kb * 128:(kb + 1) * 128, :], 512)
                for kb in range(2):
                    ld_cast(cm16[:, kb, :], ["cm16"] + rbn, cmk[j, kb * 128:(kb + 1) * 128, :], 256)
                yield
                for kb in range(4):
                    ld_cast(c_["CV"][:, kb, :, 0:64], c_["CVn"], cav[j, kb * 128:(kb + 1) * 128, :], 512)
                    t.op("dve", lambda e: e.tensor_copy(out=c_["CV"][:, kb, :, 64:65], in_=ones8[:, :].rearrange("p (h o) -> p h o", o=1)),
                         r=["ones8"], w=c_["CVn"])
                yield
                for kb in range(2):
                    ld_cast(c_["CMV"][:, kb, :, 0:64], c_["CMVn"], cmv[j, kb * 128:(kb + 1) * 128, :], 256)
                    t.op("dve", lambda e: e.tensor_copy(out=c_["CMV"][:, kb, :, 64:65], in_=ones8[:, 0:4].rearrange("p (h o) -> p h o", o=1)),
                         r=["ones8"], w=c_["CMVn"])
                yield
                for pr in range(4):
                    p, pn = nextb()
                    for kb in range(4):
                        t.op("pe", lambda e: e.transpose(out=p[:, kb * 128:(kb + 1) * 128], in_=ck16[:, kb, pr * 128:(pr + 1) * 128],
                                                         identity=id16[:]), r=["ck16", "id16"], w=[pn])
                    t.op("act", lambda e: e.copy(out=c_["CKT"][:, pr, :], in_=p[:, 0:512]), r=[pn], w=c_["CKTn"])
                    yield
                p, pn = nextb()
                for pr in range(2):
                    for kb in range(2):
                        t.op("pe", lambda e: e.transpose(out=p[:, (pr * 2 + kb) * 128:(pr * 2 + kb + 1) * 128],
                                                         in_=cm16[:, kb, pr * 128:(pr + 1) * 128], identity=id16[:]),
                             r=["cm16", "id16"], w=[pn])
                t.op("act", lambda e: e.copy(out=c_["CMKT"][:, :, :], in_=p[:, 0:512].rearrange("p (c t) -> p c t", c=2)),
                     r=[pn], w=c_["CMKTn"])
                yield

            def att_seq(j):
                c_ = CS[j % 2]
                qc = slice(j * 32, j * 32 + 32)

                def blocks_sa(h):
                    pr = h // 2
                    out = []
                    for b in range(4):
                        out.append((c_["CKT"][:, pr, b * 128:(b + 1) * 128], c_["CKTn"][0], 128,
                                    Ep[:, h, b * 128:(b + 1) * 128], "Ep", id16[:, 0:32], c_["CV"][:, b, h, :], c_["CVn"][0], 0))
                    out.append((KS[:, pr, j * 32:j * 32 + 32], "KSb", 32,
                                Ep[:, h, 512:544], "Ep", id16[:, 0:32], VS[j * 32:j * 32 + 32, h, :], "VSb", j * 32))
                    return out

                def blocks_sm(h):
                    pr = h // 2
                    return [(c_["CMKT"][:, pr, b * 128:(b + 1) * 128], c_["CMKTn"][0], 128, None, None, None,
                             c_["CMV"][:, b, h, :], c_["CMVn"][0], 0) for b in range(2)]
                def par2(*gs):
                    gs = list(gs)
                    while gs:
                        for g in list(gs):
                            try:
                                next(g)
                            except StopIteration:
                                gs.remove(g)
                        yield
                return par2(
                    attend(8, (bs["qnA"], bs["qnB"]), bn(bs, "qn16"), qc, blocks_sa, qc, PTa, "PTa", rda, "rda",
                           bs["zas"], bn(bs, "zas"), ga16, "ga16"),
                    attend(4, (bs["qmA"], bs["qmB"]), bn(bs, "qmAB"), qc, blocks_sm, qc, PTm, "PTm", rdm, "rdm",
                           bs["zms"], bn(bs, "zms"), gm16, "gm16"))
            run_all(loads(0))
            gsamp = gla_core(4, tri_s, "tri_s", S32s, "S32s", S16s, "S16s", True, 16, bs)
            for j in range(4):
                if j > 0:
                    g = 2 * (j - 1)
                    wload(WG, w_in, 8, NX + g * 512, NX + (g + 2) * 512, "WG%d" % g, d0=g * 512,
                          extra_w=allwx + ["WG%d" % (g + 1)])
                run_all(att_seq(j), loads(j + 1) if j + 1 < 4 else None, gsamp if j == 0 else None)
            wload(WO, w_o, 8, 0, 1024, "WO", extra_w=allwx)
            usrc = [(w_ua, k, "WUa") for k in range(4)] + [(w_ub, k, "WUb") for k in range(4)] + [(w_um, k, "WUm") for k in range(2)]
            wu_w = ["Ep"] + allkv + ["CKT1", "CV1", "CMKT1", "CMV1"]
            for c, (src_, k, nm) in enumerate(usrc):
                for hf in range(2):
                    sg_, sgn = stg[stgc["n"] % len(stg)]
                    eng = ("act", "dve")[stgc["n"] % 2]
                    stgc["n"] += 1
                    t.dma("sp", sg_[:, 0:512], src_[k * 128:(k + 1) * 128, hf * 512:(hf + 1) * 512], w=[sgn], sem="stg" + sgn)
                    dst = WUc[c][:, hf * 512:(hf + 1) * 512]
                    if nm == "WUb":
                        t.op("act", lambda e: e.activation(out=dst, in_=sg_[:, 0:512], func=AF.Copy, scale=ggo[:]),
                             r=[sgn, "ggo"], w=[nm] + wu_w)
                    elif eng == "act":
                        t.op("act", lambda e: e.copy(out=dst, in_=sg_[:, 0:512]), r=[sgn], w=[nm] + wu_w)
                    else:
                        t.op("dve", lambda e: e.tensor_copy(out=dst, in_=sg_[:, 0:512]), r=[sgn], w=[nm] + wu_w)
            run_all(finish_am(16))
            t.dma("sp", go_s.rearrange("j (p two) d v -> (two d) (j p) v", two=2), S32s[:], r=["S32s"], sem="o_gs")
            t.barrier()

        with ExitStack() as ey:
            wub32 = [sb("wub32_%d" % i, [128, 1024], F32, ey) for i in range(2)]
            nrow = sb("nrowY", [128, 1024], F32, ey)
            xtY = xt + [sb("xt2", [128, 1024], F32, ey)]
            sg16 = [sb("sg16_%d" % i, [128, 3072], BF16, ey) for i in range(2)]
            acc = [sb("acc%d" % i, [128, 512], F32, ey) for i in range(2)]
            tmp = [sb("tmp%d" % i, [128, 512], F32, ey) for i in range(2)]
            u16 = sb("u16", [128, 1024], BF16, ey)
            uT = [sb("uT%d" % i, [128, 1024], BF16, ey) for i in range(2)]
            y32 = [sb("y32_%d" % i, [128, 1024], F32, ey) for i in range(2)]
            t.dma("sp", nrow[:], bass.AP(norm_in.tensor, 0, [[0, 128], [1, 1024]]), w=["nrowY"], sem="c15")
            wun = ["WUa", "WUb", "WUm"]

            NTY = NT_MAIN + 1
            kch = [(0, 4), (4, 8), (8, 10)]

            def ysrc(i):
                return xm[i * 128:(i + 1) * 128, :] if i < NT_MAIN else xsd[:, :]

            def gates(i, hT, hn):
                sg = sg16[i % 2]
                for g in range(6):
                    def ev_g(p, pn, g=g):
                        t.op("act", lambda e: e.activation(out=sg[:, g * 512:(g + 1) * 512], in_=p[:, 0:512], func=AF.Sigmoid),
                             r=[pn], w=["sg%d_%d" % (i % 2, g)])
                    proj_tok(hT, hn, WG, "WG%d" % g, g * 512, 512, ev_g)

            def up_merge(i):
                sg = sg16[i % 2]
                for half in range(2):
                    f0 = half * 512
                    a_, an = acc[half], "acc%d" % half
                    m_, mn = tmp[half], "tmp%d" % half
                    pus = []
                    for b in range(3):
                        p, pn = nextf()
                        k0, k1 = kch[b]
                        for c in range(k0, k1):
                            t.op("pe", lambda e: e.matmul(p[:, 0:512], lhsT=GT[:, i, c, :], rhs=WUc[c][:, f0:f0 + 512],
                                                          start=(c == k0), stop=(c == k1 - 1)), r=["GT", wun[b]], w=[pn])
                        pus.append((p, pn))
                    t.op("dve", lambda e: e.tensor_tensor(out=a_[:], in0=pus[0][0][:, 0:512], in1=sg[:, f0:f0 + 512], op=ALU.mult),
                         r=[pus[0][1], "sg%d_%d" % (i % 2, half)], w=[an])
                    t.op("dve", lambda e: e.tensor_tensor(out=m_[:], in0=pus[1][0][:, 0:512], in1=sg[:, 1024 + f0:1024 + f0 + 512], op=ALU.mult),
                         r=[pus[1][1], "sg%d_%d" % (i % 2, 2 + half)], w=[mn])
                    t.op("pool", lambda e: e.tensor_tensor(out=a_[:], in0=a_[:], in1=m_[:], op=ALU.add), r=[an, mn], w=[an])
                    t.op("dve", lambda e: e.tensor_tensor(out=m_[:], in0=pus[2][0][:, 0:512], in1=sg[:, 2048 + f0:2048 + f0 + 512], op=ALU.mult),
                         r=[pus[2][1], "sg%d_%d" % (i % 2, 4 + half)], w=[mn])
                    t.op("pool", lambda e: e.tensor_tensor(out=u16[:, f0:f0 + 512], in0=a_[:], in1=m_[:], op=ALU.add),
                         r=[an, mn], w=["u16"])

            def u_transpose(i):
                p, pn = nextb()
                for k in range(8):
                    t.op("pe", lambda e: e.transpose(out=p[:, k * 128:(k + 1) * 128], in_=u16[:, k * 128:(k + 1) * 128], identity=id16[:]),
                         r=["u16", "id16"], w=[pn])
                t.op("act", lambda e: e.copy(out=uT[i % 2][:], in_=p[:]), r=[pn], w=["uT%d" % (i % 2)])

            def out_proj(i, x_, xn):
                y_, yn = y32[i % 2], "y32_%d" % (i % 2)
                u_, un = uT[i % 2], "uT%d" % (i % 2)
                for half in range(2):
                    f0 = half * 512
                    p, pn = nextf()
                    for k in range(8):
                        t.op("pe", lambda e: e.matmul(p[:, 0:512], lhsT=u_[:, k * 128:(k + 1) * 128], rhs=WO[:, k, f0:f0 + 512],
                                                      start=(k == 0), stop=(k == 7)), r=[un, "WO"], w=[pn])
                    t.op("dve", lambda e: e.tensor_tensor(out=y_[:, f0:f0 + 512], in0=p[:, 0:512], in1=x_[:, f0:f0 + 512], op=ALU.add),
                         r=[pn, xn], w=[yn])
                if i < NT_MAIN:
                    t.dma("sp", y_p[i * 128:(i + 1) * 128, :], y_[:], r=[yn], sem="o_y%d" % (i % 2))
                else:
                    for j in range(4):
                        t.dma("sp", y_s[j], y_[j * 32:j * 32 + 16, :], r=[yn], sem="o_y%d" % (i % 2))

            preps = {0: prep(ysrc(0), nrow, "nrowY", 0, 0, 0, xtY, use_pool=True)}
            for i in range(NTY):
                hT, hn, x_, xn = preps[i][0:4]
                fin = None
                if i + 1 < NTY:
                    preps[i + 1] = prep(ysrc(i + 1), nrow, "nrowY", (i + 1) % 3, (i + 1) % 2, (i + 1) % 3, xtY,
                                        use_pool=True, defer=True)
                    fin = preps[i + 1][4]
                gates(i, hT, hn)
                if fin is not None:
                    fin()
                up_merge(i)
                if i >= 1:
                    out_proj(i - 1, preps[i - 1][2], preps[i - 1][3])
                u_transpose(i)
            out_proj(NTY - 1, preps[NTY - 1][2], preps[NTY - 1][3])
            t.barrier(("sp",))
    except _Stop:
        pass
    return nc


_NC_CACHE = {}


def _host_consts():
    ident = np.eye(128, dtype=np.float32)
    s = np.arange(128)
    trip = (s[:, None] <= s[None, :]).astype(np.float32)
    onep = np.ones((128, 128), np.float32)
    same = (s[:, None] // 32) == (s[None, :] // 32)
    tris = (same & (s[:, None] <= s[None, :])).astype(np.float32)
    ones_s = (same & ((s[:, None] % 32) < 16)).astype(np.float32)
    blk64 = ((s[:, None] // 64) == (s[None, :] // 64)).astype(np.float32)
    return ident, trip, onep, tris, ones_s, blk64


def _bias_tables(rel_bias):
    tab = np.asarray(rel_bias)[0]
    j = np.arange(128)[:, None]
    i = np.arange(640)[None, :]
    idx = np.clip(512 + j - i, -128, 128) + 128
    ep = tab[:, idx]
    invalid = ((i < 64) & (j >= 64)) | ((i >= 576) & (j < 64))
    ep = np.where(invalid[None], np.float32(NEG), ep)
    ep = np.ascontiguousarray(ep.transpose(1, 0, 2)).astype(np.float32)
    j2 = np.arange(32)[:, None]
    i2 = np.arange(544)[None, :]
    idx2 = np.clip(512 + j2 - i2, -128, 128) + 128
    es_ = np.ascontiguousarray(tab[:, idx2].transpose(1, 0, 2)).astype(np.float32)
    return ep, es_


def _make_in_maps(x_prompt, x_sample, mem_prompt, cache_a_k, cache_a_v, state_gla, cache_mem_k, cache_mem_v,
                  norm_in, w_in, g_qa, g_ka, rel_bias, w_gate2, b_gate, g_gla_out, g_mem, w_mem_kv, g_qm, g_km,
                  w_up_a, w_up_b, w_up_m, w_out):
    f = lambda a: np.ascontiguousarray(np.asarray(a, dtype=np.float32))
    x_prompt, x_sample, mem_prompt = f(x_prompt), f(x_sample), f(mem_prompt)
    ident, trip, onep, tris, ones_s, blk64 = _host_consts()
    ep, es_ = _bias_tables(f(rel_bias))
    shared = {
        "w_in": f(w_in)[0], "w_mem": f(w_mem_kv)[0], "w_ua": f(w_up_a)[0], "w_ub": f(w_up_b)[0],
        "w_um": f(w_up_m)[0], "w_o": f(w_out)[0], "norm_in": f(norm_in), "g_mem": f(g_mem),
        "g_qa": f(g_qa).reshape(64, 1), "g_ka": f(g_ka).reshape(64, 1), "g_qm": f(g_qm).reshape(64, 1),
        "g_km": f(g_km).reshape(64, 1), "w_g2": f(w_gate2)[0], "b_g": f(b_gate), "g_go": f(g_gla_out).reshape(128, 1),
        "ep": ep, "ident": ident, "trip": trip, "onep": onep, "tris": tris, "ones_s": ones_s, "blk64": blk64,
    }
    in_maps = []
    for c in range(NCORES):
        sq, seg = c // 4, c % 4
        t0 = seg * 2048
        xh = x_prompt[sq, t0 - 512:t0] if seg > 0 else np.zeros((512, 1024), np.float32)
        xs = np.zeros((4, 32, 1024), np.float32)
        xs[:, :16] = x_sample[c * 4:(c + 1) * 4]
        cols = np.zeros((128, 4), np.float32)
        cols[:, 0] = 1.0 if seg > 0 else 0.0
        cols[:, 1] = 1.0
        cols[:, 2] = (np.arange(128) % 32 < 16).astype(np.float32)
        m = dict(shared)
        m.update({
            "xh": f(xh), "xm": f(x_prompt[sq, t0:t0 + 2048]), "xs": xs.reshape(128, 1024),
            "mem": f(mem_prompt[sq]),
            "cak": f(cache_a_k[0, c * 4:(c + 1) * 4]).reshape(4, 512, 512),
            "cav": f(cache_a_v[0, c * 4:(c + 1) * 4]).reshape(4, 512, 512),
            "sg": f(state_gla[0, c * 4:(c + 1) * 4]),
            "cmk": f(cache_mem_k[0, c * 4:(c + 1) * 4]).reshape(4, 256, 256),
            "cmv": f(cache_mem_v[0, c * 4:(c + 1) * 4]).reshape(4, 256, 256),
            "cols": cols,
        })
        in_maps.append(m)
    return in_maps


def kernel(**inputs):
    in_maps = _make_in_maps(**inputs)
    if "nc" not in _NC_CACHE:
        _NC_CACHE["nc"] = build_nc()
    nc = _NC_CACHE["nc"]
    res = run_bass_kernel_spmd(nc, in_maps, core_ids=list(range(NCORES)))
    return _assemble(res.results)


def _assemble(R):
    y_prompt = np.stack([np.concatenate([R[s * 4 + g]["y_p"] for g in range(4)], 0) for s in range(2)])
    y_sample = np.concatenate([R[c]["y_s"] for c in range(NCORES)], 0)
    akp = np.stack([R[s * 4 + 3]["ko_p"].reshape(512, 8, 64) for s in range(2)])[None]
    avp = np.stack([R[s * 4 + 3]["vo_p"].reshape(512, 8, 64) for s in range(2)])[None]
    sgp = np.stack([R[s * 4 + 3]["go_p"] for s in range(2)])[None]
    mkp = np.stack([R[s * 4]["mko"].reshape(256, 4, 64) for s in range(2)])[None]
    mvp = np.stack([R[s * 4]["mvo"].reshape(256, 4, 64) for s in range(2)])[None]
    aks = np.concatenate([R[c]["ko_s"].reshape(4, 16, 8, 64) for c in range(NCORES)], 0)[None]
    avs = np.concatenate([R[c]["vo_s"].reshape(4, 16, 8, 64) for c in range(NCORES)], 0)[None]
    sgs = np.concatenate([R[c]["go_s"] for c in range(NCORES)], 0)[None]
    out = (y_prompt, y_sample, akp, avp, sgp, mkp, mvp, aks, avs, sgs)
    return tuple(np.ascontiguousarray(o, dtype=np.float32) for o in out)
```

```python
import numpy as np
from contextlib import ExitStack
import concourse.bass as bass
import concourse.mybir as mybir
from concourse.bass_utils import run_bass_kernel_spmd

F32 = mybir.dt.float32
BF16 = mybir.dt.bfloat16
AF = mybir.ActivationFunctionType
ALU = mybir.AluOpType

EPS = 1e-6
NCORES = 8
NT_MAIN = 16
NT_HALO = 4
C_QA, C_KA, C_VA, C_ZA, C_QB, C_KB, C_VB, C_GLR, C_ZB, C_QM, C_ZM, C_GA, C_GB, C_GM = (
    0, 512, 1024, 1536, 2048, 2304, 2560, 3072, 3088, 3600, 3856, 4112, 5136, 6160)
D_IN = 7184
NX = 4112
NEG = -30000.0


class Tr:
    def __init__(self, nc, es):
        self.nc = nc
        self.es = es
        self.engs = {"pe": nc.tensor, "act": nc.scalar, "dve": nc.vector,
                     "pool": nc.gpsimd, "sp": nc.sync}
        self.sems = {}
        self.cnt = {}
        for e in self.engs:
            self.sems[e] = es.enter_context(nc.semaphore("s_" + e))
            self.cnt[e] = 0
        self.waited = {e: {} for e in self.engs}
        self.lastw = {}
        self.readers = {}

    def _deps(self, r, w):
        deps = {}

        def add(tag):
            if tag is not None and deps.get(tag[0], 0) < tag[1]:
                deps[tag[0]] = tag[1]
        for b in r:
            add(self.lastw.get(b))
        for b in w:
            add(self.lastw.get(b))
            for t in self.readers.get(b, ()):
                add(t)
        return deps

    def _wait(self, eng, deps):
        E = self.engs[eng]
        for s, v in deps.items():
            if eng == "pe" and s == "pe":
                continue
            if self.waited[eng].get(s, 0) < v:
                E.wait_ge(self.sems[s], v)
                self.waited[eng][s] = v

    def _commit(self, tag, r, w):
        for b in w:
            self.lastw[b] = tag
            self.readers[b] = []
        for b in r:
            self.readers.setdefault(b, []).append(tag)

    def op(self, eng, fn, r=(), w=()):
        ps = [b for b in r if b[:2] in ("pf", "pb")]
        if ps:
            w = list(w) + ps
        self._wait(eng, self._deps(r, w))
        ins = fn(self.engs[eng])
        self.cnt[eng] += 1
        ins.then_inc(self.sems[eng], 1)
        self._commit((eng, self.cnt[eng]), r, w)
        return ins

    def dma(self, q, out, in_, r=(), w=(), sem=None, **kw):
        if sem not in self.sems:
            self.sems[sem] = self.es.enter_context(self.nc.semaphore("d_" + sem))
            self.cnt[sem] = 0
        deps = self._deps(r, w)
        if self.cnt[sem] > 0:
            deps[sem] = max(deps.get(sem, 0), self.cnt[sem])
        self._wait(q, deps)
        ins = self.engs[q].dma_start(out=out, in_=in_, **kw)
        self.cnt[sem] += 16
        ins.then_inc(self.sems[sem], 16)
        self._commit((sem, self.cnt[sem]), r, w)
        return ins

    def barrier(self, engs=("pe", "act", "dve", "pool", "sp")):
        for e in engs:
            for s, v in self.cnt.items():
                if v > 0 and self.waited[e].get(s, 0) < v and not (e == "pe" and s == "pe"):
                    self.engs[e].wait_ge(self.sems[s], v)
                    self.waited[e][s] = v


class _Stop(Exception):
    pass


def build_nc(kstop=None):
    nc = bass.Bass("TRN2", target_bir_lowering=False)

    def din(name, shape):
        return nc.dram_tensor(name, list(shape), F32, kind="ExternalInput").ap()

    def dout(name, shape):
        return nc.dram_tensor(name, list(shape), F32, kind="ExternalOutput").ap()

    xh = din("xh", [512, 1024]); xm = din("xm", [2048, 1024]); xsd = din("xs", [128, 1024])
    memd = din("mem", [256, 1024])
    cak = din("cak", [4, 512, 512]); cav = din("cav", [4, 512, 512])
    sgd = din("sg", [4, 4, 64, 128])
    cmk = din("cmk", [4, 256, 256]); cmv = din("cmv", [4, 256, 256])
    w_in = din("w_in", [1024, D_IN]); w_mem = din("w_mem", [1024, 512])
    w_ua = din("w_ua", [512, 1024]); w_ub = din("w_ub", [512, 1024]); w_um = din("w_um", [256, 1024])
    w_o = din("w_o", [1024, 1024])
    norm_in = din("norm_in", [1, 1024]); g_mem = din("g_mem", [1, 1024])
    g_qa = din("g_qa", [64, 1]); g_ka = din("g_ka", [64, 1]); g_qm = din("g_qm", [64, 1]); g_km = din("g_km", [64, 1])
    w_g2 = din("w_g2", [16, 256]); b_g = din("b_g", [1, 256]); g_go = din("g_go", [128, 1])
    epd = din("ep", [128, 8, 640])
    identd = din("ident", [128, 128]); trip = din("trip", [128, 128]); onep = din("onep", [128, 128])
    tris = din("tris", [128, 128]); ones_s = din("ones_s", [128, 128]); blk64d = din("blk64", [128, 128])
    colsd = din("cols", [128, 4])

    y_p = dout("y_p", [2048, 1024]); y_s = dout("y_s", [4, 16, 1024])
    ko_p = dout("ko_p", [512, 512]); vo_p = dout("vo_p", [512, 512])
    go_p = dout("go_p", [4, 64, 128])
    mko = dout("mko", [256, 256]); mvo = dout("mvo", [256, 256])
    ko_s = dout("ko_s", [4, 16, 512]); vo_s = dout("vo_s", [4, 16, 512])
    go_s = dout("go_s", [4, 4, 64, 128])

    try:
      with ExitStack() as es:
        t = Tr(nc, es)

        def stage(k):
            if kstop is not None and k == kstop:
                t.barrier(("sp",))
                raise _Stop()

        def sb(name, shape, dt, ctx=es):
            return ctx.enter_context(nc.sbuf_tensor("sb_" + name, list(shape), dt))

        pf = [es.enter_context(nc.psum_tensor("pf%d" % i, [128, 512], F32)) for i in range(7)]
        pb = [es.enter_context(nc.psum_tensor("pb%d" % i, [128, 1024], BF16)) for i in range(1)]
        rr = {"f": 0, "g": 0, "s": 0, "p": 0}

        def nextf():
            i = rr["f"]; rr["f"] = (i + 1) % 4
            return pf[i], "pf%d" % i

        def nextp():
            i = (0, 1, 2, 3, 6)[rr["p"]]; rr["p"] = (rr["p"] + 1) % 5
            return pf[i], "pf%d" % i

        def nexts():
            return nextp()

        rr["a"] = 0

        def nexta():
            i = (4, 5)[rr["a"]]; rr["a"] = (rr["a"] + 1) % 2
            return pf[i], "pf%d" % i

        gpool = {"banks": (6,)}

        def nextg():
            return nextp()

        def nextb():
            return pb[0], "pb0"

        id16 = sb("id16", [128, 128], BF16)
        blk64 = sb("blk64", [128, 128], BF16)
        tri_p = sb("tri_p", [128, 128], F32); one_p = sb("one_p", [128, 128], F32)
        tri_s = sb("tri_s", [128, 128], F32); one_s = sb("one_s", [128, 128], F32)
        cols = sb("cols", [128, 4], F32)
        ones8 = sb("ones8", [128, 8], F32)
        gq8 = sb("gq8", [128, 1], F32); gk = sb("gk", [128, 1], F32)
        gqm8 = sb("gqm8", [128, 1], F32); gkm = sb("gkm", [128, 1], F32)
        ggo = sb("ggo", [128, 1], F32)
        wg2 = sb("wg2", [32, 256], F32)
        glrT = sb("glrT", [32, 128], F32)
        GT = sb("GT", [128, 17, 10, 128], BF16)
        hwstg_all = GT[:, 10:17, :, :].rearrange("p t c k -> p (t c k)")[:, 0:8192].bitcast(F32)
        hwstg = [hwstg_all[:, i * 1024:(i + 1) * 1024] for i in range(4)]
        hwn = ["hwstg%d" % i for i in range(4)]
        hwc = {"n": 0}

        def gtw(tile_):
            return ["GT"] + (hwn if tile_ >= 10 else [])

        def hw_wload(dst3, src, ks, c0, c1, bname, d0=None):
            d0 = c0 if d0 is None else d0
            for k in ks:
                for a in range(c0, c1, 1024):
                    b_ = min(a + 1024, c1)
                    i_ = hwc["n"] % 4
                    hwc["n"] += 1
                    t.dma("sp", hwstg[i_][:, 0:b_ - a], src[k * 128:(k + 1) * 128, a:b_], w=[hwn[i_]], sem="hw%d" % i_)
                    dst = dst3[:, k, d0 + a - c0:d0 + b_ - c0]
                    if i_ % 2 == 0:
                        t.op("act", lambda e: e.copy(out=dst, in_=hwstg[i_][:, 0:b_ - a]), r=[hwn[i_]], w=[bname])
                    else:
                        t.op("dve", lambda e: e.tensor_copy(out=dst, in_=hwstg[i_][:, 0:b_ - a]), r=[hwn[i_]], w=[bname])
        S32 = sb("S32", [128, 2, 128], F32); S16 = sb("S16", [128, 2, 128], BF16)
        S32s = sb("S32s", [128, 8, 128], F32); S16s = sb("S16s", [128, 8, 128], BF16)
        xt = [sb("xt%d" % i, [128, 1024], F32) for i in range(2)]
        xs16 = sb("xs16", [128, 1024], BF16)
        hTs = [sb("hT%d" % i, [128, 1024], BF16) for i in range(2)]
        st4 = [sb("st4_%d" % i, [128, 4], F32) for i in range(3)]
        WXG = sb("WXG", [128, 8 * NX], BF16)
        WX = WXG[:, :].rearrange("p (k n) -> p k n", k=8)
        WG = WXG[:, 0:8 * 3072].rearrange("p (k n) -> p k n", k=8)
        WO = WXG[:, 8 * 3072:8 * 4096].rearrange("p (k n) -> p k n", k=8)

        t.dma("pool", id16[:], identd, w=["id16"], sem="c0")
        t.dma("pool", blk64[:], blk64d, w=["blk64"], sem="c1")
        t.dma("sp", tri_p[:], trip, w=["tri_p"], sem="c2")
        t.dma("sp", one_p[:], onep, w=["one_p"], sem="c3")
        t.dma("sp", tri_s[:], tris, w=["tri_s"], sem="c4")
        t.dma("sp", one_s[:], ones_s, w=["one_s"], sem="c5")
        t.dma("sp", cols[:], colsd, w=["cols"], sem="c6")
        for half in range(2):
            sl = slice(half * 64, half * 64 + 64)
            t.dma("sp", gq8[sl, :], g_qa, w=["gq8"], sem="c9")
            t.dma("sp", gk[sl, :], g_ka, w=["gk"], sem="c10")
            t.dma("sp", gqm8[sl, :], g_qm, w=["gqm8"], sem="c11")
            t.dma("sp", gkm[sl, :], g_km, w=["gkm"], sem="c12")
        t.dma("sp", ggo[:], g_go, w=["ggo"], sem="c13")
        t.dma("sp", wg2[0:16, :], w_g2, w=["wg2"], sem="c14")
        t.dma("sp", wg2[16:17, :], b_g, w=["wg2"], sem="c14")
        t.op("dve", lambda e: e.tensor_scalar_mul(out=gq8[:], in0=gq8[:], scalar1=0.125), r=["gq8"], w=["gq8"])
        t.op("dve", lambda e: e.tensor_scalar_mul(out=gqm8[:], in0=gqm8[:], scalar1=0.125), r=["gqm8"], w=["gqm8"])
        t.op("dve", lambda e: e.memset(ones8[:], 1.0), w=["ones8"])
        t.op("dve", lambda e: e.memset(glrT[:], 1.0), w=["glrT"])
        t.op("dve", lambda e: e.memset(S32[:], 0.0), w=["S32"])
        t.op("dve", lambda e: e.memset(S16[:], 0.0), w=["S16"])

        wsem = {"n": 0}

        def wload(dst3, src, nk, c0, c1, bname, d0=None, extra_w=(), ks=None):
            d0 = c0 if d0 is None else d0
            for k in (range(nk) if ks is None else ks):
                for a in range(c0, c1, 2048):
                    b_ = min(a + 2048, c1)
                    t.dma("pool", dst3[:, k, d0 + a - c0:d0 + b_ - c0], src[k * 128:(k + 1) * 128, a:b_],
                          w=[bname] + list(extra_w), sem="w%d" % (wsem["n"] % 16))
                    wsem["n"] += 1

        mhalf = sb("mhalf", [128, 1], F32)
        t.op("dve", lambda e: e.memset(mhalf[:], -0.5), w=["mhalf"])

        def prep(src_rows, nrow, nrow_name, xs_, hs_, ss_, xpool=None, use_pool=False, defer=False, q="sp"):
            xpool = xt if xpool is None else xpool
            x_, xn = xpool[xs_], "xt%d" % xs_
            hT, hn = hTs[hs_], "hT%d" % hs_
            sv, svn = st4[ss_], "st4_%d" % ss_
            t.dma(q, x_[:], src_rows, w=[xn], sem="x%d" % xs_)
            t.op("act", lambda e: e.activation(out=hT[:], in_=x_[:], func=AF.Square, accum_out=sv[:, 0:1]),
                 r=[xn], w=[svn, hn])
            if use_pool:
                t.op("pool", lambda e: e.tensor_scalar(out=sv[:, 1:2], in0=sv[:, 0:1], scalar1=1.0 / 1024, scalar2=EPS,
                                                       op0=ALU.mult, op1=ALU.add), r=[svn], w=[svn])
                t.op("pool", lambda e: e.tensor_tensor(out=sv[:, 2:3], in0=sv[:, 1:2], in1=mhalf[:], op=ALU.pow),
                     r=[svn, "mhalf"], w=[svn])
            else:
                t.op("act", lambda e: e.activation(out=sv[:, 1:2], in_=sv[:, 0:1], func=AF.Ln, scale=1.0 / 1024, bias=EPS),
                     r=[svn], w=[svn])
                t.op("act", lambda e: e.activation(out=sv[:, 2:3], in_=sv[:, 1:2], func=AF.Exp, scale=-0.5), r=[svn], w=[svn])
            t.op("dve", lambda e: e.scalar_tensor_tensor(out=xs16[:], in0=x_[:], scalar=sv[:, 2:3], in1=nrow[:],
                                                         op0=ALU.mult, op1=ALU.mult),
                 r=[xn, svn, nrow_name], w=["xs16"])
            def fin():
                p, pn = nextb()
                for k in range(8):
                    t.op("pe", lambda e: e.transpose(out=p[:, k * 128:(k + 1) * 128], in_=xs16[:, k * 128:(k + 1) * 128],
                                                     identity=id16[:]), r=["xs16", "id16"], w=[pn])
                t.op("dve", lambda e: e.tensor_copy(out=hT[:], in_=p[:]), r=[pn], w=[hn])
            if defer:
                return hT, hn, x_, xn, fin
            fin()
            return hT, hn, x_, xn

        def proj_tok(hT, hn, W, wn, c0, n, evac, pool=None):
            p, pn = (pool or nextf)()
            for k in range(8):
                t.op("pe", lambda e: e.matmul(p[:, 0:n], lhsT=hT[:, k * 128:(k + 1) * 128], rhs=W[:, k, c0:c0 + n],
                                              start=(k == 0), stop=(k == 7)), r=[hn, wn], w=[pn])
            evac(p, pn)

        def proj_feat(hT, hn, W, wn, c0, m, pout, pn):
            for k in range(8):
                t.op("pe", lambda e: e.matmul(pout, lhsT=W[:, k, c0:c0 + m], rhs=hT[:, k * 128:(k + 1) * 128],
                                              start=(k == 0), stop=(k == 7)), r=[hn, wn], w=[pn])

        def qknorm(p, pn, n, gcol, gname, out16, outname, wk, outB=None):
            sq, r1 = wk
            t.op("act", lambda e: e.activation(out=sq[:, 0:n], in_=p[:, 0:n], func=AF.Square), r=[pn], w=["qn16k"])
            p2, p2n = nextp()
            t.op("pe", lambda e: e.matmul(p2[:, 0:n], lhsT=blk64[:], rhs=sq[:, 0:n], start=True, stop=True),
                 r=["qn16k", "blk64"], w=[p2n])
            t.op("act", lambda e: e.activation(out=r1[:, 0:n], in_=p2[:, 0:n], func=AF.Ln, scale=1.0 / 64, bias=EPS),
                 r=[p2n], w=["nr1"])
            t.op("act", lambda e: e.activation(out=r1[:, 0:n], in_=r1[:, 0:n], func=AF.Exp, scale=-0.5), r=["nr1"], w=["nr1"])
            if outB is None:
                t.op("dve", lambda e: e.scalar_tensor_tensor(out=out16, in0=p[:, 0:n], scalar=gcol[:], in1=r1[:, 0:n],
                                                             op0=ALU.mult, op1=ALU.mult),
                     r=[pn, gname, "nr1"], w=[outname])
            else:
                t.op("dve", lambda e: e.scalar_tensor_tensor(out=out16[0:64, 0:n], in0=p[0:64, 0:n], scalar=gcol[0:64, :],
                                                             in1=r1[0:64, 0:n], op0=ALU.mult, op1=ALU.mult),
                     r=[pn, gname, "nr1"], w=[outname])
                t.op("dve", lambda e: e.scalar_tensor_tensor(out=outB[64:128, 0:n], in0=p[64:128, 0:n], scalar=gcol[64:128, :],
                                                             in1=r1[64:128, 0:n], op0=ALU.mult, op1=ALU.mult),
                     r=[pn, gname, "nr1"], w=[outname])

        def transpose_out(src16, sname, nchunk, dst32, dname):
            p, pn = nextb()
            for c in range(nchunk):
                t.op("pe", lambda e: e.transpose(out=p[:, c * 128:(c + 1) * 128], in_=src16[:, c * 128:(c + 1) * 128],
                                                 identity=id16[:]), r=[sname, "id16"], w=[pn])
            t.op("act", lambda e: e.copy(out=dst32[:, 0:nchunk * 128], in_=p[:, 0:nchunk * 128]), r=[pn], w=[dname])

        def attend(nh, qT, qname, qcols, blocks, orow, PTs, ptn, otile, oname, zs16, zname, gout, goname):
            nq = qcols.stop - qcols.start
            po = pon = pov = None
            for h in range(nh):
                hh = h % 4
                g0 = h - hh
                pr = h // 2
                if hh == 0:
                    po, pon = nexta()
                    pov = po[:, 0:260].rearrange("p (h d) -> p h d", h=4)
                bl = blocks(h)
                banks = []
                col = 0
                cur = None
                for bi, (kT, kname, nk, E, ename, I, V, vname, kbase) in enumerate(bl):
                    if cur is None or col + nq > 512:
                        cur = nexts(); col = 0
                        banks.append([cur, []])
                    p, pn = cur
                    o_ap = p[kbase:kbase + nk, col:col + nq]
                    t.op("pe", lambda e: e.matmul(o_ap, lhsT=kT, rhs=qT[h % 2][:, pr * 128 + qcols.start:pr * 128 + qcols.stop],
                                                  start=True, stop=(E is None), tile_position=(0, kbase)),
                         r=[kname, qname], w=[pn])
                    if E is not None:
                        t.op("pe", lambda e: e.matmul(o_ap, lhsT=E, rhs=I, start=False, stop=True,
                                                      tile_position=(0, kbase)),
                             r=[ename, "id16"], w=[pn])
                    banks[-1][1].append((bi, col, nk, kbase))
                    col += nq
                PT, pt_name = PTs[h % 2], "%s%d" % (ptn, h % 2)
                for (p, pn), lst in banks:
                    i = 0
                    while i < len(lst):
                        j = i
                        while j + 1 < len(lst) and lst[j + 1][2] == lst[i][2] and lst[j + 1][3] == lst[i][3]:
                            j += 1
                        bi0, c0, nk, kb_ = lst[i]
                        c1 = lst[j][1] + nq
                        t.op("act", lambda e: e.activation(out=PT[kb_:kb_ + nk, bi0 * nq:bi0 * nq + (c1 - c0)],
                                                           in_=p[kb_:kb_ + nk, c0:c1], func=AF.Exp),
                             r=[pn], w=[pt_name])
                        i = j + 1
                yield
                for bi, (kT, kname, nk, E, ename, I, V, vname, kbase) in enumerate(bl):
                    t.op("pe", lambda e: e.matmul(pov[orow, hh, :], lhsT=PT[kbase:kbase + nk, bi * nq:(bi + 1) * nq], rhs=V,
                                                  start=(bi == 0), stop=(bi == len(bl) - 1),
                                                  tile_position=(kbase, orow.start)),
                         r=[pt_name, vname], w=[pon])
                if hh == 3:
                    rd = otile
                    t.op("dve", lambda e: e.reciprocal(out=rd[orow, g0:g0 + 4], in_=pov[orow, :, 64]), r=[pon], w=[oname])
                    for h2 in range(4):
                        hx = g0 + h2
                        t.op("dve", lambda e: e.scalar_tensor_tensor(out=gout[orow, hx * 64:(hx + 1) * 64], in0=pov[orow, h2, 0:64],
                                                                     scalar=rd[orow, hx:hx + 1], in1=zs16[orow, hx * 64:(hx + 1) * 64],
                                                                     op0=ALU.mult, op1=ALU.mult),
                             r=[pon, oname, zname], w=[goname])

        def run_all(*gens):
            gens = [g for g in gens if g is not None]
            while gens:
                for g in list(gens):
                    try:
                        next(g)
                    except StopIteration:
                        gens.remove(g)

        def chain(*gens):
            for g in gens:
                for _ in g:
                    yield

        Ep = sb("Ep", [128, 8, 640], BF16)
        KT = sb("KT", [128, 4, 6 * 128], BF16)
        V = sb("V", [128, 6, 8, 65], BF16)
        Epf_ = Ep[:, :, :].rearrange("p h n -> p (h n)")
        KTf_ = KT[:, :, :].rearrange("p a b -> p (a b)")
        Vf_ = V[:, :, :, :].rearrange("p a h d -> p (a h d)")
        WUc = [Epf_[:, c_ * 1024:(c_ + 1) * 1024] for c_ in range(5)] + \
              [KTf_[:, c_ * 1024:(c_ + 1) * 1024] for c_ in range(3)] + [Vf_[:, c_ * 1024:(c_ + 1) * 1024] for c_ in range(2)]
        with ExitStack() as ex:
            WM = sb("WM", [128, 8, 512], BF16, ex)
            nrow = sb("nrow", [128, 1024], F32, ex)
            MKT = sb("MKT", [128, 2, 256], BF16, ex); MV = sb("MV", [128, 2, 4, 65], BF16, ex)
            nr1 = sb("nr1", [128, 512], F32, ex)
            qm16 = sb("qm16", [128, 256], BF16, ex)
            RB = sb("RB", [128, 4864], BF16, ex)
            BS = []
            for si in range(2):
                d_ = {}
                off_ = 0
                for nm_, w_ in (("qnA", 512), ("qnB", 512), ("qmA", 256), ("qmB", 256), ("qpA", 256), ("qpB", 256),
                                ("zas", 512), ("zbs", 512), ("zms", 256), ("vb16", 512), ("kpT", 256), ("kpk", 256)):
                    if si == 0:
                        d_[nm_] = sb("%s_%d" % (nm_, si), [128, w_], BF16, ex)
                    else:
                        d_[nm_] = RB[:, off_:off_ + w_]
                        off_ += w_
                if si == 0:
                    d_["etot"] = sb("etot_%d" % si, [128, 256], F32, ex)
                else:
                    d_["etot"] = RB[:, off_:off_ + 512].bitcast(F32)
                d_["i"] = si
                for nm_ in ("qnA", "qnB", "qmA", "qmB", "qpA", "qpB"):
                    z_ = d_[nm_]
                    t.op("dve", lambda e: e.memset(z_[:], 0.0), w=[])
                BS.append(d_)

            def bn(bs, nm_):
                return "%s_%d" % (nm_, bs["i"])
            qn16k = sb("qn16k", [128, 512], BF16, ex)
            nsq = qn16k
            wk = (nsq, nr1)
            o32 = sb("o32", [128, 512], F32, ex); k32 = nr1
            sp32 = sb("sp32", [128, 256], F32, ex)
            E1T = sb("E1T", [128, 256], F32, ex); E2T = sb("E2T", [128, 256], F32, ex)
            E2k = sb("E2k", [128, 256], F32, ex)
            attT = sb("attT", [128, 512], BF16, ex)
            Tst = sb("Tst", [128, 2, 128], F32, ex)
            ssb = sb("ssb", [128, 12], F32, ex)
            gb16 = sb("gb16", [128, 512], BF16, ex); ga16 = sb("ga16", [128, 512], BF16, ex); gm16 = sb("gm16", [128, 256], BF16, ex)
            PTa = [sb("PTa%d" % i, [128, 640], BF16, ex) for i in range(2)]
            PTm = [sb("PTm%d" % i, [128, 256], BF16, ex) for i in range(2)]
            rda = sb("rda", [128, 8], F32, ex); rdm = sb("rdm", [128, 4], F32, ex)
            KS = sb("KSb", [128, 4, 128], BF16, ex); VS = sb("VSb", [128, 8, 65], BF16, ex)
            ck16 = WM[:, 0:4, :]; CKT = WM[:, 4:8, :]
            CV = RB[:, 0:2080].rearrange("p (a h d) -> p a h d", a=4, h=8)
            cm16 = RB[:, 2080:2592].rearrange("p (a f) -> p a f", a=2)
            CMKT = RB[:, 2592:3104].rearrange("p (a f) -> p a f", a=2)
            CMV = RB[:, 3104:3624].rearrange("p (a h d) -> p a h d", a=2, h=4)
            rbn = ["%s_1" % n_ for n_ in ("qn16", "qmAB", "qpT", "zas", "zbs", "zms", "vb16", "kpT", "kpk", "etot")]

            segs = {"qa": (0, 512), "kava": (512, 1536), "zaqb": (1536, 2304), "kbvbgl": (2304, 3600), "qmzm": (3600, 4112)}

            def wn(c0):
                for k_, (a, b_) in segs.items():
                    if a <= c0 < b_:
                        return "WX_" + k_
                raise KeyError(c0)
            t.dma("sp", nrow[:], bass.AP(g_mem.tensor, 0, [[0, 128], [1, 1024]]), w=["nrow"], sem="c15")
            wload(WM, w_mem, 8, 0, 512, "WM")
            wload(WX, w_in, 8, segs["kava"][0], segs["kava"][1], "WX_kava")
            hw_wload(WX, w_in, range(8), segs["kbvbgl"][0], segs["kbvbgl"][1], "WX_kbvbgl")
            for h in range(8):
                t.dma("pool", Ep[:, h, :], epd[:, h, :], w=["Ep"], sem="ce%d" % (h % 2))
            hw_wload(WX, w_in, range(8), segs["qa"][0], segs["qa"][1], "WX_qa")
            wload(WX, w_in, 8, segs["zaqb"][0], segs["zaqb"][1], "WX_zaqb")
            hw_wload(WX, w_in, range(8), segs["qmzm"][0], segs["qmzm"][1], "WX_qmzm")
            t.dma("sp", S32s[:], sgd.rearrange("j (p two) d v -> (two d) (j p) v", two=2), w=["S32s"], sem="c17")
            t.op("act", lambda e: e.copy(out=S16s[:], in_=S32s[:]), r=["S32s"], w=["S16s"])

            for mt in range(2):
                hT, hn, _, _ = prep(memd[mt * 128:(mt + 1) * 128, :], nrow, "nrow", mt % 2, mt % 2, mt % 2, q="act")
                p, pn = nextp()
                for c in range(2):
                    proj_feat(hT, hn, WM, "WM", c * 128, 128, p[:, c * 128:(c + 1) * 128], pn)
                qknorm(p, pn, 256, gkm, "gkm", qm16[:, 0:256], "qm16", wk)
                for c in range(2):
                    t.op("act", lambda e: e.copy(out=MKT[:, c, mt * 128:(mt + 1) * 128], in_=qm16[:, c * 128:(c + 1) * 128]),
                         r=["qm16"], w=["MKT"])
                transpose_out(qm16, "qm16", 2, k32, "nr1")
                t.dma("sp", mko[mt * 128:(mt + 1) * 128, :], k32[:, 0:256], r=["nr1"], sem="o_mk")

                def ev_mv(p, pn, mt=mt):
                    t.op("act", lambda e: e.copy(out=MV[:, mt, :, 0:64], in_=p[:, 0:256].rearrange("p (h d) -> p h d", h=4)),
                         r=[pn], w=["MV"])
                    t.op("dve", lambda e: e.tensor_copy(out=o32[:, 0:256], in_=p[:, 0:256]), r=[pn], w=["o32"])
                    t.dma("sp", mvo[mt * 128:(mt + 1) * 128, :], o32[:, 0:256], r=["o32"], sem="o_mv")
                proj_tok(hT, hn, WM, "WM", 256, 256, ev_mv, nextp)
                t.op("dve", lambda e: e.tensor_copy(out=MV[:, mt, :, 64:65], in_=ones8[:, 0:4].rearrange("p (h o) -> p h o", o=1)),
                     r=["ones8"], w=["MV"])
            t.dma("sp", nrow[:], bass.AP(norm_in.tensor, 0, [[0, 128], [1, 1024]]), r=[], w=["nrow"], sem="c15")

            def gla_gate(hT, hn):
                pg, pgn = nextp()
                proj_feat(hT, hn, WX, wn(C_GLR), C_GLR, 16, pg[0:16, 0:128], pgn)
                t.op("act", lambda e: e.copy(out=glrT[0:16, :], in_=pg[0:16, 0:128]), r=[pgn], w=["glrT"])
                pg2, pg2n = nextp()
                t.op("pe", lambda e: e.matmul(pg2[:, 0:256], lhsT=glrT[0:17, :], rhs=wg2[0:17, :], start=True, stop=True),
                     r=["glrT", "wg2"], w=[pg2n])
                t.op("act", lambda e: e.activation(out=sp32[:], in_=pg2[:, 0:256], func=AF.Exp, scale=-1.0), r=[pg2n], w=["sp32"])
                t.op("act", lambda e: e.activation(out=sp32[:], in_=sp32[:], func=AF.Ln, bias=1.0), r=["sp32"], w=["sp32"])

            def gla_cum(tri, trin, onem, onen, full, bs):
                etot = bs["etot"]
                pc, pcn = nextp()
                for pr in range(2):
                    if full:
                        t.op("pe", lambda e: e.matmul(pc[:, pr * 128:(pr + 1) * 128], lhsT=sp32[:, pr * 128:(pr + 1) * 128], rhs=tri[:],
                                                      start=True, stop=True), r=["sp32", trin], w=[pcn])
                    t.op("pe", lambda e: e.matmul(pc[:, 256 + pr * 128:256 + (pr + 1) * 128], lhsT=sp32[:, pr * 128:(pr + 1) * 128],
                                                  rhs=onem[:], start=True, stop=True), r=["sp32", onen], w=[pcn])
                pct, pctn = nextp()
                t.op("pe", lambda e: e.matmul(pct[:, 0:256], lhsT=tri[:], rhs=sp32[:], start=True, stop=True),
                     r=["sp32", trin], w=[pctn])
                if full:
                    t.op("act", lambda e: e.activation(out=E1T[:], in_=pc[:, 0:256], func=AF.Exp, scale=-1.0 / 16), r=[pcn], w=["E1T"])
                    t.op("act", lambda e: e.activation(out=E2T[:], in_=pc[:, 0:256], func=AF.Exp, scale=1.0 / 16), r=[pcn], w=["E2T"])
                t.op("act", lambda e: e.activation(out=etot[:], in_=pc[:, 256:512], func=AF.Exp, scale=-1.0 / 16), r=[pcn], w=[bn(bs, "etot")])
                t.op("act", lambda e: e.activation(out=E2k[:], in_=pct[:, 0:256], func=AF.Exp, scale=1.0 / 16), r=[pctn], w=["E2k"])

            def gla_proj(hT, hn, full, bs):
                kpk, vb16, kpT, qpA, qpB = bs["kpk"], bs["vb16"], bs["kpT"], bs["qpA"], bs["qpB"]

                def ev_kb(p, pn):
                    t.op("dve", lambda e: e.tensor_tensor(out=kpk[:], in0=p[:, 0:256], in1=E2k[:], op=ALU.mult),
                         r=[pn, "E2k"], w=[bn(bs, "kpk")])
                proj_tok(hT, hn, WX, wn(C_KB), C_KB, 256, ev_kb, nextp)
                yield

                def ev_vb(p, pn):
                    t.op("dve", lambda e: e.tensor_copy(out=vb16[:], in_=p[:, 0:512]), r=[pn], w=[bn(bs, "vb16")])
                proj_tok(hT, hn, WX, wn(C_VB), C_VB, 512, ev_vb, nextp)
                yield
                if full:
                    pqk, pqkn = nextp()
                    for c in range(2):
                        proj_feat(hT, hn, WX, wn(C_QB), C_QB + c * 128, 128, pqk[:, c * 128:(c + 1) * 128], pqkn)
                        proj_feat(hT, hn, WX, wn(C_KB), C_KB + c * 128, 128, pqk[:, 256 + c * 128:256 + (c + 1) * 128], pqkn)
                    t.op("dve", lambda e: e.scalar_tensor_tensor(out=qpA[0:64, :], in0=pqk[0:64, 0:256], scalar=0.125, in1=E1T[0:64, :],
                                                                 op0=ALU.mult, op1=ALU.mult), r=[pqkn, "E1T"], w=[bn(bs, "qpT")])
                    t.op("dve", lambda e: e.scalar_tensor_tensor(out=qpB[64:128, :], in0=pqk[64:128, 0:256], scalar=0.125, in1=E1T[64:128, :],
                                                                 op0=ALU.mult, op1=ALU.mult), r=[pqkn, "E1T"], w=[bn(bs, "qpT")])
                    t.op("dve", lambda e: e.tensor_tensor(out=kpT[:], in0=pqk[:, 256:512], in1=E2T[:], op=ALU.mult),
                         r=[pqkn, "E2T"], w=[bn(bs, "kpT")])
                    yield

            def gla_core(nseq, tri, trin, Sf, Sfn, Sb, Sbn, full, gt_tile, bs):
                kpk, vb16, kpT, qpA, qpB, zbs, etot = (bs["kpk"], bs["vb16"], bs["kpT"], bs["qpA"], bs["qpB"], bs["zbs"], bs["etot"])
                L = 128 // nseq
                if full:
                    pa, pan = nextg()
                    for h in range(4):
                        pr = h // 2
                        qp = (qpA, qpB)[h % 2]
                        t.op("pe", lambda e: e.matmul(pa[:, h * 128:(h + 1) * 128], lhsT=kpT[:, pr * 128:(pr + 1) * 128],
                                                      rhs=qp[:, pr * 128:(pr + 1) * 128], start=True, stop=True),
                             r=[bn(bs, "kpT"), bn(bs, "qpT")], w=[pan])
                    for h in range(4):
                        t.op("dve", lambda e: e.tensor_tensor(out=attT[:, h * 128:(h + 1) * 128], in0=pa[:, h * 128:(h + 1) * 128],
                                                              in1=tri[:], op=ALU.mult), r=[pan, trin], w=["attT"])
                    yield
                    po, pon = nextg()
                    for h in range(4):
                        pr, base = h // 2, (h % 2) * 64
                        t.op("pe", lambda e: e.matmul(po[:, h * 128:(h + 1) * 128], lhsT=attT[:, h * 128:(h + 1) * 128],
                                                      rhs=vb16[:, h * 128:(h + 1) * 128], start=True, stop=False),
                             r=["attT", bn(bs, "vb16")], w=[pon])
                        qp = (qpA, qpB)[h % 2]
                        for j in range(nseq):
                            t.op("pe", lambda e: e.matmul(po[j * L:(j + 1) * L, h * 128:(h + 1) * 128],
                                                          lhsT=qp[:, pr * 128 + j * L:pr * 128 + (j + 1) * L],
                                                          rhs=Sb[:, j * 2 + pr, :], start=False, stop=True,
                                                          tile_position=(0, j * L)),
                                 r=[bn(bs, "qpT"), Sbn], w=[pon])
                    for h in range(4):
                        t.op("act", lambda e: e.activation(out=gb16[:, h * 128:(h + 1) * 128], in_=po[:, h * 128:(h + 1) * 128], func=AF.Square,
                                                           accum_out=ssb[:, h:h + 1]), r=[pon], w=["ssb", "gb16"])
                    t.op("act", lambda e: e.activation(out=ssb[:, 4:8], in_=ssb[:, 0:4], func=AF.Ln, scale=1.0 / 128, bias=EPS),
                         r=["ssb"], w=["ssb"])
                    t.op("act", lambda e: e.activation(out=ssb[:, 8:12], in_=ssb[:, 4:8], func=AF.Exp, scale=-0.5), r=["ssb"], w=["ssb"])
                    for h in range(4):
                        t.op("dve", lambda e: e.scalar_tensor_tensor(out=gb16[:, h * 128:(h + 1) * 128], in0=po[:, h * 128:(h + 1) * 128],
                                                                     scalar=ssb[:, 8 + h:9 + h], in1=zbs[:, h * 128:(h + 1) * 128],
                                                                     op0=ALU.mult, op1=ALU.mult),
                             r=[pon, "ssb", bn(bs, "zbs")], w=["gb16"])
                    yield
                    p, pn = nextb()
                    for c in range(4):
                        t.op("pe", lambda e: e.transpose(out=p[:, c * 128:(c + 1) * 128], in_=gb16[:, c * 128:(c + 1) * 128],
                                                         identity=id16[:]), r=["gb16", "id16"], w=[pn])
                    t.op("act", lambda e: e.copy(out=GT[:, gt_tile, 4:8, :], in_=p[:, 0:512].rearrange("p (c t) -> p c t", c=4)),
                         r=[pn], w=gtw(gt_tile))
                    yield
                for j in range(nseq):
                    pu, pun = nextg()
                    if nseq == 1:
                        for pr in range(2):
                            t.op("pe", lambda e: e.matmul(pu[:, pr * 256:(pr + 1) * 256], lhsT=kpk[:, pr * 128:(pr + 1) * 128],
                                                          rhs=vb16[:, pr * 256:(pr + 1) * 256], start=True, stop=True),
                                 r=[bn(bs, "kpk"), bn(bs, "vb16")], w=[pun])
                    else:
                        for h in range(4):
                            pr, base = h // 2, (h % 2) * 64
                            sl = pr * 256 + (h % 2) * 128
                            t.op("pe", lambda e: e.matmul(pu[base:base + 64, sl:sl + 128], lhsT=kpk[j * L:(j + 1) * L, h * 64:(h + 1) * 64],
                                                          rhs=vb16[j * L:(j + 1) * L, h * 128:(h + 1) * 128], start=True, stop=True,
                                                          tile_position=(j * L, base)), r=[bn(bs, "kpk"), bn(bs, "vb16")], w=[pun])
                    for pr in range(2):
                        slot = j * 2 + pr
                        ec = etot[:, pr * 128 + j * L:pr * 128 + j * L + 1]
                        t.op("act", lambda e: e.activation(out=Tst[:, pr, :], in_=Sf[:, slot, :], func=AF.Copy, scale=ec),
                             r=[Sfn, bn(bs, "etot")], w=["Tst%d" % pr])
                        for hb in range(2):
                            rows = slice(hb * 64, hb * 64 + 64)
                            sl = pr * 256 + hb * 128
                            t.op("dve", lambda e: e.scalar_tensor_tensor(out=Sf[rows, slot, :], in0=pu[rows, sl:sl + 128], scalar=ec[rows, :],
                                                                         in1=Tst[rows, pr, :], op0=ALU.mult, op1=ALU.add),
                                 r=[pun, bn(bs, "etot"), "Tst%d" % pr], w=[Sfn])
                        t.op("act", lambda e: e.copy(out=Sb[:, slot, :], in_=Sf[:, slot, :]), r=[Sfn], w=[Sbn])
                    yield

            def kv_tile(hT, hn, bt, vcol, want_out, rows_list=None, dst=None):
                slot = bt % 6
                if dst is None:
                    Kd, kn, Vd, vn = KT[:, :, slot * 128:(slot + 1) * 128], "KT%d" % slot, V[:, slot, :, :], "V%d" % slot
                else:
                    Kd, kn, Vd, vn = dst
                p, pn = nextp()
                for c in range(4):
                    proj_feat(hT, hn, WX, wn(C_KA), C_KA + c * 128, 128, p[:, c * 128:(c + 1) * 128], pn)
                qknorm(p, pn, 512, gk, "gk", qn16k[:, :], "qn16k", wk)
                t.op("dve", lambda e: e.tensor_copy(out=Kd, in_=qn16k[:, :].rearrange("p (c t) -> p c t", c=4)), r=["qn16k"], w=[kn])
                yield

                def ev_va(p, pn):
                    t.op("dve", lambda e: e.tensor_copy(out=Vd[:, :, 0:64], in_=p[:, 0:512].rearrange("p (h d) -> p h d", h=8)),
                         r=[pn], w=[vn])
                    if want_out is not None:
                        t.op("dve", lambda e: e.tensor_copy(out=o32[:], in_=p[:, 0:512]), r=[pn], w=["o32"])
                proj_tok(hT, hn, WX, wn(C_VA), C_VA, 512, ev_va, nextp)
                t.op("dve", lambda e: e.tensor_scalar(out=Vd[:, :, 64:65], in0=ones8[:, :].rearrange("p (h o) -> p h o", o=1),
                                                      scalar1=cols[:, vcol:vcol + 1], scalar2=None, op0=ALU.mult),
                     r=["ones8", "cols"], w=[vn])
                if want_out is not None:
                    kdst, vdst = want_out
                    transpose_out(qn16k, "qn16k", 4, k32, "nr1")
                    for (dk, dv, rows) in zip(kdst, vdst, rows_list):
                        t.dma("sp", dk, k32[rows, :], r=["nr1"], sem="o_k")
                        t.dma("sp", dv, o32[rows, :], r=["o32"], sem="o_v")
                yield

            def q_and_gates(hT, hn, bs):
                p, pn = nextp()
                for c in range(4):
                    proj_feat(hT, hn, WX, wn(C_QA), C_QA + c * 128, 128, p[:, c * 128:(c + 1) * 128], pn)
                qknorm(p, pn, 512, gq8, "gq8", bs["qnA"], bn(bs, "qn16"), wk, outB=bs["qnB"])
                yield
                p, pn = nextp()
                for c in range(2):
                    proj_feat(hT, hn, WX, wn(C_QM), C_QM + c * 128, 128, p[:, c * 128:(c + 1) * 128], pn)
                qknorm(p, pn, 256, gqm8, "gqm8", bs["qmA"], bn(bs, "qmAB"), wk, outB=bs["qmB"])
                yield

            def silu_gates(hT, hn, bs):
                zas, zms, zbs = bs["zas"], bs["zms"], bs["zbs"]

                def silu_ev(dst, dname, n):
                    def ev(p, pn):
                        t.op("act", lambda e: e.activation(out=nr1[:, 0:n], in_=p[:, 0:n], func=AF.Exp, scale=-1.0), r=[pn], w=["nr1"])
                        t.op("act", lambda e: e.activation(out=nr1[:, 0:n], in_=nr1[:, 0:n], func=AF.Ln, bias=1.0), r=["nr1"], w=["nr1"])
                        t.op("act", lambda e: e.activation(out=nr1[:, 0:n], in_=nr1[:, 0:n], func=AF.Exp, scale=-1.0), r=["nr1"], w=["nr1"])
                        t.op("dve", lambda e: e.tensor_tensor(out=dst[:, 0:n], in0=p[:, 0:n], in1=nr1[:, 0:n], op=ALU.mult),
                             r=[pn, "nr1"], w=[dname])
                    return ev
                proj_tok(hT, hn, WX, wn(C_ZA), C_ZA, 512, silu_ev(zas, bn(bs, "zas"), 512), nextp)
                yield
                proj_tok(hT, hn, WX, wn(C_ZM), C_ZM, 256, silu_ev(zms, bn(bs, "zms"), 256), nextp)
                yield
                proj_tok(hT, hn, WX, wn(C_ZB), C_ZB, 512, silu_ev(zbs, bn(bs, "zbs"), 512), nextp)
                yield

            def P_stage(hT, hn, bs, bt, vcol, want_out, rows_list, tri, trin, onem, onen, full, kvdst=None):
                gla_gate(hT, hn)
                yield
                for _ in kv_tile(hT, hn, bt, vcol, want_out, rows_list, kvdst):
                    yield
                gla_cum(tri, trin, onem, onen, full, bs)
                yield
                if full:
                    for _ in q_and_gates(hT, hn, bs):
                        yield
                for _ in gla_proj(hT, hn, full, bs):
                    yield
                if full:
                    for _ in silu_gates(hT, hn, bs):
                        yield

            def finish_am(gt_tile):
                p, pn = nextb()
                for c in range(4):
                    t.op("pe", lambda e: e.transpose(out=p[:, c * 128:(c + 1) * 128], in_=ga16[:, c * 128:(c + 1) * 128],
                                                     identity=id16[:]), r=["ga16", "id16"], w=[pn])
                for c in range(2):
                    t.op("pe", lambda e: e.transpose(out=p[:, (4 + c) * 128:(5 + c) * 128], in_=gm16[:, c * 128:(c + 1) * 128],
                                                     identity=id16[:]), r=["gm16", "id16"], w=[pn])
                t.op("act", lambda e: e.copy(out=GT[:, gt_tile, 0:4, :], in_=p[:, 0:512].rearrange("p (c t) -> p c t", c=4)),
                     r=[pn], w=gtw(gt_tile))
                t.op("dve", lambda e: e.tensor_copy(out=GT[:, gt_tile, 8:10, :], in_=p[:, 512:768].rearrange("p (c t) -> p c t", c=2)),
                     r=[pn], w=gtw(gt_tile))
                yield

            def C_stage(i, bs):
                bt = i + NT_HALO

                def blocks_a(h, bt=bt):
                    pr = h // 2
                    out = []
                    for b in range(5):
                        s_ = (bt - 4 + b) % 6
                        out.append((KT[:, pr, s_ * 128:(s_ + 1) * 128], "KT%d" % s_, 128,
                                    Ep[:, h, b * 128:(b + 1) * 128], "Ep", id16[:], V[:, s_, h, :], "V%d" % s_, 0))
                    return out

                def blocks_m(h):
                    pr = h // 2
                    return [(MKT[:, pr, b * 128:(b + 1) * 128], "MKT", 128, None, None, None,
                             MV[:, b, h, :], "MV", 0) for b in range(2)]
                def par(*gs):
                    gs = list(gs)
                    while gs:
                        for g in list(gs):
                            try:
                                next(g)
                            except StopIteration:
                                gs.remove(g)
                        yield
                att = chain(par(
                    attend(8, (bs["qnA"], bs["qnB"]), bn(bs, "qn16"), slice(0, 128), blocks_a, slice(0, 128), PTa, "PTa", rda, "rda",
                           bs["zas"], bn(bs, "zas"), ga16, "ga16"),
                    attend(4, (bs["qmA"], bs["qmB"]), bn(bs, "qmAB"), slice(0, 128), blocks_m, slice(0, 128), PTm, "PTm", rdm, "rdm",
                           bs["zms"], bn(bs, "zms"), gm16, "gm16")),
                    finish_am(i))
                def delayed(g, n):
                    for _ in range(n):
                        yield
                    for _ in g:
                        yield
                return att, delayed(gla_core(1, tri_p, "tri_p", S32, "S32", S16, "S16", True, i, bs), 3)

            def fin_gen(fin):
                for _ in range(6):
                    yield
                fin()
                yield

            def hsrc(b):
                return xh[b * 128:(b + 1) * 128, :] if b < NT_HALO else xm[(b - NT_HALO) * 128:(b - NT_HALO + 1) * 128, :]
            hp = {0: prep(hsrc(0), nrow, "nrow", 0, 0, 0, q="act"), 1: prep(hsrc(1), nrow, "nrow", 1, 1, 1, q="act")}
            for b in (0, 2):
                run_all(P_stage(hp[b][0], hp[b][1], BS[0], b, 0, None, None, tri_p, "tri_p", one_p, "one_p", False),
                        P_stage(hp[b + 1][0], hp[b + 1][1], BS[1], b + 1, 0, None, None, tri_p, "tri_p", one_p, "one_p", False))
                hp[b + 2] = prep(hsrc(b + 2), nrow, "nrow", 0, 0, 0, q="act")
                run_all(gla_core(1, tri_p, "tri_p", S32, "S32", S16, "S16", False, None, BS[0]))
                hp[b + 3] = prep(hsrc(b + 3), nrow, "nrow", 1, 1, 1, q="act")
                run_all(gla_core(1, tri_p, "tri_p", S32, "S32", S16, "S16", False, None, BS[1]))

            def tsrc(k):
                return xm[k * 128:(k + 1) * 128, :] if k < NT_MAIN else xsd[:, :]

            def P_main(k, hT, hn):
                bt = k + NT_HALO
                if k >= NT_MAIN - 4:
                    r0 = (k - (NT_MAIN - 4)) * 128
                    return P_stage(hT, hn, BS[k % 2], bt, 1, ([ko_p[r0:r0 + 128, :]], [vo_p[r0:r0 + 128, :]]), [slice(0, 128)],
                                   tri_p, "tri_p", one_p, "one_p", True)
                return P_stage(hT, hn, BS[k % 2], bt, 1, None, None, tri_p, "tri_p", one_p, "one_p", True)

            hts = {0: hp[NT_HALO], 1: hp[NT_HALO + 1]}
            run_all(P_main(0, hts[0][0], hts[0][1]))
            for i in range(NT_MAIN):
                gens = list(C_stage(i, BS[i % 2]))
                if i + 2 <= NT_MAIN:
                    k = i + 2
                    hts[k] = prep(tsrc(k), nrow, "nrow", k % 2, k % 2, k % 2, defer=True)
                    gens.append(fin_gen(hts[k][4]))
                if i + 1 < NT_MAIN:
                    gens.append(P_main(i + 1, hts[i + 1][0], hts[i + 1][1]))
                else:
                    gens.append(P_stage(hts[NT_MAIN][0], hts[NT_MAIN][1], BS[0], NT_MAIN + NT_HALO, 2,
                                        ([ko_s[j] for j in range(4)], [vo_s[j] for j in range(4)]),
                                        [slice(j * 32, j * 32 + 16) for j in range(4)], tri_s, "tri_s", one_s, "one_s", True,
                                        kvdst=(KS[:, :, :], "KSb", VS[:, :, :], "VSb")))
                run_all(*gens)
            t.dma("sp", go_p.rearrange("(p two) d v -> (two d) p v", two=2), S32[:], r=["S32"], sem="o_g")

            gpool["banks"] = (6, 5)
            bs = BS[0]
            hT, hn = hts[NT_MAIN][0], hts[NT_MAIN][1]
            bt = NT_MAIN + NT_HALO
            allwx = ["WX_" + k_ for k_ in segs]
            allkv = ["KT%d" % i_ for i_ in range(6)] + ["V%d" % i_ for i_ in range(6)]
            KTf = KT[:, :, :].rearrange("p a b -> p (a b)")
            Vf = V[:, :, :, :].rearrange("p a h d -> p (a h d)")
            CS = [dict(CKT=CKT, CKTn=["CKT", "WM"], CV=CV, CVn=["CV"] + rbn, CMKT=CMKT, CMKTn=["CMKT"] + rbn, CMV=CMV, CMVn=["CMV"] + rbn),
                  dict(CKT=KTf[:, 0:2048].rearrange("p (a f) -> p a f", a=4), CKTn=["CKT1"] + allkv,
                       CV=Vf[:, 0:2080].rearrange("p (a h d) -> p a h d", a=4, h=8), CVn=["CV1"] + allkv,
                       CMKT=KTf[:, 2048:2560].rearrange("p (a f) -> p a f", a=2), CMKTn=["CMKT1"] + allkv,
                       CMV=Vf[:, 2080:2600].rearrange("p (a h d) -> p a h d", a=2, h=4), CMVn=["CMV1"] + allkv)]
            stgc = {"n": 0}
            stg = [(o32, "o32"), (nr1, "nr1"), (xs16[:, :].bitcast(F32), "xs16"), (hTs[1][:, :].bitcast(F32), "hT1"), (xt[1][:, 0:512], "xt1")]

            def ld_cast(dst, dnames, src, ncol):
                sg_, sgn = stg[stgc["n"] % len(stg)]
                eng = ("act", "dve")[stgc["n"] % 2]
                stgc["n"] += 1
                t.dma("sp", sg_[:, 0:ncol], src, w=[sgn], sem="stg" + sgn)
                src_v = sg_[:, 0:ncol] if len(dst.shape) == 2 else sg_[:, 0:ncol].rearrange("p (h d) -> p h d", d=64)
                if eng == "act":
                    t.op("act", lambda e: e.copy(out=dst, in_=src_v), r=[sgn], w=dnames)
                else:
                    t.op("dve", lambda e: e.tensor_copy(out=dst, in_=src_v), r=[sgn], w=dnames)

            def loads(j):
                c_ = CS[j % 2]
                for kb in range(4):
                    ld_cast(ck16[:, kb, :], ["ck16", "WM"], cak[j, kb * 128:(kb + 1) * 128, :], 512)
                for kb in range(2):
                    ld_cast(cm16[:, kb, :], ["cm16"] + rbn, cmk[j, kb * 128:(kb + 1) * 128, :], 256)
                yield
                for kb in range(4):
                    ld_cast(c_["CV"][:, kb, :, 0:64], c_["CVn"], cav[j, kb * 128:(kb + 1) * 128, :], 512)
                    t.op("dve", lambda e: e.tensor_copy(out=c_["CV"][:, kb, :, 64:65], in_=ones8[:, :].rearrange("p (h o) -> p h o", o=1)),
                         r=["ones8"], w=c_["CVn"])
                yield
                for kb in range(2):
                    ld_cast(c_["CMV"][:, kb, :, 0:64], c_["CMVn"], cmv[j, kb * 128:(kb + 1) * 128, :], 256)
                    t.op("dve", lambda e: e.tensor_copy(out=c_["CMV"][:, kb, :, 64:65], in_=ones8[:, 0:4].rearrange("p (h o) -> p h o", o=1)),
                         r=["ones8"], w=c_["CMVn"])
                yield
                for pr in range(4):
                    p, pn = nextb()
                    for kb in range(4):
                        t.op("pe", lambda e: e.transpose(out=p[:, kb * 128:(kb + 1) * 128], in_=ck16[:, kb, pr * 128:(pr + 1) * 128],
                                                         identity=id16[:]), r=["ck16", "id16"], w=[pn])
                    t.op("act", lambda e: e.copy(out=c_["CKT"][:, pr, :], in_=p[:, 0:512]), r=[pn], w=c_["CKTn"])
                    yield
                p, pn = nextb()
                for pr in range(2):
                    for kb in range(2):
                        t.op("pe", lambda e: e.transpose(out=p[:, (pr * 2 + kb) * 128:(pr * 2 + kb + 1) * 128],
                                                         in_=cm16[:, kb, pr * 128:(pr + 1) * 128], identity=id16[:]),
                             r=["cm16", "id16"], w=[pn])
                t.op("act", lambda e: e.copy(out=c_["CMKT"][:, :, :], in_=p[:, 0:512].rearrange("p (c t) -> p c t", c=2)),
                     r=[pn], w=c_["CMKTn"])
                yield

            def att_seq(j):
                c_ = CS[j % 2]
                qc = slice(j * 32, j * 32 + 32)

                def blocks_sa(h):
                    pr = h // 2
                    out = []
                    for b in range(4):
                        out.append((c_["CKT"][:, pr, b * 128:(b + 1) * 128], c_["CKTn"][0], 128,
                                    Ep[:, h, b * 128:(b + 1) * 128], "Ep", id16[:, 0:32], c_["CV"][:, b, h, :], c_["CVn"][0], 0))
                    out.append((KS[:, pr, j * 32:j * 32 + 32], "KSb", 32,
                                Ep[:, h, 512:544], "Ep", id16[:, 0:32], VS[j * 32:j * 32 + 32, h, :], "VSb", j * 32))
                    return out

                def blocks_sm(h):
                    pr = h // 2
                    return [(c_["CMKT"][:, pr, b * 128:(b + 1) * 128], c_["CMKTn"][0], 128, None, None, None,
                             c_["CMV"][:, b, h, :], c_["CMVn"][0], 0) for b in range(2)]
                def par2(*gs):
                    gs = list(gs)
                    while gs:
                        for g in list(gs):
                            try:
                                next(g)
                            except StopIteration:
                                gs.remove(g)
                        yield
                return par2(
                    attend(8, (bs["qnA"], bs["qnB"]), bn(bs, "qn16"), qc, blocks_sa, qc, PTa, "PTa", rda, "rda",
                           bs["zas"], bn(bs, "zas"), ga16, "ga16"),
                    attend(4, (bs["qmA"], bs["qmB"]), bn(bs, "qmAB"), qc, blocks_sm, qc, PTm, "PTm", rdm, "rdm",
                           bs["zms"], bn(bs, "zms"), gm16, "gm16"))
            run_all(loads(0))
            gsamp = gla_core(4, tri_s, "tri_s", S32s, "S32s", S16s, "S16s", True, 16, bs)
            for j in range(4):
                if j > 0:
                    g = 2 * (j - 1)
                    wload(WG, w_in, 8, NX + g * 512, NX + (g + 2) * 512, "WG%d" % g, d0=g * 512,
                          extra_w=allwx + ["WG%d" % (g + 1)])
                run_all(att_seq(j), loads(j + 1) if j + 1 < 4 else None, gsamp if j == 0 else None)
            wload(WO, w_o, 8, 0, 1024, "WO", extra_w=allwx)
            usrc = [(w_ua, k, "WUa") for k in range(4)] + [(w_ub, k, "WUb") for k in range(4)] + [(w_um, k, "WUm") for k in range(2)]
            wu_w = ["Ep"] + allkv + ["CKT1", "CV1", "CMKT1", "CMV1"]
            for c, (src_, k, nm) in enumerate(usrc):
                for hf in range(2):
                    sg_, sgn = stg[stgc["n"] % len(stg)]
                    eng = ("act", "dve")[stgc["n"] % 2]
                    stgc["n"] += 1
                    t.dma("sp", sg_[:, 0:512], src_[k * 128:(k + 1) * 128, hf * 512:(hf + 1) * 512], w=[sgn], sem="stg" + sgn)
                    dst = WUc[c][:, hf * 512:(hf + 1) * 512]
                    if nm == "WUb":
                        t.op("act", lambda e: e.activation(out=dst, in_=sg_[:, 0:512], func=AF.Copy, scale=ggo[:]),
                             r=[sgn, "ggo"], w=[nm] + wu_w)
                    elif eng == "act":
                        t.op("act", lambda e: e.copy(out=dst, in_=sg_[:, 0:512]), r=[sgn], w=[nm] + wu_w)
                    else:
                        t.op("dve", lambda e: e.tensor_copy(out=dst, in_=sg_[:, 0:512]), r=[sgn], w=[nm] + wu_w)
            run_all(finish_am(16))
            t.dma("sp", go_s.rearrange("j (p two) d v -> (two d) (j p) v", two=2), S32s[:], r=["S32s"], sem="o_gs")
            t.barrier()

        with ExitStack() as ey:
            wub32 = [sb("wub32_%d" % i, [128, 1024], F32, ey) for i in range(2)]
            nrow = sb("nrowY", [128, 1024], F32, ey)
            xtY = xt + [sb("xt2", [128, 1024], F32, ey)]
            sg16 = [sb("sg16_%d" % i, [128, 3072], BF16, ey) for i in range(2)]
            acc = [sb("acc%d" % i, [128, 512], F32, ey) for i in range(2)]
            tmp = [sb("tmp%d" % i, [128, 512], F32, ey) for i in range(2)]
            u16 = sb("u16", [128, 1024], BF16, ey)
            uT = [sb("uT%d" % i, [128, 1024], BF16, ey) for i in range(2)]
            y32 = [sb("y32_%d" % i, [128, 1024], F32, ey) for i in range(2)]
            t.dma("sp", nrow[:], bass.AP(norm_in.tensor, 0, [[0, 128], [1, 1024]]), w=["nrowY"], sem="c15")
            wun = ["WUa", "WUb", "WUm"]

            NTY = NT_MAIN + 1
            kch = [(0, 4), (4, 8), (8, 10)]

            def ysrc(i):
                return xm[i * 128:(i + 1) * 128, :] if i < NT_MAIN else xsd[:, :]

            def gates(i, hT, hn):
                sg = sg16[i % 2]
                for g in range(6):
                    def ev_g(p, pn, g=g):
                        t.op("act", lambda e: e.activation(out=sg[:, g * 512:(g + 1) * 512], in_=p[:, 0:512], func=AF.Sigmoid),
                             r=[pn], w=["sg%d_%d" % (i % 2, g)])
                    proj_tok(hT, hn, WG, "WG%d" % g, g * 512, 512, ev_g)

            def up_merge(i):
                sg = sg16[i % 2]
                for half in range(2):
                    f0 = half * 512
                    a_, an = acc[half], "acc%d" % half
                    m_, mn = tmp[half], "tmp%d" % half
                    pus = []
                    for b in range(3):
                        p, pn = nextf()
                        k0, k1 = kch[b]
                        for c in range(k0, k1):
                            t.op("pe", lambda e: e.matmul(p[:, 0:512], lhsT=GT[:, i, c, :], rhs=WUc[c][:, f0:f0 + 512],
                                                          start=(c == k0), stop=(c == k1 - 1)), r=["GT", wun[b]], w=[pn])
                        pus.append((p, pn))
                    t.op("dve", lambda e: e.tensor_tensor(out=a_[:], in0=pus[0][0][:, 0:512], in1=sg[:, f0:f0 + 512], op=ALU.mult),
                         r=[pus[0][1], "sg%d_%d" % (i % 2, half)], w=[an])
                    t.op("dve", lambda e: e.tensor_tensor(out=m_[:], in0=pus[1][0][:, 0:512], in1=sg[:, 1024 + f0:1024 + f0 + 512], op=ALU.mult),
                         r=[pus[1][1], "sg%d_%d" % (i % 2, 2 + half)], w=[mn])
                    t.op("pool", lambda e: e.tensor_tensor(out=a_[:], in0=a_[:], in1=m_[:], op=ALU.add), r=[an, mn], w=[an])
                    t.op("dve", lambda e: e.tensor_tensor(out=m_[:], in0=pus[2][0][:, 0:512], in1=sg[:, 2048 + f0:2048 + f0 + 512], op=ALU.mult),
                         r=[pus[2][1], "sg%d_%d" % (i % 2, 4 + half)], w=[mn])
                    t.op("pool", lambda e: e.tensor_tensor(out=u16[:, f0:f0 + 512], in0=a_[:], in1=m_[:], op=ALU.add),
                         r=[an, mn], w=["u16"])

            def u_transpose(i):
                p, pn = nextb()
                for k in range(8):
                    t.op("pe", lambda e: e.transpose(out=p[:, k * 128:(k + 1) * 128], in_=u16[:, k * 128:(k + 1) * 128], identity=id16[:]),
                         r=["u16", "id16"], w=[pn])
                t.op("act", lambda e: e.copy(out=uT[i % 2][:], in_=p[:]), r=[pn], w=["uT%d" % (i % 2)])

            def out_proj(i, x_, xn):
                y_, yn = y32[i % 2], "y32_%d" % (i % 2)
                u_, un = uT[i % 2], "uT%d" % (i % 2)
                for half in range(2):
                    f0 = half * 512
                    p, pn = nextf()
                    for k in range(8):
                        t.op("pe", lambda e: e.matmul(p[:, 0:512], lhsT=u_[:, k * 128:(k + 1) * 128], rhs=WO[:, k, f0:f0 + 512],
                                                      start=(k == 0), stop=(k == 7)), r=[un, "WO"], w=[pn])
                    t.op("dve", lambda e: e.tensor_tensor(out=y_[:, f0:f0 + 512], in0=p[:, 0:512], in1=x_[:, f0:f0 + 512], op=ALU.add),
                         r=[pn, xn], w=[yn])
                if i < NT_MAIN:
                    t.dma("sp", y_p[i * 128:(i + 1) * 128, :], y_[:], r=[yn], sem="o_y%d" % (i % 2))
                else:
                    for j in range(4):
                        t.dma("sp", y_s[j], y_[j * 32:j * 32 + 16, :], r=[yn], sem="o_y%d" % (i % 2))

            preps = {0: prep(ysrc(0), nrow, "nrowY", 0, 0, 0, xtY, use_pool=True)}
            for i in range(NTY):
                hT, hn, x_, xn = preps[i][0:4]
                fin = None
                if i + 1 < NTY:
                    preps[i + 1] = prep(ysrc(i + 1), nrow, "nrowY", (i + 1) % 3, (i + 1) % 2, (i + 1) % 3, xtY,
                                        use_pool=True, defer=True)
                    fin = preps[i + 1][4]
                gates(i, hT, hn)
                if fin is not None:
                    fin()
                up_merge(i)
                if i >= 1:
                    out_proj(i - 1, preps[i - 1][2], preps[i - 1][3])
                u_transpose(i)
            out_proj(NTY - 1, preps[NTY - 1][2], preps[NTY - 1][3])
            t.barrier(("sp",))
    except _Stop:
        pass
    return nc


_NC_CACHE = {}


def _host_consts():
    ident = np.eye(128, dtype=np.float32)
    s = np.arange(128)
    trip = (s[:, None] <= s[None, :]).astype(np.float32)
    onep = np.ones((128, 128), np.float32)
    same = (s[:, None] // 32) == (s[None, :] // 32)
    tris = (same & (s[:, None] <= s[None, :])).astype(np.float32)
    ones_s = (same & ((s[:, None] % 32) < 16)).astype(np.float32)
    blk64 = ((s[:, None] // 64) == (s[None, :] // 64)).astype(np.float32)
    return ident, trip, onep, tris, ones_s, blk64


def _bias_tables(rel_bias):
    tab = np.asarray(rel_bias)[0]
    j = np.arange(128)[:, None]
    i = np.arange(640)[None, :]
    idx = np.clip(512 + j - i, -128, 128) + 128
    ep = tab[:, idx]
    invalid = ((i < 64) & (j >= 64)) | ((i >= 576) & (j < 64))
    ep = np.where(invalid[None], np.float32(NEG), ep)
    ep = np.ascontiguousarray(ep.transpose(1, 0, 2)).astype(np.float32)
    j2 = np.arange(32)[:, None]
    i2 = np.arange(544)[None, :]
    idx2 = np.clip(512 + j2 - i2, -128, 128) + 128
    es_ = np.ascontiguousarray(tab[:, idx2].transpose(1, 0, 2)).astype(np.float32)
    return ep, es_


def _make_in_maps(x_prompt, x_sample, mem_prompt, cache_a_k, cache_a_v, state_gla, cache_mem_k, cache_mem_v,
                  norm_in, w_in, g_qa, g_ka, rel_bias, w_gate2, b_gate, g_gla_out, g_mem, w_mem_kv, g_qm, g_km,
                  w_up_a, w_up_b, w_up_m, w_out):
    f = lambda a: np.ascontiguousarray(np.asarray(a, dtype=np.float32))
    x_prompt, x_sample, mem_prompt = f(x_prompt), f(x_sample), f(mem_prompt)
    ident, trip, onep, tris, ones_s, blk64 = _host_consts()
    ep, es_ = _bias_tables(f(rel_bias))
    shared = {
        "w_in": f(w_in)[0], "w_mem": f(w_mem_kv)[0], "w_ua": f(w_up_a)[0], "w_ub": f(w_up_b)[0],
        "w_um": f(w_up_m)[0], "w_o": f(w_out)[0], "norm_in": f(norm_in), "g_mem": f(g_mem),
        "g_qa": f(g_qa).reshape(64, 1), "g_ka": f(g_ka).reshape(64, 1), "g_qm": f(g_qm).reshape(64, 1),
        "g_km": f(g_km).reshape(64, 1), "w_g2": f(w_gate2)[0], "b_g": f(b_gate), "g_go": f(g_gla_out).reshape(128, 1),
        "ep": ep, "ident": ident, "trip": trip, "onep": onep, "tris": tris, "ones_s": ones_s, "blk64": blk64,
    }
    in_maps = []
    for c in range(NCORES):
        sq, seg = c // 4, c % 4
        t0 = seg * 2048
        xh = x_prompt[sq, t0 - 512:t0] if seg > 0 else np.zeros((512, 1024), np.float32)
        xs = np.zeros((4, 32, 1024), np.float32)
        xs[:, :16] = x_sample[c * 4:(c + 1) * 4]
        cols = np.zeros((128, 4), np.float32)
        cols[:, 0] = 1.0 if seg > 0 else 0.0
        cols[:, 1] = 1.0
        cols[:, 2] = (np.arange(128) % 32 < 16).astype(np.float32)
        m = dict(shared)
        m.update({
            "xh": f(xh), "xm": f(x_prompt[sq, t0:t0 + 2048]), "xs": xs.reshape(128, 1024),
            "mem": f(mem_prompt[sq]),
            "cak": f(cache_a_k[0, c * 4:(c + 1) * 4]).reshape(4, 512, 512),
            "cav": f(cache_a_v[0, c * 4:(c + 1) * 4]).reshape(4, 512, 512),
            "sg": f(state_gla[0, c * 4:(c + 1) * 4]),
            "cmk": f(cache_mem_k[0, c * 4:(c + 1) * 4]).reshape(4, 256, 256),
            "cmv": f(cache_mem_v[0, c * 4:(c + 1) * 4]).reshape(4, 256, 256),
            "cols": cols,
        })
        in_maps.append(m)
    return in_maps


def kernel(**inputs):
    in_maps = _make_in_maps(**inputs)
    if "nc" not in _NC_CACHE:
        _NC_CACHE["nc"] = build_nc()
    nc = _NC_CACHE["nc"]
    res = run_bass_kernel_spmd(nc, in_maps, core_ids=list(range(NCORES)))
    return _assemble(res.results)


def _assemble(R):
    y_prompt = np.stack([np.concatenate([R[s * 4 + g]["y_p"] for g in range(4)], 0) for s in range(2)])
    y_sample = np.concatenate([R[c]["y_s"] for c in range(NCORES)], 0)
    akp = np.stack([R[s * 4 + 3]["ko_p"].reshape(512, 8, 64) for s in range(2)])[None]
    avp = np.stack([R[s * 4 + 3]["vo_p"].reshape(512, 8, 64) for s in range(2)])[None]
    sgp = np.stack([R[s * 4 + 3]["go_p"] for s in range(2)])[None]
    mkp = np.stack([R[s * 4]["mko"].reshape(256, 4, 64) for s in range(2)])[None]
    mvp = np.stack([R[s * 4]["mvo"].reshape(256, 4, 64) for s in range(2)])[None]
    aks = np.concatenate([R[c]["ko_s"].reshape(4, 16, 8, 64) for c in range(NCORES)], 0)[None]
    avs = np.concatenate([R[c]["vo_s"].reshape(4, 16, 8, 64) for c in range(NCORES)], 0)[None]
    sgs = np.concatenate([R[c]["go_s"] for c in range(NCORES)], 0)[None]
    out = (y_prompt, y_sample, akp, avp, sgp, mkp, mvp, aks, avs, sgs)
    return tuple(np.ascontiguousarray(o, dtype=np.float32) for o in out)
```
